# Optimizing a Trainium2 kernel written in Bass

```python
import math
import jax
import jax.numpy as jnp
from jax import lax
import numpy as np

D_MODEL = 1024
BATCH = 8
SEQ = 4096
DEPTH = 2

N_BRANCH = 4
BRANCH_W = 512
SSD_HEADS = 8
SSD_HEAD_DIM = 64
SSD_GROUPS = 2
SSD_STATE = 128
SSD_CHUNK = 128
SSD_CONV = 4
SSD_XBC = BRANCH_W + 2 * SSD_GROUPS * SSD_STATE
DN_HEADS = 4
DN_HEAD_DIM = 128
DN_CHUNK = 64
DN_CONV = 4
SG_GROUPS = 4
SG_GROUP_DIM = BRANCH_W // SG_GROUPS
SG_CHUNK = 128
FOX_HEADS = 8
FOX_HEAD_DIM = 64
FOX_BLOCK = 128
D_FF = 4 * D_MODEL
IN_SIZES = (BRANCH_W, SSD_XBC, SSD_HEADS, 3 * BRANCH_W, DN_HEADS, DN_HEADS, BRANCH_W,
            2 * BRANCH_W, 3 * BRANCH_W, FOX_HEADS, N_BRANCH * D_MODEL)
D_IN = sum(IN_SIZES)
LN_EPS = 1e-5
NORM_EPS = 1e-6
DEEPNORM_ALPHA = (2 * DEPTH) ** 0.25
DEEPNORM_BETA = (8 * DEPTH) ** -0.25

kernel_name = "hybrid_ssd_deltanet_sgmlp_fox"


def layer_norm(x, g, b):
    xf = x.astype(jnp.float32)
    mu = jnp.mean(xf, -1, keepdims=True)
    var = jnp.mean(jnp.square(xf - mu), -1, keepdims=True)
    y = (xf - mu) * lax.rsqrt(var + LN_EPS) * g.astype(jnp.float32) + b.astype(jnp.float32)
    return y.astype(x.dtype)


def rms_norm(x, w):
    xf = x.astype(jnp.float32)
    return xf * lax.rsqrt(jnp.mean(xf * xf, -1, keepdims=True) + NORM_EPS) * w.astype(jnp.float32)


def l2_normalize(t):
    return t * lax.rsqrt(jnp.sum(t * t, -1, keepdims=True) + NORM_EPS)


def causal_dwconv(x, w):
    k = w.shape[0]
    return lax.conv_general_dilated(x, w[:, None, :], window_strides=(1,), padding=[(k - 1, 0)],
                                    dimension_numbers=('NWC', 'WIO', 'NWC'),
                                    feature_group_count=x.shape[-1])


def ssd_mixer(z, xbc, dt_raw, conv_w, conv_b, dt_bias, a_log, d_skip, norm_w):
    f32 = jnp.float32
    b, L, _ = z.shape
    nc, Q, G = L // SSD_CHUNK, SSD_CHUNK, SSD_GROUPS
    e = SSD_HEADS // G
    xbc = jax.nn.silu(causal_dwconv(xbc, conv_w) + conv_b)
    xs, bm, cm = jnp.split(xbc.astype(f32), [BRANCH_W, BRANCH_W + G * SSD_STATE], axis=-1)
    dt = jax.nn.softplus(dt_raw.astype(f32) + dt_bias.astype(f32))
    a = -jnp.exp(a_log.astype(f32))
    xh = xs.reshape(b, nc, Q, G, e, SSD_HEAD_DIM)
    xdt = xh * dt.reshape(b, nc, Q, G, e)[..., None]
    bm = bm.reshape(b, nc, Q, G, SSD_STATE)
    cm = cm.reshape(b, nc, Q, G, SSD_STATE)
    a_cs = jnp.cumsum((dt * a).reshape(b, nc, Q, G, e).transpose(0, 3, 4, 1, 2), axis=-1)
    tri = jnp.tril(jnp.ones((Q, Q), bool))
    seg = jnp.exp(jnp.where(tri, a_cs[..., :, None] - a_cs[..., None, :], -jnp.inf))
    cb = jnp.einsum('bclgn,bcsgn->bgcls', cm, bm)
    y_diag = jnp.einsum('bgcls,bgecls,bcsgep->bclgep', cb, seg, xdt)
    decay_to_end = jnp.exp(a_cs[..., -1:] - a_cs)
    states = jnp.einsum('bcsgn,bgecs,bcsgep->cbgepn', bm, decay_to_end, xdt)
    chunk_decay = jnp.exp(a_cs[..., -1]).transpose(3, 0, 1, 2)

    def step(h, inp):
        st, dec = inp
        return h * dec[..., None, None] + st, h

    _, prev = lax.scan(step, jnp.zeros(states.shape[1:], f32), (states, chunk_decay))
    y_off = jnp.einsum('bclgn,cbgepn,bgecl->bclgep', cm, prev, jnp.exp(a_cs))
    y = y_diag + y_off + d_skip.astype(f32).reshape(G, e)[:, :, None] * xh
    y = y.reshape(b, L, BRANCH_W) * jax.nn.silu(z.astype(f32))
    return rms_norm(y, norm_w).astype(z.dtype)


def gated_deltanet_mixer(qkv, beta_raw, a_raw, gate, conv_w, a_log, dt_bias, norm_w):
    f32 = jnp.float32
    b, L, _ = qkv.shape
    H, Dk, C = DN_HEADS, DN_HEAD_DIM, DN_CHUNK
    nc = L // C
    qkv = jax.nn.silu(causal_dwconv(qkv, conv_w)).astype(f32)
    q, k, v = jnp.split(qkv, 3, axis=-1)
    q = l2_normalize(q.reshape(b, L, H, Dk)) * (Dk ** -0.5)
    k = l2_normalize(k.reshape(b, L, H, Dk))
    v = v.reshape(b, L, H, Dk)

    def chunk4(t):
        return t.reshape(b, nc, C, H, Dk).transpose(1, 0, 3, 2, 4)

    def chunk3(t):
        return t.reshape(b, nc, C, H).transpose(1, 0, 3, 2)

    q, k, v = chunk4(q), chunk4(k), chunk4(v)
    beta = chunk3(jax.nn.sigmoid(beta_raw.astype(f32)))
    g = -jnp.exp(a_log.astype(f32)) * jax.nn.softplus(a_raw.astype(f32) + dt_bias.astype(f32))
    g_cs = jnp.cumsum(chunk3(g), axis=-1)
    tri = jnp.tril(jnp.ones((C, C), bool))
    strict = jnp.tril(jnp.ones((C, C), f32), -1)
    gamma = jnp.exp(jnp.where(tri, g_cs[..., :, None] - g_cs[..., None, :], -jnp.inf))
    kb = k * beta[..., None]
    a_mat = jnp.einsum('nbhid,nbhjd->nbhij', kb, k) * gamma * strict
    m = a_mat + jnp.eye(C, dtype=f32)
    rhs = jnp.concatenate([kb * jnp.exp(g_cs)[..., None], v * beta[..., None]], axis=-1)
    sol = lax.linalg.triangular_solve(m, rhs, left_side=True, lower=True, unit_diagonal=True)
    w_c, u_c = jnp.split(sol, 2, axis=-1)
    qg = q * jnp.exp(g_cs)[..., None]
    qk = jnp.einsum('nbhid,nbhjd->nbhij', q, k) * gamma
    k_dec = k * jnp.exp(g_cs[..., -1:] - g_cs)[..., None]
    last = jnp.exp(g_cs[..., -1])

    def step(S, inp):
        qg_c, qk_c, w_i, u_i, kd_c, last_c = inp
        v_new = u_i - jnp.einsum('bhcd,bhde->bhce', w_i, S)
        o = jnp.einsum('bhcd,bhde->bhce', qg_c, S) + jnp.einsum('bhij,bhje->bhie', qk_c, v_new)
        S = S * last_c[..., None, None] + jnp.einsum('bhcd,bhce->bhde', kd_c, v_new)
        return S, o

    S0 = jnp.zeros((b, H, Dk, Dk), f32)
    _, o = lax.scan(step, S0, (qg, qk, w_c, u_c, k_dec, last))
    o = o.transpose(1, 0, 3, 2, 4).reshape(b, L, H, Dk)
    o = rms_norm(o, norm_w) * jax.nn.silu(gate.astype(f32).reshape(b, L, H, Dk))
    return o.reshape(b, L, BRANCH_W).astype(gate.dtype)


def spatial_gating_mixer(uv, ln_g, ln_b, w_s, b_s):
    b, L, _ = uv.shape
    nc = L // SG_CHUNK
    u, v = jnp.split(jax.nn.gelu(uv), 2, axis=-1)
    v = layer_norm(v, ln_g, ln_b).reshape(b, nc, SG_CHUNK, SG_GROUPS, SG_GROUP_DIM)
    w = w_s * jnp.tril(jnp.ones((SG_CHUNK, SG_CHUNK), w_s.dtype))
    v = jnp.einsum('gts,bcsgd->bctgd', w, v) + b_s.T[None, None, :, :, None]
    return u * v.reshape(b, L, BRANCH_W)


def forgetting_attention_mixer(qkv, f_raw, f_bias):
    f32 = jnp.float32
    b, L, _ = qkv.shape
    H, Dh, BLK = FOX_HEADS, FOX_HEAD_DIM, FOX_BLOCK
    nb = L // BLK
    q, k, v = (t.reshape(b, L, H, Dh) for t in jnp.split(qkv, 3, axis=-1))
    c = jnp.cumsum(jax.nn.log_sigmoid(f_raw.astype(f32) + f_bias.astype(f32)), axis=1)
    c_keys = c.transpose(0, 2, 1)
    qb = q.reshape(b, nb, BLK, H, Dh).transpose(1, 0, 2, 3, 4)
    cb = c.reshape(b, nb, BLK, H).transpose(1, 0, 3, 2)
    starts = jnp.arange(nb, dtype=jnp.int32) * BLK
    k_pos = jnp.arange(L, dtype=jnp.int32)
    scale = Dh ** -0.5

    def block(args):
        q_blk, c_blk, start = args
        s = jnp.einsum('bqhd,bkhd->bhqk', q_blk, k).astype(f32) * scale
        s = s + c_blk[..., None] - c_keys[:, :, None, :]
        q_pos = start + jnp.arange(BLK, dtype=jnp.int32)
        s = jnp.where(k_pos[None, :] <= q_pos[:, None], s, -jnp.inf)
        p = jax.nn.softmax(s, axis=-1)
        return jnp.einsum('bhqk,bkhd->bqhd', p.astype(v.dtype), v)

    o = lax.map(block, (qb, cb, starts))
    return o.transpose(1, 0, 2, 3, 4).reshape(b, L, BRANCH_W)


def setup_inputs(seed: int = 0) -> dict:
    key = jax.random.key(seed)
    ks = iter(jax.random.split(key, 40))
    f32 = jnp.float32

    def nrm(shape, scale):
        return scale * jax.random.normal(next(ks), shape, f32)

    def gain(shape):
        return 1.0 + nrm(shape, 0.02)

    def dt_bias_init(shape):
        u = jax.random.uniform(next(ks), shape, f32)
        dt = jnp.exp(u * (math.log(0.1) - math.log(0.001)) + math.log(0.001))
        return dt + jnp.log(-jnp.expm1(-dt))

    def a_log_init(shape):
        return jnp.log(jax.random.uniform(next(ks), shape, f32, minval=1.0, maxval=16.0))

    x = nrm((BATCH, SEQ, D_MODEL), 1.0)
    return {
        "x": x,
        "ln_in_g": gain((D_MODEL,)),
        "ln_in_b": nrm((D_MODEL,), 0.02),
        "w_in": nrm((DEPTH, D_MODEL, D_IN), D_MODEL ** -0.5),
        "ssd_conv_w": nrm((DEPTH, SSD_CONV, SSD_XBC), SSD_CONV ** -0.5),
        "ssd_conv_b": nrm((DEPTH, SSD_XBC), 0.02),
        "ssd_dt_bias": dt_bias_init((DEPTH, SSD_HEADS)),
        "ssd_a_log": a_log_init((DEPTH, SSD_HEADS)),
        "ssd_d": gain((DEPTH, SSD_HEADS)),
        "ssd_norm_w": gain((DEPTH, BRANCH_W)),
        "dn_conv_w": nrm((DEPTH, DN_CONV, 3 * BRANCH_W), DN_CONV ** -0.5),
        "dn_a_log": a_log_init((DEPTH, DN_HEADS)),
        "dn_dt_bias": dt_bias_init((DEPTH, DN_HEADS)),
        "dn_norm_w": gain((DEPTH, DN_HEAD_DIM)),
        "sg_ln_g": gain((DEPTH, BRANCH_W)),
        "sg_ln_b": nrm((DEPTH, BRANCH_W), 0.02),
        "sg_w": nrm((DEPTH, SG_GROUPS, SG_CHUNK, SG_CHUNK), SG_CHUNK ** -0.5),
        "sg_b": 1.0 + nrm((DEPTH, SG_GROUPS, SG_CHUNK), 0.1),
        "fox_f_bias": 2.0 + nrm((DEPTH, FOX_HEADS), 0.5),
        "gate_b": nrm((DEPTH, N_BRANCH, D_MODEL), 0.02),
        "w_branch": nrm((DEPTH, N_BRANCH, BRANCH_W, D_MODEL), BRANCH_W ** -0.5),
        "w_out": nrm((DEPTH, D_MODEL, D_MODEL), DEEPNORM_BETA * D_MODEL ** -0.5),
        "ln1_g": gain((DEPTH, D_MODEL)),
        "ln1_b": nrm((DEPTH, D_MODEL), 0.02),
        "w_up": nrm((DEPTH, D_MODEL, D_FF), D_MODEL ** -0.5),
        "w_down": nrm((DEPTH, D_FF, D_MODEL), DEEPNORM_BETA * D_FF ** -0.5),
        "ln2_g": gain((DEPTH, D_MODEL)),
        "ln2_b": nrm((DEPTH, D_MODEL), 0.02),
    }


def reference(x, ln_in_g, ln_in_b, w_in, ssd_conv_w, ssd_conv_b, ssd_dt_bias, ssd_a_log, ssd_d,
              ssd_norm_w, dn_conv_w, dn_a_log, dn_dt_bias, dn_norm_w, sg_ln_g, sg_ln_b, sg_w, sg_b,
              fox_f_bias, gate_b, w_branch, w_out, ln1_g, ln1_b, w_up, w_down, ln2_g, ln2_b):
    b, L, _ = x.shape
    splits = np.cumsum(IN_SIZES)[:-1].tolist()
    h = layer_norm(x, ln_in_g, ln_in_b)
    for l in range(DEPTH):
        proj = jnp.einsum('bld,dc->blc', h, w_in[l])
        (ssd_z, ssd_xbc, ssd_dt, dn_qkv, dn_beta, dn_a, dn_gate,
         sg_uv, fox_qkv, fox_f, gate_logits) = jnp.split(proj, splits, axis=-1)
        y_a = ssd_mixer(ssd_z, ssd_xbc, ssd_dt, ssd_conv_w[l], ssd_conv_b[l], ssd_dt_bias[l],
                        ssd_a_log[l], ssd_d[l], ssd_norm_w[l])
        y_b = gated_deltanet_mixer(dn_qkv, dn_beta, dn_a, dn_gate, dn_conv_w[l], dn_a_log[l],
                                   dn_dt_bias[l], dn_norm_w[l])
        y_c = spatial_gating_mixer(sg_uv, sg_ln_g[l], sg_ln_b[l], sg_w[l], sg_b[l])
        y_d = forgetting_attention_mixer(fox_qkv, fox_f, fox_f_bias[l])
        gates = jax.nn.sigmoid(gate_logits.reshape(b, L, N_BRANCH, D_MODEL) + gate_b[l])
        merged = gates[:, :, 0] * jnp.einsum('blc,cd->bld', y_a, w_branch[l, 0])
        merged = merged + gates[:, :, 1] * jnp.einsum('blc,cd->bld', y_b, w_branch[l, 1])
        merged = merged + gates[:, :, 2] * jnp.einsum('blc,cd->bld', y_c, w_branch[l, 2])
        merged = merged + gates[:, :, 3] * jnp.einsum('blc,cd->bld', y_d, w_branch[l, 3])
        mix = jnp.einsum('bld,de->ble', merged, w_out[l])
        h = layer_norm(DEEPNORM_ALPHA * h + mix, ln1_g[l], ln1_b[l])
        ff = jnp.einsum('blf,fd->bld', jnp.square(jax.nn.relu(jnp.einsum('bld,df->blf', h, w_up[l]))), w_down[l])
        h = layer_norm(DEEPNORM_ALPHA * h + ff, ln2_g[l], ln2_b[l])
    return h
```

```python
import numpy as np
import concourse.bass as bass
import concourse.mybir as mybir
from contextlib import ExitStack

F32 = mybir.dt.float32
BF16 = mybir.dt.bfloat16
AF = mybir.ActivationFunctionType
ALU = mybir.AluOpType
AX = mybir.AxisListType
_ISZ = {F32: 4, BF16: 2}


def _ap(h):
    return h.ap() if hasattr(h, "ap") else h[:]


class Buf:
    __slots__ = ("name", "writers", "readers", "war_base", "lo", "hi", "psum")

    def __init__(self, name):
        self.name = name
        self.psum = False
        self.writers = []
        self.readers = []
        self.war_base = []


class V:
    __slots__ = ("buf", "ap")

    def __init__(self, buf, ap):
        self.buf = buf
        self.ap = ap

    def __getitem__(self, k):
        return V(self.buf, self.ap[k])

    def r(self, s, **kw):
        return V(self.buf, self.ap.rearrange(s, **kw))

    def bc(self, shape):
        return V(self.buf, self.ap.to_broadcast(list(shape)))

    def bitcast(self, dt):
        return V(self.buf, self.ap.bitcast(dt))

    def alias(self, buf):
        return V(buf, self.ap)


DMA_ENGS = ("sp", "act", "pool")
COMPUTE = ("pe", "act", "dve", "pool")
NSEM_DMA = {"sp": 16, "act": 8, "pool": 8}


class Prog:
    def __init__(self, nc):
        self.nc = nc
        self.ops = []
        self.ndma = {q: 0 for q in DMA_ENGS}
        self.sb_top = 0
        self.sb_regions = []
        self.sb_max = 0
        self.uid = 0
        self.psum_n = 0
        self.arena = None

    def sb(self, name, shape, dtype, nbufs=None):
        if self.arena is None:
            self.arena_bytes = 206 * 1024
            self.arena = _ap(self.nc.alloc_sbuf_tensor("arena", [128, self.arena_bytes // 4], F32))
        per = int(np.prod(shape[1:])) * _ISZ[dtype]
        lo = (self.sb_top + 31) // 32 * 32
        hi = lo + (per + 3) // 4 * 4
        assert hi <= self.arena_bytes, f"SBUF arena overflow: {name} {hi}"
        self.sb_top = hi
        self.sb_max = max(self.sb_max, hi)
        ap = self.arena[0:shape[0], lo // 4:hi // 4]
        if dtype != F32:
            ap = ap.bitcast(dtype)
        ap = ap[:, 0:int(np.prod(shape[1:]))]
        if len(shape) == 3:
            ap = ap.rearrange("p (a b) -> p a b", a=shape[1])
        elif len(shape) == 4:
            ap = ap.rearrange("p (a b c) -> p a b c", a=shape[1], b=shape[2])
        inherit = []
        for (l2, h2, b2) in self.sb_regions:
            if l2 < hi and lo < h2:
                inherit += b2.readers + b2.writers
        if nbufs is None:
            b = Buf(name)
            b.readers = list(set(inherit))
            b.lo, b.hi = lo, hi
            self.sb_regions.append((lo, hi, b))
            return V(b, ap)
        outs = []
        n = shape[1] // nbufs
        step = per // nbufs
        for i in range(nbufs):
            b = Buf(f"{name}{i}")
            b.readers = list(set(inherit))
            b.lo, b.hi = lo + i * step, lo + (i + 1) * step
            self.sb_regions.append((b.lo, b.hi, b))
            outs.append(V(b, ap[:, i * n:(i + 1) * n]))
        return outs

    def mark(self):
        return self.sb_top

    def release(self, m):
        self.sb_top = m
        if len(self.sb_regions) > 400:
            pass

    def psum(self, name, dtype=F32, cols=512):
        h = self.nc.alloc_psum_tensor(f"{name}", [128, cols], dtype)
        b = Buf(name)
        b.psum = True
        return V(b, _ap(h))

    def dram(self, name, shape, dtype, kind="Internal"):
        h = self.nc.dram_tensor(name, list(shape), dtype, kind=kind)
        return V(Buf(name), h.ap())

    def token(self, v, name="tok"):
        return V(Buf(name), v.ap)

    def add(self, eng, fn, reads, writes, partial=False, dma=False):
        idx = len(self.ops)
        deps = set()
        rb = {v.buf for v in reads}
        wb = {v.buf for v in writes}
        for b in rb:
            for w in b.writers:
                deps.add((w, "raw"))
            if b.psum:
                for r in b.readers:
                    deps.add((r, "rar"))
        for b in wb:
            for r in b.readers:
                deps.add((r, "war"))
            for r in b.war_base:
                deps.add((r, "war"))
            if not partial:
                for w in b.writers:
                    deps.add((w, "waw"))
        fdeps = set()
        for (d, kind) in deps:
            o = self.ops[d]
            if d == idx:
                continue
            if (not dma) and (not o[3]) and o[0] == eng and kind != "raw" and (eng == "pe" or kind == "rar"):
                continue
            fdeps.add(d)
        for b in wb:
            if partial:
                if b.readers:
                    b.war_base = list(b.readers)
                    b.writers = [idx]
                    b.readers = []
                else:
                    b.writers.append(idx)
            else:
                b.writers = [idx]
                b.readers = []
                b.war_base = []
        for b in rb:
            b.readers.append(idx)
        semi = None
        if dma:
            k = self.ndma[eng]
            self.ndma[eng] += 1
            S = NSEM_DMA[eng]
            semi = (eng, k % S, 16 * (k // S + 1), k)
        self.ops.append([eng, fn, sorted(fdeps), dma, semi, False, 0, [(b.name, getattr(b, 'lo', None), getattr(b, 'hi', None)) for b in rb], [(b.name, getattr(b, 'lo', None), getattr(b, 'hi', None)) for b in wb]])
        return idx

    def _vs(self, kw):
        reads, writes = [], []
        for k, v in kw.items():
            if isinstance(v, V):
                (writes if k in ("out", "accum_out") else reads).append(v)
        return reads, writes

    def I(self, eng, meth, _partial=False, _extra_r=(), _extra_w=(), **kw):
        reads, writes = self._vs(kw)
        reads += list(_extra_r)
        writes += list(_extra_w)
        args = {k: (v.ap if isinstance(v, V) else v) for k, v in kw.items()}

        def fn(e, meth=meth, args=args):
            return getattr(e, meth)(**args)
        return self.add(eng, fn, reads, writes, partial=_partial)

    def mm(self, out, lhsT, rhs, start=True, stop=True, **kw):
        return self.I("pe", "matmul", out=out, lhsT=lhsT, rhs=rhs, start=start, stop=stop, _partial=True, **kw)

    def tr(self, out, in_, ident):
        return self.I("pe", "transpose", out=out, in_=in_, identity=ident, _partial=True)

    def dma(self, out, in_, q="sp", partial=True, **kw):
        args = dict(out=out.ap, in_=in_.ap, **kw)

        def fn(e, args=args):
            return e.dma_start(**args)
        return self.add(q, fn, [in_], [out], partial=partial, dma=True)

    def emit(self):
        nc = self.nc
        ops = self.ops
        last = {}
        for i, o in enumerate(ops):
            if o[3]:
                last[(o[4][0], o[4][1])] = i
        fin_deps = sorted(last.values())
        ops.append(["sp", None, fin_deps, False, None, False, 0, [], []])
        for o in ops:
            for d in o[2]:
                if not ops[d][3]:
                    ops[d][5] = True
        cnt = {e: 0 for e in COMPUTE + ("sp",)}
        for o in ops:
            if not o[3] and o[5]:
                cnt[o[0]] += 1
                o[6] = cnt[o[0]]
        with ExitStack() as st:
            esem = {e: st.enter_context(nc.semaphore(f"s_{e}")) for e in COMPUTE}
            dsem = {q: [st.enter_context(nc.semaphore(f"d_{q}{i}")) for i in range(NSEM_DMA[q])] for q in DMA_ENGS}
            block = st.enter_context(nc.Block())
            per_eng = {e: [] for e in ("pe", "act", "dve", "pool", "sp")}
            for i, o in enumerate(ops):
                per_eng[o[0]].append(i)

            def run(ename, e):
                waited = {}
                for i in per_eng[ename]:
                    o = ops[i]
                    need = {}
                    for d in o[2]:
                        od = ops[d]
                        if od[3]:
                            key = ("d", od[4][0], od[4][1])
                            val = od[4][2]
                        else:
                            key = ("e", od[0])
                            val = od[6]
                        need[key] = max(need.get(key, 0), val)
                    if o[3]:
                        q, si, val, k = o[4]
                        if k >= NSEM_DMA[q]:
                            key = ("d", q, si)
                            need[key] = max(need.get(key, 0), val - 16)
                    for key, val in need.items():
                        if waited.get(key, 0) >= val:
                            continue
                        waited[key] = val
                        sem = dsem[key[1]][key[2]] if key[0] == "d" else esem[key[1]]
                        e.wait_ge(sem, val)
                    if o[1] is None:
                        continue
                    ins = o[1](e)
                    if o[3]:
                        ins.then_inc(dsem[o[4][0]][o[4][1]], 16)
                    elif o[5]:
                        ins.then_inc(esem[ename], 1)

            @block.tensor
            def _(e):
                run("pe", e)

            @block.scalar
            def _(e):
                run("act", e)

            @block.vector
            def _(e):
                run("dve", e)

            @block.gpsimd
            def _(e):
                run("pool", e)

            @block.sync
            def _(e):
                run("sp", e)
        return cnt

T = 4096
D = 1024
KD = 8
DIN = 10264
ALPHA = 4 ** 0.25
OFF = dict(z=0, xs=512, bm=1024, cm=1280, dt=1536, dq=1544, dk=2056, dv=2568, dbeta=3080, da=3084, dgate=3088,
           su=3600, sv=4112, fq=4624, fk=5136, fv=5648, ff=6160, gates=6168)


class Ctx:
    pass


def build(cfg):
    nc = bass.Bass("TRN2", target_bir_lowering=False)
    P = Prog(nc)
    C = Ctx()
    C.P, C.cfg = P, cfg
    L = cfg.get("nlayers", 2)
    real = cfg.get("mixers", "abcd")
    taps = cfg.get("taps", ())
    I = {}
    shapes = dict(x=[T, D], ln_in_g=[D], ln_in_b=[D], w_in=[2, D, DIN], ssd_conv_w=[2, 4, 1024], ssd_conv_b=[2, 1024],
                  ssd_dt_bias=[2, 8], ssd_a_log=[2, 8], ssd_d=[2, 8], ssd_norm_w=[2, 512], dn_conv_w=[2, 4, 1536],
                  dn_a_log=[2, 4], dn_dt_bias=[2, 4], dn_norm_w=[2, 128], sg_ln_g=[2, 512], sg_ln_b=[2, 512],
                  sg_w=[2, 4, 128, 128], sg_b=[2, 4, 128], fox_f_bias=[2, 8], gate_b=[2, 4, 1024],
                  w_branch=[2, 4, 512, 1024], w_out=[2, D, D], ln1_g=[2, D], ln1_b=[2, D], w_up=[2, D, 4096],
                  w_down=[2, 4096, D], ln2_g=[2, D], ln2_b=[2, D])
    for k, s in shapes.items():
        I[k] = P.dram(k, s, F32, kind="ExternalInput")
    for m in "abcd":
        if m not in real:
            I["yinj_" + m] = P.dram("yinj_" + m, [2, 512, T], F32, kind="ExternalInput")
    C.I = I
    C.out = P.dram("out", [T, D], F32, kind="ExternalOutput")
    C.tap = {}
    for t in taps:
        C.tap[t] = P.dram("tap_" + t, [512 if t.startswith("y") else D, T], F32, kind="ExternalOutput")
    def tiled(v):
        return [V(Buf(v.buf.name + str(i)), v.ap) for i in range(8)]
    C.hres_d = tiled(P.dram("hres_d", [D, T], F32))
    C.hT_d = tiled(P.dram("hT_d", [D, T], BF16))
    C.G_d = [tiled(P.dram(f"G_d{i}", [D, T], BF16)) for i in range(4)]
    for t in list(C.tap):
        C.tap[t] = tiled(C.tap[t])
    C.ps = [P.psum(f"ps{i}") for i in range(8)]
    C.psi = 0
    C.identf = P.sb("identf", [128, 128], F32)
    C.identb = P.sb("identb", [128, 128], BF16)
    C.onesb = P.sb("onesb", [128, 128], BF16)
    P.I("pool", "memset", ap=C.identf, constant=1.0, _extra_w=[C.identf])
    P.I("pool", "affine_select", out=C.identf, in_=C.identf, pattern=[[-1, 128]], compare_op=ALU.is_equal, fill=0.0,
        base=0, channel_multiplier=1)
    P.I("dve", "tensor_copy", out=C.identb, in_=C.identf)
    P.I("pool", "memset", ap=C.onesb, constant=1.0, _extra_w=[C.onesb])

    stage_entry(C)
    for l in range(L):
        phase_a(C, l)
        phase_b(C, l)
        phase_c(C, l, last=(l == L - 1))
    cnt = P.emit()
    return nc, P, cnt


def dbg(C, name, v, cast=False):
    if name not in C.cfg.get("dbg", ()):
        return
    P = C.P
    shp = list(v.ap.shape)
    if cast:
        mk = P.mark()
        tmp = P.sb("dbgtmp", shp, F32)
        P.I("dve", "tensor_copy", out=tmp, in_=v)
        v = tmp
    d = P.dram("dbg_" + name, shp, F32, kind="ExternalOutput")
    P.dma(d, v, q="sp")
    if cast:
        P.release(mk)


def nextps(C):
    v = C.ps[C.psi % 8]
    C.psi += 1
    return v


def colvec(C, name, src, J, q="sp"):
    P = C.P
    t = P.sb(name, [128, J], F32)
    P.dma(t, src.r("(j p) -> p j", p=128), q=q, allow_slow_non_contiguous=True)
    return t


def dsl(v, t0, n):
    if isinstance(v, list):
        v = v[t0 // 512]
    return v.r("(j p) t -> p j t", p=128)[:, :, t0:t0 + n]


def ln_fm(C, xf, g, b, N, out_f=None, out_b=None):
    P = C.P
    m = P.mark()
    xb = P.sb("ln_xb", [128, 8, N], BF16)
    sq = P.sb("ln_sq", [128, 8, N], BF16)
    st = P.sb("ln_st", [128, 3, N], F32)
    P.I("act", "activation", out=xb, in_=xf, func=AF.Copy)
    P.I("act", "activation", out=sq, in_=xf, func=AF.Square)
    p1, p2 = nextps(C), nextps(C)
    for k in range(8):
        P.mm(out=p1[:, 0:N], lhsT=C.onesb, rhs=xb[:, k, :], start=(k == 0), stop=(k == 7))
    for k in range(8):
        P.mm(out=p2[:, 0:N], lhsT=C.onesb, rhs=sq[:, k, :], start=(k == 0), stop=(k == 7))
    mean, msq, rstd = st[:, 0, :], st[:, 1, :], st[:, 2, :]
    P.I("dve", "tensor_scalar", out=mean, in0=p1[:, 0:N], scalar1=1.0 / D, scalar2=0.0, op0=ALU.mult, op1=ALU.add, _partial=True)
    P.I("dve", "tensor_tensor", out=msq, in0=mean, in1=mean, op=ALU.mult, _partial=True)
    P.I("dve", "scalar_tensor_tensor", out=rstd, in0=p2[:, 0:N], scalar=1.0 / D, in1=msq, op0=ALU.mult, op1=ALU.subtract,
        _partial=True)
    P.I("act", "activation", out=rstd, in_=rstd, func=AF.Ln, bias=1e-5, scale=1.0, _partial=True)
    P.I("act", "activation", out=rstd, in_=rstd, func=AF.Exp, scale=-0.5, _partial=True)
    P.I("dve", "tensor_tensor", out=xf, in0=xf, in1=mean[:, None, :].bc([128, 8, N]), op=ALU.subtract)
    P.I("dve", "tensor_tensor", out=xf, in0=xf, in1=rstd[:, None, :].bc([128, 8, N]), op=ALU.mult)
    for k in range(8):
        if out_b is not None:
            P.I("act", "activation", out=out_b[:, k, :], in_=xf[:, k, :], func=AF.Identity, scale=g[:, k:k + 1],
                bias=b[:, k:k + 1], _partial=True)
        if out_f is not None:
            P.I("act", "activation", out=out_f[:, k, :], in_=xf[:, k, :], func=AF.Identity, scale=g[:, k:k + 1],
                bias=b[:, k:k + 1], _partial=True)
    P.release(m)


def stage_entry(C):
    P, I = C.P, C.I
    m = P.mark()
    g = colvec(C, "ln0g", I["ln_in_g"], 8)
    b = colvec(C, "ln0b", I["ln_in_b"], 8)
    N = 512
    xin = [P.sb(f"xin{i}", [128, 4, D], F32) for i in range(2)]
    xfs = [P.sb(f"xf{i}", [128, 8, N], F32) for i in range(2)]
    hfs = [P.sb(f"hf{i}", [128, 8, N], F32) for i in range(2)]
    hbs = [P.sb(f"hb{i}", [128, 8, N], BF16) for i in range(2)]
    for t in range(T // N):
        xi, xf, hf, hb = xin[t % 2], xfs[t % 2], hfs[t % 2], hbs[t % 2]
        P.dma(xi, C.I["x"][t * N:(t + 1) * N, :].r("(s p) d -> p s d", p=128))
        for k in range(8):
            pk = nextps(C)
            for s in range(4):
                P.tr(out=pk[:, s * 128:(s + 1) * 128], in_=xi[:, s, k * 128:(k + 1) * 128], ident=C.identf)
            P.I("act", "activation", out=xf[:, k, :], in_=pk, func=AF.Copy, _partial=True)
        ln_fm(C, xf, g, b, N, out_f=hf, out_b=hb)
        P.dma(dsl(C.hres_d, t * N, N), hf, q="sp")
        P.dma(dsl(C.hT_d, t * N, N), hb, q="sp")
        if "h0" in C.tap:
            P.dma(dsl(C.tap["h0"], t * N, N), hf, q="sp")
    P.release(m)


def load_w(C, dst, src, q="pool"):
    C.P.dma(dst, src, q=q)


def gate_pass(C, l, i, hT, Y):
    P, I = C.P, C.I
    m = P.mark()
    wg = P.sb("wg", [128, 8, 1024], BF16)
    wb = P.sb("wb", [128, 4, 1024], BF16)
    c0 = OFF["gates"] + i * 1024
    for k in range(8):
        load_w(C, wg[:, k, :], I["w_in"][l, k * 128:(k + 1) * 128, c0:c0 + 1024])
    for k in range(4):
        load_w(C, wb[:, k, :], I["w_branch"][l, i, k * 128:(k + 1) * 128, :])
    gb = colvec(C, "gb", I["gate_b"][l, i], 8)
    N = 512
    gts = [P.sb(f"gt{i}", [128, N], F32) for i in range(3)]
    gbuf = [P.sb(f"gbuf{i}", [128, 8, N], BF16) for i in range(4)]
    n = 0
    for t in range(T // N):
        gbf = gbuf[t % 4]
        sl = slice(t * N, (t + 1) * N)
        for j in range(8):
            p1, p2 = nextps(C), nextps(C)
            for k in range(8):
                P.mm(out=p1, lhsT=wg[:, k, j * 128:(j + 1) * 128], rhs=hT[:, k, sl], start=(k == 0), stop=(k == 7))
            gt = gts[n % 3]
            n += 1
            P.I("act", "activation", out=gt, in_=p1, func=AF.Sigmoid, bias=gb[:, j:j + 1])
            for k in range(4):
                P.mm(out=p2, lhsT=wb[:, k, j * 128:(j + 1) * 128], rhs=Y[:, k, sl], start=(k == 0), stop=(k == 3))
            P.I("dve", "tensor_tensor", out=gbf[:, j, :], in0=p2, in1=gt, op=ALU.mult, _partial=True)
        P.dma(dsl(C.G_d[i], t * N, N), gbf, q="sp")
    P.release(m)


def phase_a(C, l):
    P, I = C.P, C.I
    m = P.mark()
    hT = P.sb("hT", [128, 8, T], BF16)
    for t in range(8):
        P.dma(hT[:, :, t * 512:(t + 1) * 512], dsl(C.hT_d, t * 512, 512))
    real = C.cfg.get("mixers", "abcd")
    fns = dict(a=mixer_ssd, b=mixer_dn, c=mixer_sg, d=mixer_fox)
    for i, mx in enumerate("abcd"):
        m2 = P.mark()
        Y = P.sb("Y", [128, 4, T], BF16)
        if mx in real:
            fns[mx](C, l, hT, Y)
        else:
            for k in range(4):
                load_w(C, Y[:, k, :], I["yinj_" + mx][l, k * 128:(k + 1) * 128, :])
        tn = f"y_{mx}_{l}"
        if tn in C.tap:
            yf = P.sb("ytap", [128, 4, 1024], F32)
            for t4 in range(4):
                P.I("act", "activation", out=yf, in_=Y[:, :, t4 * 1024:(t4 + 1) * 1024], func=AF.Copy)
                for t2 in range(2):
                    P.dma(dsl(C.tap[tn], t4 * 1024 + t2 * 512, 512), yf[:, :, t2 * 512:(t2 + 1) * 512], q="sp")
        gate_pass(C, l, i, hT, Y)
        P.release(m2)
    P.release(m)


def phase_b(C, l):
    P, I = C.P, C.I
    m = P.mark()
    wo = P.sb("wo", [128, 8, 1024], BF16)
    for k in range(8):
        load_w(C, wo[:, k, :], I["w_out"][l, k * 128:(k + 1) * 128, :])
    g = colvec(C, "ln1g", I["ln1_g"][l], 8)
    b = colvec(C, "ln1b", I["ln1_b"][l], 8)
    N = 512
    gl = [P.sb(f"gl{i}", [128, 8, N], BF16) for i in range(8)]
    macc = P.sb("macc", [128, 8, N], F32)
    mb = P.sb("mb", [128, 8, N], BF16)
    hr = P.sb("hr", [128, 8, N], F32)
    x1s = [P.sb(f"x1_{i}", [128, 8, N], F32) for i in range(2)]
    hbs = [P.sb(f"hb_{i}", [128, 8, N], BF16) for i in range(2)]
    n = 0
    for t in range(T // N):
        x1 = x1s[t % 2]
        hf = x1
        hb = hbs[t % 2]
        P.dma(hr, dsl(C.hres_d, t * N, N), q="sp")
        gs4 = []
        for i in range(4):
            gg = gl[n % 8]
            n += 1
            P.dma(gg, dsl(C.G_d[i], t * N, N), q="sp")
            gs4.append(gg)
        P.I("dve", "tensor_tensor", out=macc, in0=gs4[0], in1=gs4[1], op=ALU.add)
        P.I("dve", "tensor_tensor", out=macc, in0=macc, in1=gs4[2], op=ALU.add)
        P.I("dve", "tensor_tensor", out=macc, in0=macc, in1=gs4[3], op=ALU.add)
        tn = f"merged_{l}"
        if tn in C.tap:
            P.dma(dsl(C.tap[tn], t * N, N), macc, q="sp")
        P.I("act", "activation", out=mb, in_=macc, func=AF.Copy)
        for j in range(8):
            p1 = nextps(C)
            for k in range(8):
                P.mm(out=p1, lhsT=wo[:, k, j * 128:(j + 1) * 128], rhs=mb[:, k, :], start=(k == 0), stop=(k == 7))
            P.I("dve", "scalar_tensor_tensor", out=x1[:, j, :], in0=hr[:, j, :], scalar=ALPHA, in1=p1, op0=ALU.mult,
                op1=ALU.add, _partial=True)
        ln_fm(C, x1, g, b, N, out_f=hf, out_b=hb)
        P.dma(dsl(C.hres_d, t * N, N), hf, q="sp")
        P.dma(dsl(C.hT_d, t * N, N), hb, q="sp")
        tn = f"h1_{l}"
        if tn in C.tap:
            P.dma(dsl(C.tap[tn], t * N, N), hf, q="sp")
    P.release(m)


def phase_c(C, l, last):
    P, I = C.P, C.I
    m = P.mark()
    wu = P.sb("wu", [128, 8, 4096], BF16)
    wd = P.sb("wd", [128, 32, 1024], BF16)
    for k in range(8):
        for hh in range(2):
            load_w(C, wu[:, k, hh * 2048:(hh + 1) * 2048], I["w_up"][l, k * 128:(k + 1) * 128, hh * 2048:(hh + 1) * 2048])
    for f in range(32):
        load_w(C, wd[:, f, :], I["w_down"][l, f * 128:(f + 1) * 128, :])
    g = colvec(C, "ln2g", I["ln2_g"][l], 8)
    b = colvec(C, "ln2b", I["ln2_b"][l], 8)
    N = 256
    hbt = [P.sb(f"c_hb{i}", [128, 8, N], BF16) for i in range(2)]
    hrt = [P.sb(f"c_hr{i}", [128, 8, N], F32) for i in range(1)]
    act = P.sb("c_act", [128, 32, N], BF16)
    rl = [P.sb(f"c_rl{i}", [128, N], F32) for i in range(3)]
    x2s = [P.sb(f"c_x2_{i}", [128, 8, N], F32) for i in range(2)]
    hb = None if last else P.sb("c_hbo", [128, 8, N], BF16)
    ot = [P.sb(f"c_ot{i}", [128, D], F32) for i in range(2)] if last else None
    n = 0
    no = 0
    for t in range(T // N):
        h1b, h1 = hbt[t % 2], hrt[0]
        x2 = x2s[t % 2]
        hf = x2
        P.dma(h1b, dsl(C.hT_d, t * N, N), q="sp")
        P.dma(h1, dsl(C.hres_d, t * N, N), q="sp")
        for f in range(32):
            p1 = nextps(C)
            for k in range(8):
                P.mm(out=p1[:, 0:N], lhsT=wu[:, k, f * 128:(f + 1) * 128], rhs=h1b[:, k, :], start=(k == 0), stop=(k == 7))
            r = rl[n % 3]
            n += 1
            P.I("act", "activation", out=r, in_=p1[:, 0:N], func=AF.Relu)
            P.I("dve", "tensor_tensor", out=act[:, f, :], in0=r, in1=r, op=ALU.mult, _partial=True)
        for j in range(8):
            p1 = nextps(C)
            for f in range(32):
                P.mm(out=p1[:, 0:N], lhsT=wd[:, f, j * 128:(j + 1) * 128], rhs=act[:, f, :], start=(f == 0), stop=(f == 31))
            P.I("dve", "scalar_tensor_tensor", out=x2[:, j, :], in0=h1[:, j, :], scalar=ALPHA, in1=p1[:, 0:N], op0=ALU.mult,
                op1=ALU.add, _partial=True)
        ln_fm(C, x2, g, b, N, out_f=hf, out_b=(None if last else hb))
        tn = f"h2_{l}"
        if tn in C.tap:
            P.dma(dsl(C.tap[tn], t * N, N), hf, q="sp")
        if not last:
            P.dma(dsl(C.hres_d, t * N, N), hf, q="sp")
            P.dma(dsl(C.hT_d, t * N, N), hb, q="sp")
        else:
            for s in range(N // 128):
                o = ot[no % 2]
                no += 1
                for half in range(2):
                    pk = nextps(C)
                    for kk in range(4):
                        k = half * 4 + kk
                        P.tr(out=pk[:, kk * 128:(kk + 1) * 128], in_=hf[:, k, s * 128:(s + 1) * 128], ident=C.identf)
                    P.I("act", "activation", out=o[:, half * 512:(half + 1) * 512], in_=pk, func=AF.Copy, _partial=True)
                r0 = t * N + s * 128
                P.dma(C.out[r0:r0 + 128, :], o, q="sp")
    P.release(m)


def make_tri(C, name):
    P = C.P
    t = P.sb(name, [128, 128], F32)
    P.I("pool", "memset", ap=t, constant=1.0, _extra_w=[t])
    P.I("pool", "affine_select", out=t, in_=t, pattern=[[1, 128]], compare_op=ALU.is_ge, fill=0.0, base=0,
        channel_multiplier=-1)
    return t


def conv4(C, acc, raw, cw, nj, NT):
    P = C.P
    for j in range(nj):
        P.I("dve", "tensor_scalar", out=acc[:, j, :], in0=raw[:, j, 3:3 + NT], scalar1=cw[:, 3, j:j + 1], scalar2=0.0,
            op0=ALU.mult, op1=ALU.add, _partial=True)
        for k in range(3):
            P.I("dve", "scalar_tensor_tensor", out=acc[:, j, :], in0=raw[:, j, k:k + NT], scalar=cw[:, k, j:j + 1],
                in1=acc[:, j, :], op0=ALU.mult, op1=ALU.add, _partial=True)


def mixer_ssd(C, l, hT, Y):
    P, I = C.P, C.I
    m = P.mark()
    NT = 256
    NC = NT // 128
    wz = P.sb("sd_wz", [128, 8, 512], BF16)
    wx = P.sb("sd_wx", [128, 8, 1024], BF16)
    wdt = P.sb("sd_wdt", [128, 8, 8], BF16)
    for k in range(8):
        rows = slice(k * 128, (k + 1) * 128)
        load_w(C, wz[:, k, :], I["w_in"][l, rows, OFF["z"]:OFF["z"] + 512])
        load_w(C, wx[:, k, :], I["w_in"][l, rows, OFF["xs"]:OFF["xs"] + 1024])
        load_w(C, wdt[:, k, :], I["w_in"][l, rows, OFF["dt"]:OFF["dt"] + 8])
    cw = P.sb("sd_cw", [128, 4, 8], F32)
    for k in range(4):
        P.dma(cw[:, k, :], I["ssd_conv_w"][l, k].r("(j p) -> p j", p=128), allow_slow_non_contiguous=True)
    cb = colvec(C, "sd_cb", I["ssd_conv_b"][l], 8)
    dtb = bcast_rows(C, "sd_dtb", I["ssd_dt_bias"][l], 8)
    abc = bcast_rows(C, "sd_abc", I["ssd_a_log"][l], 8)
    P.I("act", "activation", out=abc, in_=abc, func=AF.Exp)
    P.I("dve", "tensor_scalar", out=abc, in0=abc, scalar1=-1.0, scalar2=0.0, op0=ALU.mult, op1=ALU.add)
    drep = P.sb("sd_drep", [128, 4], F32)
    for j in range(4):
        for hf in range(2):
            src = I["ssd_d"][l, 2 * j + hf:2 * j + hf + 1]
            P.dma(drep[hf * 64:(hf + 1) * 64, j:j + 1], V(src.buf, src.ap.partition_broadcast(64)))
    nws = colvec(C, "sd_nw", I["ssd_norm_w"][l], 4)
    P.I("dve", "tensor_scalar", out=nws, in0=nws, scalar1=float(512 ** 0.5), scalar2=0.0, op0=ALU.mult, op1=ALU.add)
    tri = make_tri(C, "sd_tri")
    S = P.sb("sd_S", [128, 8, 64], F32)
    Sb = P.sb("sd_Sb", [128, 512], BF16)
    P.I("pool", "memset", ap=S, constant=0.0, _extra_w=[S])
    P.I("pool", "memset", ap=Sb, constant=0.0, _extra_w=[Sb])
    raw = P.sb("sd_raw", [128, 8, 3 + NT], F32)
    P.I("pool", "memset", ap=raw[:, :, 0:3], constant=0.0, _extra_w=[raw])
    acc = P.sb("sd_acc", [128, 8, NT], F32)
    xs = P.sb("sd_xs", [128, 4, NT], F32)
    xsb = P.sb("sd_xsb", [128, 4, NT], BF16)
    bmT = P.sb("sd_bmT", [128, 2, NT], BF16)
    cmT = P.sb("sd_cmT", [128, 2, NT], BF16)
    zs = P.sb("sd_zs", [128, 4, NT], F32)
    y1 = P.sb("sd_y1", [128, 4, NT], F32)
    sq = P.sb("sd_sq", [128, 4, NT], BF16)
    rs = P.sb("sd_rs", [128, NT], F32)
    sm = P.sb("sd_sm", [128, 5, 8], F32)
    dAs = P.sb("sd_dAs", [128, 2, 8], BF16)
    dAsf = P.sb("sd_dAsf", [128, 2, 8], F32)
    dAb = P.sb("sd_dAb", [128, 2, 8, 128], BF16)
    trib = P.sb("sd_trib", [128, 128], BF16)
    P.I("dve", "tensor_copy", out=trib, in_=tri)
    E = P.sb("sd_E", [128, 8, 128], F32)
    Dt = P.sb("sd_Dt", [128, 8, 128], F32)
    MT = P.sb("sd_MT", [128, 8, 128], BF16)
    cms = P.sb("sd_cms", [128, 8, 128], BF16)
    xdt = P.sb("sd_xdt", [128, 8, 64], BF16)
    xdd = P.sb("sd_xdd", [128, 8, 64], BF16)
    bmtm = P.sb("sd_bmtm", [128, 2, 128], BF16)
    dt_, dA, acs, dte, dt2 = sm[:, 0, :], sm[:, 1, :], sm[:, 2, :], sm[:, 3, :], sm[:, 4, :]
    for t in range(T // NT):
        sl = slice(t * NT, (t + 1) * NT)
        if t > 0:
            P.I("dve", "tensor_copy", out=raw[:, :, 0:3], in_=raw[:, :, NT:NT + 3])
        for j in range(4):
            p1 = nextps(C)
            for k in range(8):
                P.mm(out=p1[:, 0:NT], lhsT=wz[:, k, j * 128:(j + 1) * 128], rhs=hT[:, k, sl], start=(k == 0), stop=(k == 7))
            P.I("act", "activation", out=zs[:, j, :], in_=p1[:, 0:NT], func=AF.Silu, _partial=True)
        for j in range(8):
            p1 = nextps(C)
            for k in range(8):
                P.mm(out=p1[:, 0:NT], lhsT=wx[:, k, j * 128:(j + 1) * 128], rhs=hT[:, k, sl], start=(k == 0), stop=(k == 7))
            P.I("act", "activation", out=raw[:, j, 3:3 + NT], in_=p1[:, 0:NT], func=AF.Copy, _partial=True)
        conv4(C, acc, raw, cw, 8, NT)
        for j in range(8):
            if j < 4:
                P.I("act", "activation", out=xs[:, j, :], in_=acc[:, j, :], func=AF.Silu, bias=cb[:, j:j + 1], _partial=True)
            else:
                dst = bmT[:, j - 4, :] if j < 6 else cmT[:, j - 6, :]
                P.I("act", "activation", out=dst, in_=acc[:, j, :], func=AF.Silu, bias=cb[:, j:j + 1], _partial=True)
        P.I("dve", "tensor_copy", out=xsb, in_=xs)
        stop = C.cfg.get("sd_stop", 99)
        for c in range(NC if stop > 1 else 0):
            c0 = t * NT + c * 128
            cl = slice(c * 128, (c + 1) * 128)
            p1 = nextps(C)
            for k in range(8):
                P.mm(out=p1[:, 0:8], lhsT=hT[:, k, c0:c0 + 128], rhs=wdt[:, k, :], start=(k == 0), stop=(k == 7))
            P.I("dve", "tensor_tensor", out=dt_, in0=p1[:, 0:8], in1=dtb, op=ALU.add, _partial=True)
            P.I("act", "activation", out=dt_, in_=dt_, func=AF.Exp, _partial=True)
            P.I("act", "activation", out=dt_, in_=dt_, func=AF.Ln, bias=1.0, scale=1.0, _partial=True)
            P.I("dve", "tensor_tensor", out=dA, in0=dt_, in1=abc, op=ALU.mult, _partial=True)
            if stop <= 1.1:
                continue
            P.I("dve", "tensor_copy", out=dAs[:, 0, :], in_=dA, _partial=True)
            P.I("dve", "tensor_tensor", out=dt2, in0=dA, in1=dAs[:, 0, :], op=ALU.subtract, _partial=True)
            P.I("dve", "tensor_copy", out=dAs[:, 1, :], in_=dt2, _partial=True)
            P.I("dve", "tensor_copy", out=dAsf, in_=dAs)
            for i2 in range(2):
                for h in range(8):
                    P.I("dve", "tensor_scalar", out=dAb[:, i2, h, :], in0=C.onesb, scalar1=dAsf[:, i2, h:h + 1], scalar2=0.0,
                        op0=ALU.mult, op1=ALU.add, _partial=True)
            if stop <= 1.2:
                continue
            p2 = nextps(C)
            P.mm(out=p2[:, 0:8], lhsT=trib, rhs=dAs[:, 0, :], start=True, stop=False)
            P.mm(out=p2[:, 0:8], lhsT=trib, rhs=dAs[:, 1, :], start=False, stop=True)
            P.I("dve", "tensor_copy", out=acs, in_=p2[:, 0:8], _partial=True)
            if stop <= 1.3:
                continue
            pB = [nextps(C), nextps(C)]
            for h in range(8):
                o = pB[h // 4][:, (h % 4) * 128:(h % 4 + 1) * 128]
                P.mm(out=o, lhsT=dAb[:, 0, h, :], rhs=trib, start=True, stop=False)
                P.mm(out=o, lhsT=dAb[:, 1, h, :], rhs=trib, start=False, stop=True)
            if stop <= 1.4:
                continue
            for hh in range(2):
                hs = slice(hh * 4, (hh + 1) * 4)
                pv = pB[hh].r("p (h l) -> p h l", h=4)
                P.I("act", "activation", out=E[:, hs, :], in_=pv, func=AF.Exp, _partial=True)
                if stop <= 1.5:
                    continue
                for h4 in range(4):
                    h = hh * 4 + h4
                    P.I("dve", "tensor_scalar", out=Dt[:, h, :], in0=pv[:, h4, :], scalar1=acs[:, h:h + 1], scalar2=0.0,
                        op0=ALU.subtract, op1=ALU.min, _partial=True)
                if stop <= 1.6:
                    continue
                P.I("dve", "tensor_tensor", out=dte[:, hs], in0=pv[:, :, 127], in1=acs[:, hs], op=ALU.subtract, _partial=True)
            if stop <= 1.7:
                continue
            P.I("act", "activation", out=Dt, in_=Dt, func=AF.Exp)
            P.I("act", "activation", out=dte, in_=dte, func=AF.Exp, _partial=True)
            if stop <= 1.8:
                continue
            P.I("dve", "tensor_tensor", out=Dt, in0=Dt, in1=tri[:, None, :].bc([128, 8, 128]), op=ALU.mult)
            if stop <= 2:
                continue
            pcb = nextps(C)
            for g in range(2):
                P.mm(out=pcb[:, g * 128:(g + 1) * 128], lhsT=bmT[:, g, cl], rhs=cmT[:, g, cl])
            for g in range(2):
                hs = slice(g * 4, (g + 1) * 4)
                P.I("dve", "tensor_tensor", out=MT[:, hs, :], in0=Dt[:, hs, :],
                    in1=pcb[:, g * 128:(g + 1) * 128][:, None, :].bc([128, 4, 128]), op=ALU.mult, _partial=True)
                P.I("dve", "tensor_tensor", out=cms[:, hs, :], in0=E[:, hs, :],
                    in1=cmT[:, g, cl][:, None, :].bc([128, 4, 128]), op=ALU.mult, _partial=True)
            if stop <= 3:
                continue
            pt = nextps(C).bitcast(BF16)
            for j in range(4):
                P.tr(out=pt[:, j * 128:(j + 1) * 128], in_=xsb[:, j, cl], ident=C.identb)
            for h in range(8):
                P.I("dve", "tensor_scalar", out=xdt[:, h, :], in0=pt[:, h * 64:(h + 1) * 64], scalar1=dt_[:, h:h + 1],
                    scalar2=0.0, op0=ALU.mult, op1=ALU.add, _partial=True)
                P.I("dve", "tensor_scalar", out=xdd[:, h, :], in0=pt[:, h * 64:(h + 1) * 64], scalar1=dt_[:, h:h + 1],
                    scalar2=dte[:, h:h + 1], op0=ALU.mult, op1=ALU.mult, _partial=True)
            pb2 = nextps(C).bitcast(BF16)
            for g in range(2):
                P.tr(out=pb2[:, g * 128:(g + 1) * 128], in_=bmT[:, g, cl], ident=C.identb)
            P.I("act", "activation", out=bmtm, in_=pb2[:, 0:256].r("p (g n) -> p g n", g=2), func=AF.Copy)
            if stop <= 4:
                continue
            py = nextps(C)
            for h in range(8):
                po = (h % 2) * 64
                o = py[po:po + 64, (h // 2) * 128:(h // 2 + 1) * 128]
                P.mm(out=o, lhsT=xdt[:, h, :], rhs=MT[:, h, :], start=True, stop=False)
                P.mm(out=o, lhsT=Sb[:, h * 64:(h + 1) * 64], rhs=cms[:, h, :], start=False, stop=True)
            pS = nextps(C)
            for g in range(2):
                P.mm(out=pS[:, g * 256:(g + 1) * 256], lhsT=bmtm[:, g, :], rhs=xdd[:, g * 4:(g + 1) * 4, :])
            for h in range(8):
                P.I("dve", "scalar_tensor_tensor", out=S[:, h, :], in0=S[:, h, :], scalar=E[:, h, 127:128],
                    in1=pS[:, h * 64:(h + 1) * 64], op0=ALU.mult, op1=ALU.add, _partial=True)
            P.I("act", "activation", out=Sb, in_=S.r("p h d -> p (h d)"), func=AF.Copy)
            for j in range(4):
                P.I("dve", "scalar_tensor_tensor", out=y1[:, j, cl], in0=xs[:, j, cl], scalar=drep[:, j:j + 1],
                    in1=py[:, j * 128:(j + 1) * 128], op0=ALU.mult, op1=ALU.add, _partial=True)
        P.I("dve", "tensor_tensor", out=y1, in0=y1, in1=zs, op=ALU.mult)
        P.I("act", "activation", out=sq, in_=y1, func=AF.Square)
        pr = nextps(C)
        for j in range(4):
            P.mm(out=pr[:, 0:NT], lhsT=C.onesb, rhs=sq[:, j, :], start=(j == 0), stop=(j == 3))
        P.I("act", "activation", out=rs, in_=pr[:, 0:NT], func=AF.Ln, bias=float(512 * 1e-6), scale=1.0)
        P.I("act", "activation", out=rs, in_=rs, func=AF.Exp, scale=-0.5)
        P.I("dve", "tensor_tensor", out=y1, in0=y1, in1=rs[:, None, :].bc([128, 4, NT]), op=ALU.mult)
        for j in range(4):
            P.I("act", "activation", out=Y[:, j, sl], in_=y1[:, j, :], func=AF.Identity, scale=nws[:, j:j + 1], _partial=True)
    P.release(m)


def mixer_dn(C, l, hT, Y):
    P, I = C.P, C.I
    m = P.mark()
    NT = 128
    wq = P.sb("dn_wq", [128, 8, 512], BF16)
    wk = P.sb("dn_wk", [128, 8, 512], BF16)
    wv = P.sb("dn_wv", [128, 8, 512], BF16)
    wg = P.sb("dn_wg", [128, 8, 512], BF16)
    wba = P.sb("dn_wba", [128, 8, 8], BF16)
    for k in range(8):
        rows = slice(k * 128, (k + 1) * 128)
        load_w(C, wq[:, k, :], I["w_in"][l, rows, OFF["dq"]:OFF["dq"] + 512])
        load_w(C, wk[:, k, :], I["w_in"][l, rows, OFF["dk"]:OFF["dk"] + 512])
        load_w(C, wv[:, k, :], I["w_in"][l, rows, OFF["dv"]:OFF["dv"] + 512])
        load_w(C, wg[:, k, :], I["w_in"][l, rows, OFF["dgate"]:OFF["dgate"] + 512])
        load_w(C, wba[:, k, :], I["w_in"][l, rows, OFF["dbeta"]:OFF["dbeta"] + 8])
    cw = P.sb("dn_cw", [128, 4, 12], F32)
    for k in range(4):
        P.dma(cw[:, k, :], I["dn_conv_w"][l, k].r("(j p) -> p j", p=128), allow_slow_non_contiguous=True)
    dtb = bcast_rows(C, "dn_dtb", I["dn_dt_bias"][l], 4)
    nal = bcast_rows(C, "dn_nal", I["dn_a_log"][l], 4)
    P.I("act", "activation", out=nal, in_=nal, func=AF.Exp)
    P.I("dve", "tensor_scalar", out=nal, in0=nal, scalar1=-1.0, scalar2=0.0, op0=ALU.mult, op1=ALU.add)
    nw = colvec(C, "dn_nw", I["dn_norm_w"][l], 1)
    def mk_mask(name, strict, lower):
        t = P.sb(name, [128, 128], F32)
        P.I("pool", "memset", ap=t, constant=1.0, _extra_w=[t])
        sgn = -1 if not lower else 1
        P.I("pool", "affine_select", out=t, in_=t, pattern=[[-sgn, 128]], compare_op=(ALU.is_gt if strict else ALU.is_ge),
            fill=0.0, base=0, channel_multiplier=sgn)
        P.I("pool", "memset", ap=t[0:64, 64:128], constant=0.0, _extra_w=[t])
        P.I("pool", "memset", ap=t[64:128, 0:64], constant=0.0, _extra_w=[t])
        return t
    Mui = mk_mask("dn_Mui", False, False)
    Mls = mk_mask("dn_Mls", True, True)
    tri2b = P.sb("dn_tri2b", [128, 128], BF16)
    P.I("dve", "tensor_copy", out=tri2b, in_=Mui)
    S = P.sb("dn_S", [128, 4, 128], F32)
    Sb = P.sb("dn_Sb", [128, 4, 128], BF16)
    P.I("pool", "memset", ap=S, constant=0.0, _extra_w=[S])
    P.I("pool", "memset", ap=Sb, constant=0.0, _extra_w=[Sb])
    raw = P.sb("dn_raw", [128, 12, 3 + NT], F32)
    P.I("pool", "memset", ap=raw[:, :, 0:3], constant=0.0, _extra_w=[raw])
    acc = P.sb("dn_acc", [128, 12, NT], F32)
    qk = P.sb("dn_qk", [128, 8, NT], F32)
    vTb = P.sb("dn_vTb", [128, 4, NT], BF16)
    sq = P.sb("dn_sq", [128, 8, NT], BF16)
    rn = [P.sb(f"dn_rn{i}", [128, NT], F32) for i in range(2)]
    qT = P.sb("dn_qT", [128, 4, NT], BF16)
    kT = P.sb("dn_kT", [128, 4, NT], BF16)
    gs = P.sb("dn_gs", [128, 4, NT], F32)
    oT = P.sb("dn_oT", [128, 4, NT], F32)
    sm = P.sb("dn_sm", [128, 8, 4], F32)
    beta, g_, gcs, egc, sc1, dk, tmp4 = (sm[:, i, :] for i in range(7))
    gsp = P.sb("dn_gsp", [128, 2, 4], BF16)
    gspf = P.sb("dn_gspf", [128, 2, 4], F32)
    gb = P.sb("dn_gb", [128, 2, 4, 128], BF16)
    Eg = P.sb("dn_Eg", [128, 4, 128], F32)
    GL = P.sb("dn_GL", [128, 4, 128], F32)
    G = P.sb("dn_G", [128, 4, 128], F32)
    A32 = P.sb("dn_A32", [128, 4, 128], F32)
    N32 = P.sb("dn_N32", [128, 4, 128], F32)
    Ab = P.sb("dn_Ab", [128, 4, 128], BF16)
    Nb = P.sb("dn_Nb", [128, 4, 128], BF16)
    R = P.sb("dn_R", [128, 4, 128], F32)
    Rb = P.sb("dn_Rb", [128, 4, 128], BF16)
    Xb = [P.sb(f"dn_Xb{i}", [128, 4, 128], BF16) for i in range(2)]
    XTb = [P.sb(f"dn_XTb{i}", [128, 4, 128], BF16) for i in range(2)]
    TTh = P.sb("dn_TTh", [128, 4, 128], BF16)
    TTl = P.sb("dn_TTl", [128, 4, 128], BF16)
    kbeg = P.sb("dn_kbeg", [128, 4, 128], BF16)
    kdec = P.sb("dn_kdec", [128, 4, 128], BF16)
    vb = P.sb("dn_vb", [128, 4, 128], BF16)
    wTb = P.sb("dn_wTb", [128, 4, 128], BF16)
    u = P.sb("dn_u", [128, 4, 128], F32)
    qgT = P.sb("dn_qgT", [128, 4, 128], BF16)
    qkTm = P.sb("dn_qkTm", [128, 4, 128], BF16)
    vnb = P.sb("dn_vnb", [128, 4, 128], BF16)
    QS = float(128 ** -0.5)
    dstop = C.cfg.get("dn_stop", 99)
    for t in range(T // NT):
        sl = slice(t * NT, (t + 1) * NT)
        if t > 0:
            P.I("dve", "tensor_copy", out=raw[:, :, 0:3], in_=raw[:, :, NT:NT + 3])
        for j in range(12):
            w_ = (wq, wk, wv)[j // 4]
            jj = j % 4
            p1 = nextps(C)
            for k in range(8):
                P.mm(out=p1[:, 0:NT], lhsT=w_[:, k, jj * 128:(jj + 1) * 128], rhs=hT[:, k, sl], start=(k == 0), stop=(k == 7))
            P.I("act", "activation", out=raw[:, j, 3:3 + NT], in_=p1[:, 0:NT], func=AF.Copy, _partial=True)
        for j in range(4):
            p1 = nextps(C)
            for k in range(8):
                P.mm(out=p1[:, 0:NT], lhsT=wg[:, k, j * 128:(j + 1) * 128], rhs=hT[:, k, sl], start=(k == 0), stop=(k == 7))
            P.I("act", "activation", out=gs[:, j, :], in_=p1[:, 0:NT], func=AF.Silu, _partial=True)
        conv4(C, acc, raw, cw, 12, NT)
        P.I("act", "activation", out=qk, in_=acc[:, 0:8, :], func=AF.Silu)
        P.I("act", "activation", out=vTb, in_=acc[:, 8:12, :], func=AF.Silu)
        P.I("act", "activation", out=sq, in_=qk, func=AF.Square)
        for i in range(8):
            p1 = nextps(C)
            P.mm(out=p1[:, 0:NT], lhsT=C.onesb, rhs=sq[:, i, :])
            r_ = rn[i % 2]
            P.I("act", "activation", out=r_, in_=p1[:, 0:NT], func=AF.Ln, bias=1e-6, scale=1.0)
            P.I("act", "activation", out=r_, in_=r_, func=AF.Exp, scale=-0.5)
            dst = qT[:, i, :] if i < 4 else kT[:, i - 4, :]
            P.I("dve", "scalar_tensor_tensor", out=dst, in0=qk[:, i, :], scalar=(QS if i < 4 else 1.0), in1=r_,
                op0=ALU.mult, op1=ALU.mult, _partial=True)
        if dstop <= 1:
            continue
        p1 = nextps(C)
        for k in range(8):
            P.mm(out=p1[:, 0:8], lhsT=hT[:, k, sl], rhs=wba[:, k, :], start=(k == 0), stop=(k == 7))
        P.I("act", "activation", out=beta, in_=p1[:, 0:4], func=AF.Sigmoid, _partial=True)
        P.I("dve", "tensor_tensor", out=g_, in0=p1[:, 4:8], in1=dtb, op=ALU.add, _partial=True)
        P.I("act", "activation", out=g_, in_=g_, func=AF.Exp, _partial=True)
        P.I("act", "activation", out=g_, in_=g_, func=AF.Ln, bias=1.0, scale=1.0, _partial=True)
        P.I("dve", "tensor_tensor", out=g_, in0=g_, in1=nal, op=ALU.mult, _partial=True)
        P.I("dve", "tensor_copy", out=gsp[:, 0, :], in_=g_, _partial=True)
        P.I("dve", "tensor_tensor", out=tmp4, in0=g_, in1=gsp[:, 0, :], op=ALU.subtract, _partial=True)
        P.I("dve", "tensor_copy", out=gsp[:, 1, :], in_=tmp4, _partial=True)
        P.I("dve", "tensor_copy", out=gspf, in_=gsp)
        for i2 in range(2):
            for h in range(4):
                P.I("dve", "tensor_scalar", out=gb[:, i2, h, :], in0=C.onesb, scalar1=gspf[:, i2, h:h + 1], scalar2=0.0,
                    op0=ALU.mult, op1=ALU.add, _partial=True)
        p2 = nextps(C)
        P.mm(out=p2[:, 0:4], lhsT=tri2b, rhs=gsp[:, 0, :], start=True, stop=False)
        P.mm(out=p2[:, 0:4], lhsT=tri2b, rhs=gsp[:, 1, :], start=False, stop=True)
        P.I("dve", "tensor_copy", out=gcs, in_=p2[:, 0:4], _partial=True)
        pBg = nextps(C)
        for h in range(4):
            o = pBg[:, h * 128:(h + 1) * 128]
            P.mm(out=o, lhsT=gb[:, 0, h, :], rhs=tri2b, start=True, stop=False)
            P.mm(out=o, lhsT=gb[:, 1, h, :], rhs=tri2b, start=False, stop=True)
        pBv = pBg.r("p (h i) -> p h i", h=4)
        P.I("act", "activation", out=Eg, in_=pBv, func=AF.Exp)
        for h in range(4):
            P.I("dve", "tensor_scalar", out=G[:, h, :], in0=pBv[:, h, :], scalar1=gcs[:, h:h + 1], scalar2=0.0,
                op0=ALU.subtract, op1=ALU.min, _partial=True)
            P.I("dve", "tensor_scalar", out=GL[:, h, :], in0=pBv[:, h, :], scalar1=gcs[:, h:h + 1], scalar2=0.0,
                op0=ALU.subtract, op1=ALU.max, _partial=True)
        P.I("dve", "tensor_tensor", out=dk[0:64, :], in0=pBv[0:64, :, 63], in1=gcs[0:64, :], op=ALU.subtract, _partial=True)
        P.I("dve", "tensor_tensor", out=dk[64:128, :], in0=pBv[64:128, :, 127], in1=gcs[64:128, :], op=ALU.subtract, _partial=True)
        P.I("act", "activation", out=G, in_=G, func=AF.Exp)
        P.I("act", "activation", out=GL, in_=GL, func=AF.Exp, scale=-1.0)
        P.I("act", "activation", out=dk, in_=dk, func=AF.Exp, _partial=True)
        P.I("act", "activation", out=egc, in_=gcs, func=AF.Exp, _partial=True)
        P.I("dve", "tensor_tensor", out=sc1, in0=egc, in1=beta, op=ALU.mult, _partial=True)
        P.I("dve", "tensor_tensor", out=G, in0=G, in1=Mui[:, None, :].bc([128, 4, 128]), op=ALU.mult)
        P.I("dve", "tensor_tensor", out=GL, in0=GL, in1=Mls[:, None, :].bc([128, 4, 128]), op=ALU.mult)
        if dstop <= 2:
            continue
        pkk, pqk = nextps(C), nextps(C)
        for h in range(4):
            P.mm(out=pkk[:, h * 128:(h + 1) * 128], lhsT=kT[:, h, :], rhs=kT[:, h, :])
        for h in range(4):
            P.mm(out=pqk[:, h * 128:(h + 1) * 128], lhsT=kT[:, h, :], rhs=qT[:, h, :])
        for h in range(4):
            P.I("dve", "scalar_tensor_tensor", out=A32[:, h, :], in0=pkk[:, h * 128:(h + 1) * 128], scalar=beta[:, h:h + 1],
                in1=GL[:, h, :], op0=ALU.mult, op1=ALU.mult, _partial=True)
        P.I("dve", "tensor_tensor", out=qkTm, in0=pqk.r("p (h i) -> p h i", h=4), in1=G, op=ALU.mult)
        P.I("dve", "tensor_tensor", out=qgT, in0=qT, in1=Eg, op=ALU.mult)
        pN = nextps(C)
        for h in range(4):
            P.tr(out=pN[:, h * 128:(h + 1) * 128], in_=A32[:, h, :], ident=C.identf)
        pNv = pN.r("p (h i) -> p h i", h=4)
        P.I("act", "activation", out=Nb, in_=pNv, func=AF.Copy)
        P.I("dve", "tensor_tensor", out=R, in0=C.identf[:, None, :].bc([128, 4, 128]), in1=pNv, op=ALU.subtract)
        P.I("act", "activation", out=Ab, in_=A32, func=AF.Copy)
        P.I("act", "activation", out=Rb, in_=R, func=AF.Copy)
        if dstop <= 3:
            continue
        Xc, XTc = Nb, Ab
        for lev in range(5):
            pX, pXT = nextps(C), nextps(C)
            last_lev = (lev == 4)
            for h in range(4):
                if not last_lev:
                    P.mm(out=pX[:, h * 128:(h + 1) * 128], lhsT=XTc[:, h, :], rhs=Xc[:, h, :])
                P.mm(out=pXT[:, h * 128:(h + 1) * 128], lhsT=Xc[:, h, :], rhs=XTc[:, h, :])
            Xn, XTn = Xb[lev % 2], XTb[lev % 2]
            if not last_lev:
                P.I("act", "activation", out=Xn, in_=pX.r("p (h i) -> p h i", h=4), func=AF.Copy)
            P.I("dve", "tensor_copy", out=XTn, in_=pXT.r("p (h i) -> p h i", h=4))
            pR = nextps(C)
            for h in range(4):
                P.mm(out=pR[:, h * 128:(h + 1) * 128], lhsT=XTn[:, h, :], rhs=Rb[:, h, :])
            P.I("dve", "tensor_tensor", out=R, in0=R, in1=pR.r("p (h i) -> p h i", h=4), op=ALU.add)
            if not last_lev:
                P.I("act", "activation", out=Rb, in_=R, func=AF.Copy)
            Xc, XTc = Xn, XTn
        P.I("act", "activation", out=TTh, in_=R, func=AF.Copy)
        P.I("dve", "tensor_tensor", out=N32, in0=R, in1=TTh, op=ALU.subtract)
        P.I("dve", "tensor_copy", out=TTl, in_=N32)
        if dstop <= 4:
            continue
        pkt = nextps(C).bitcast(BF16)
        for h in range(4):
            P.tr(out=pkt[:, h * 128:(h + 1) * 128], in_=kT[:, h, :], ident=C.identb)
        for h in range(4):
            P.I("dve", "tensor_scalar", out=kbeg[:, h, :], in0=pkt[:, h * 128:(h + 1) * 128], scalar1=sc1[:, h:h + 1], scalar2=0.0,
                op0=ALU.mult, op1=ALU.add, _partial=True)
            P.I("dve", "tensor_scalar", out=kdec[:, h, :], in0=pkt[:, h * 128:(h + 1) * 128], scalar1=dk[:, h:h + 1], scalar2=0.0,
                op0=ALU.mult, op1=ALU.add, _partial=True)
        pvt = nextps(C).bitcast(BF16)
        for h in range(4):
            P.tr(out=pvt[:, h * 128:(h + 1) * 128], in_=vTb[:, h, :], ident=C.identb)
        for h in range(4):
            P.I("dve", "tensor_scalar", out=vb[:, h, :], in0=pvt[:, h * 128:(h + 1) * 128], scalar1=beta[:, h:h + 1], scalar2=0.0,
                op0=ALU.mult, op1=ALU.add, _partial=True)
        pw, pu = nextps(C), nextps(C)
        for h in range(4):
            o = pw[:, h * 128:(h + 1) * 128]
            P.mm(out=o, lhsT=kbeg[:, h, :], rhs=TTh[:, h, :], start=True, stop=False)
            P.mm(out=o, lhsT=kbeg[:, h, :], rhs=TTl[:, h, :], start=False, stop=True)
        for h in range(4):
            o = pu[:, h * 128:(h + 1) * 128]
            P.mm(out=o, lhsT=TTh[:, h, :], rhs=vb[:, h, :], start=True, stop=False)
            P.mm(out=o, lhsT=TTl[:, h, :], rhs=vb[:, h, :], start=False, stop=True)
        P.I("act", "activation", out=wTb, in_=pw.r("p (h i) -> p h i", h=4), func=AF.Copy)
        P.I("act", "activation", out=u, in_=pu.r("p (h i) -> p h i", h=4), func=AF.Copy)
        if dstop <= 5:
            continue
        for X in range(2):
            r = slice(X * 64, (X + 1) * 64)
            lc = X * 64 + 63
            pvn = nextps(C)
            for h in range(4):
                P.mm(out=pvn[r, h * 128:(h + 1) * 128], lhsT=wTb[:, h, r], rhs=Sb[:, h, :])
            P.I("dve", "tensor_tensor", out=vnb[r, :, :], in0=u[r, :, :], in1=pvn[r, :].r("p (h e) -> p h e", h=4),
                op=ALU.subtract, _partial=True)
            po = nextps(C)
            for h in range(4):
                o = po[:, h * 64:(h + 1) * 64]
                P.mm(out=o, lhsT=Sb[:, h, :], rhs=qgT[:, h, r], start=True, stop=False)
                P.mm(out=o, lhsT=vnb[r, h, :], rhs=qkTm[r, h, r], start=False, stop=True)
            P.I("act", "activation", out=oT[:, :, r], in_=po[:, 0:256].r("p (h i) -> p h i", h=4), func=AF.Copy, _partial=True)
            pS = nextps(C)
            for h in range(4):
                P.mm(out=pS[:, h * 128:(h + 1) * 128], lhsT=kdec[r, h, :], rhs=vnb[r, h, :])
            for h in range(4):
                P.I("dve", "scalar_tensor_tensor", out=S[:, h, :], in0=S[:, h, :], scalar=Eg[:, h, lc:lc + 1],
                    in1=pS[:, h * 128:(h + 1) * 128], op0=ALU.mult, op1=ALU.add, _partial=True)
            P.I("act", "activation", out=Sb, in_=S, func=AF.Copy)
        if dstop <= 6:
            continue
        P.I("act", "activation", out=sq[:, 0:4, :], in_=oT, func=AF.Square)
        for h in range(4):
            p1 = nextps(C)
            P.mm(out=p1[:, 0:NT], lhsT=C.onesb, rhs=sq[:, h, :])
            r_ = rn[h % 2]
            P.I("act", "activation", out=r_, in_=p1[:, 0:NT], func=AF.Ln, bias=1e-6, scale=1.0 / 128)
            P.I("act", "activation", out=r_, in_=r_, func=AF.Exp, scale=-0.5)
            P.I("dve", "tensor_tensor", out=r_, in0=r_, in1=oT[:, h, :], op=ALU.mult)
            P.I("dve", "tensor_tensor", out=r_, in0=r_, in1=gs[:, h, :], op=ALU.mult)
            P.I("act", "activation", out=Y[:, h, sl], in_=r_, func=AF.Identity, scale=nw[:, 0:1], _partial=True)
    P.release(m)


def bcast_rows(C, name, src1d, n):
    P = C.P
    t = P.sb(name, [128, n], F32)
    P.dma(t, V(src1d.buf, src1d.ap.partition_broadcast(128)), q="sp")
    return t


def mixer_sg(C, l, hT, Y):
    P, I = C.P, C.I
    m = P.mark()
    wu = P.sb("sg_wu", [128, 8, 512], BF16)
    wv = P.sb("sg_wv", [128, 8, 512], BF16)
    for k in range(8):
        load_w(C, wu[:, k, :], I["w_in"][l, k * 128:(k + 1) * 128, OFF["su"]:OFF["su"] + 512])
        load_w(C, wv[:, k, :], I["w_in"][l, k * 128:(k + 1) * 128, OFF["sv"]:OFF["sv"] + 512])
    gbc = bcast_rows(C, "sg_g", I["sg_ln_g"][l], 512)
    bbc = bcast_rows(C, "sg_b", I["sg_ln_b"][l], 512)
    sbc = bcast_rows(C, "sg_sb", I["sg_b"][l].r("g t -> (g t)"), 512)
    wraw = P.sb("sg_wraw", [128, 4, 128], F32)
    wtf = P.sb("sg_wtf", [128, 4, 128], F32)
    wtb = P.sb("sg_wtb", [128, 4, 128], BF16)
    P.dma(wraw, I["sg_w"][l].r("g t s -> t g s"))
    pk = nextps(C)
    for g in range(4):
        P.tr(out=pk[:, g * 128:(g + 1) * 128], in_=wraw[:, g, :], ident=C.identf)
    P.I("act", "activation", out=wtf, in_=pk.r("p (g t) -> p g t", g=4), func=AF.Copy)
    P.I("pool", "affine_select", out=wtf, in_=wtf, pattern=[[0, 4], [1, 128]], compare_op=ALU.is_ge, fill=0.0,
        base=0, channel_multiplier=-1)
    P.I("dve", "tensor_copy", out=wtb, in_=wtf)
    N = 512
    uT = [P.sb(f"sg_uT{i}", [128, 4, N], F32) for i in range(2)]
    vg = [P.sb(f"sg_vg{i}", [128, 512], F32) for i in range(2)]
    vtm = [P.sb(f"sg_vtm{i}", [128, 512], BF16) for i in range(2)]
    stt = [P.sb(f"sg_st{i}", [128, 16], F32) for i in range(2)]
    tmp = [P.sb(f"sg_tmp{i}", [128, 4, 128], F32) for i in range(2)]
    n = 0
    for t in range(T // N):
        u = uT[t % 2]
        sl = slice(t * N, (t + 1) * N)
        for j in range(4):
            p1 = nextps(C)
            for k in range(8):
                P.mm(out=p1, lhsT=wu[:, k, j * 128:(j + 1) * 128], rhs=hT[:, k, sl], start=(k == 0), stop=(k == 7))
            P.I("act", "activation", out=u[:, j, :], in_=p1, func=AF.Gelu, _partial=True)
        for c in range(4):
            c0 = t * N + c * 128
            v, vb, st, tm = vg[n % 2], vtm[n % 2], stt[n % 2], tmp[n % 2]
            n += 1
            p1 = nextps(C)
            for k in range(8):
                P.mm(out=p1, lhsT=hT[:, k, c0:c0 + 128], rhs=wv[:, k, :], start=(k == 0), stop=(k == 7))
            P.I("act", "activation", out=v, in_=p1, func=AF.Gelu)
            P.I("dve", "bn_stats", out=st[:, 0:6], in_=v, _partial=True)
            P.I("dve", "bn_aggr", out=st[:, 8:10], in_=st[:, 0:6], _partial=True)
            P.I("act", "activation", out=st[:, 10:11], in_=st[:, 9:10], func=AF.Ln, bias=1e-5, scale=1.0, _partial=True)
            P.I("act", "activation", out=st[:, 10:11], in_=st[:, 10:11], func=AF.Exp, scale=-0.5, _partial=True)
            P.I("dve", "tensor_scalar", out=v, in0=v, scalar1=st[:, 8:9], scalar2=st[:, 10:11], op0=ALU.subtract, op1=ALU.mult)
            P.I("dve", "tensor_tensor", out=v, in0=v, in1=gbc, op=ALU.mult)
            P.I("dve", "tensor_tensor", out=vb, in0=v, in1=bbc, op=ALU.add)
            p2 = nextps(C)
            for g in range(4):
                P.mm(out=p2[:, g * 128:(g + 1) * 128], lhsT=vb[:, g * 128:(g + 1) * 128], rhs=wtb[:, g, :])
            P.I("dve", "tensor_tensor", out=tm, in0=p2.r("p (g t) -> p g t", g=4), in1=sbc.r("p (g t) -> p g t", g=4), op=ALU.add)
            P.I("dve", "tensor_tensor", out=Y[:, :, c0:c0 + 128], in0=tm, in1=u[:, :, c * 128:(c + 1) * 128], op=ALU.mult,
                _partial=True)
    P.release(m)


def mixer_fox(C, l, hT, Y):
    P, I = C.P, C.I
    m = P.mark()
    wq = P.sb("fx_wq", [128, 8, 512], BF16)
    wk = P.sb("fx_wk", [128, 8, 512], BF16)
    wv = P.sb("fx_wv", [128, 8, 512], BF16)
    wf = P.sb("fx_wf", [128, 8, 8], BF16)
    for k in range(8):
        rows = slice(k * 128, (k + 1) * 128)
        load_w(C, wq[:, k, :], I["w_in"][l, rows, OFF["fq"]:OFF["fq"] + 512])
        load_w(C, wk[:, k, :], I["w_in"][l, rows, OFF["fk"]:OFF["fk"] + 512])
        load_w(C, wv[:, k, :], I["w_in"][l, rows, OFF["fv"]:OFF["fv"] + 512])
        load_w(C, wf[:, k, :], I["w_in"][l, rows, OFF["ff"]:OFF["ff"] + 8])
    fb = P.sb("fx_fb", [8, 2], F32)
    P.dma(fb[:, 0:1], I["fox_f_bias"][l].r("(h o) -> h o", o=1))
    P.I("dve", "tensor_scalar", out=fb[:, 1:2], in0=fb[:, 0:1], scalar1=-1.0, scalar2=0.0, op0=ALU.mult, op1=ALU.add, _partial=True)
    chm = P.sb("fx_chm", [8, 2, T], BF16)
    negc = P.sb("fx_negc", [128, 32, 8], F32)
    m2 = P.mark()
    sp = P.sb("fx_sp", [8, T], F32)
    cc = P.sb("fx_c", [8, T], F32)
    r1 = P.sb("fx_r1", [8, T], F32)
    for t in range(8):
        sl = slice(t * 512, (t + 1) * 512)
        p1 = nextps(C)
        for k in range(8):
            P.mm(out=p1[0:8, :], lhsT=wf[:, k, :], rhs=hT[:, k, sl], start=(k == 0), stop=(k == 7))
        P.I("act", "activation", out=sp[:, sl], in_=p1[0:8, :], func=AF.Exp, scale=-1.0, bias=fb[:, 1:2], _partial=True)
    P.I("act", "activation", out=sp, in_=sp, func=AF.Ln, bias=1.0, scale=1.0)
    P.I("dve", "tensor_tensor_scan", out=cc, data0=sp, data1=sp, initial=0.0, op0=ALU.min, op1=ALU.subtract)
    P.I("dve", "tensor_copy", out=chm[:, 0, :], in_=cc, _partial=True)
    P.I("dve", "tensor_tensor", out=r1, in0=cc, in1=chm[:, 0, :], op=ALU.subtract)
    P.I("dve", "tensor_copy", out=chm[:, 1, :], in_=r1, _partial=True)
    pk = nextps(C)
    for blk in range(32):
        P.tr(out=pk[:, blk * 8:(blk + 1) * 8], in_=cc[:, blk * 128:(blk + 1) * 128], ident=C.identf[0:8, 0:8])
    P.I("dve", "tensor_scalar", out=negc, in0=pk[:, 0:256].r("p (b h) -> p b h", h=8), scalar1=-1.0, scalar2=0.0,
        op0=ALU.mult, op1=ALU.add)
    dbg(C, "fx_c", cc)
    dbg(C, "fx_negc", negc)
    P.release(m2)
    maskneg = P.sb("fx_mask", [128, 4, 512], BF16)
    P.I("pool", "memset", ap=maskneg, constant=0.0, _extra_w=[maskneg])
    for b in range(4):
        P.I("pool", "affine_select", out=maskneg[:, b, :], in_=maskneg[:, b, :], pattern=[[1, 512]], compare_op=ALU.is_ge,
            fill=-30000.0, base=-128 * b, channel_multiplier=-1)
    qa = P.sb("fx_qa", [128, T], BF16)
    ka = P.sb("fx_ka", [128, T], BF16)
    vaug = P.sb("fx_vaug", [128, 32, 128], BF16)
    pTs = [P.sb(f"fx_pT{i}", [128, 512], BF16) for i in range(4)]
    den = [P.sb(f"fx_den{i}", [64, 512], F32) for i in range(2)]
    P.I("pool", "memset", ap=vaug[:, :, 64:128], constant=1.0, _extra_w=[vaug])
    P.I("pool", "memset", ap=ka[64:66, :], constant=1.0, _extra_w=[ka])
    rot = 0
    npT = 0
    nq = 0
    for h in range(8):
        cs = slice(h * 64, (h + 1) * 64)
        for t in range(8):
            sl = slice(t * 512, (t + 1) * 512)
            p1, p2 = nextps(C), nextps(C)
            for k in range(8):
                P.mm(out=p1[0:64, :], lhsT=wq[:, k, cs], rhs=hT[:, k, sl], start=(k == 0), stop=(k == 7))
            P.I("act", "activation", out=qa[0:64, sl], in_=p1[0:64, :], func=AF.Identity, scale=0.125, _partial=True)
            for k in range(8):
                P.mm(out=p2[0:64, :], lhsT=wk[:, k, cs], rhs=hT[:, k, sl], start=(k == 0), stop=(k == 7))
            P.I("dve", "tensor_copy", out=ka[0:64, sl], in_=p2[0:64, :], _partial=True)
        P.dma(qa[64:65, :], chm[h:h + 1, 0, :], q="sp")
        P.dma(qa[65:66, :], chm[h:h + 1, 1, :], q="sp")
        for b4 in range(8):
            p1 = nextps(C)
            for bb in range(4):
                blk = b4 * 4 + bb
                for k in range(8):
                    P.mm(out=p1[:, bb * 64:(bb + 1) * 64], lhsT=hT[:, k, blk * 128:(blk + 1) * 128], rhs=wv[:, k, cs],
                         start=(k == 0), stop=(k == 7))
            P.I("act", "activation", out=vaug[:, b4 * 4:(b4 + 1) * 4, 0:64], in_=p1[:, 0:256].r("p (b d) -> p b d", d=64),
                func=AF.Copy, _partial=True)
        blocks = [(qt, kb) for qt in range(8) for kb in range(4 * qt + 4)]
        LA = 3
        sbank = {}
        accs = {}
        for i in range(len(blocks) + LA):
            if i < len(blocks):
                qt, kb = blocks[i]
                qs = slice(qt * 512, (qt + 1) * 512)
                sps = C.ps[2 + rot % 6]
                rot += 1
                sbank[i] = sps
                diag = kb >= 4 * qt
                P.mm(out=sps, lhsT=ka[0:66, kb * 128:(kb + 1) * 128], rhs=qa[0:66, qs], start=True, stop=(not diag))
                if diag:
                    P.mm(out=sps, lhsT=C.identb, rhs=maskneg[:, kb - 4 * qt, :], start=False, stop=True)
            j = i - LA
            if j < 0:
                continue
            qt, kb = blocks[j]
            qs = slice(qt * 512, (qt + 1) * 512)
            nkb = 4 * qt + 4
            if kb == 0:
                accs[qt] = (C.ps[nq % 2], den[nq % 2])
                nq += 1
            acc, dn = accs[qt]
            pT = pTs[npT % 4]
            npT += 1
            P.I("act", "activation", out=pT, in_=sbank.pop(j), func=AF.Exp, bias=negc[:, kb, h:h + 1], scale=1.0)
            P.mm(out=acc, lhsT=vaug[:, kb, :], rhs=pT, start=(kb == 0), stop=(kb == nkb - 1))
            if kb == nkb - 1:
                P.I("act", "activation", out=dn, in_=acc[64:128, :], func=AF.Copy)
                P.I("dve", "reciprocal", out=dn, in_=dn)
                po = (h % 2) * 64
                P.I("dve", "tensor_tensor", out=Y[po:po + 64, h // 2, qs], in0=acc[0:64, :], in1=dn, op=ALU.mult, _partial=True)
    C.psi = 0
    P.release(m)

from concourse.bass_utils import run_bass_kernel_spmd

_CACHE = {}


def kernel(**inputs):
    inputs = {k: np.ascontiguousarray(np.asarray(v, dtype=np.float32)) for k, v in inputs.items()}
    if "nc" not in _CACHE:
        _CACHE["nc"] = build(dict(nlayers=2, mixers="abcd"))[0]
    nc = _CACHE["nc"]
    x = inputs["x"]
    n = x.shape[0]
    in_maps = []
    for b in range(n):
        m = {k: v for k, v in inputs.items() if k != "x"}
        m["x"] = x[b]
        in_maps.append(m)
    res = run_bass_kernel_spmd(nc, in_maps, core_ids=list(range(n)))
    return np.stack([r["out"] for r in res.results], axis=0).astype(np.float32)
```

```python
import numpy as np
import concourse.bass as bass
import concourse.mybir as mybir
from contextlib import ExitStack

F32 = mybir.dt.float32
BF16 = mybir.dt.bfloat16
AF = mybir.ActivationFunctionType
ALU = mybir.AluOpType
AX = mybir.AxisListType
_ISZ = {F32: 4, BF16: 2}


def _ap(h):
    return h.ap() if hasattr(h, "ap") else h[:]


class Buf:
    __slots__ = ("name", "writers", "readers", "war_base", "lo", "hi", "psum")

    def __init__(self, name):
        self.name = name
        self.psum = False
        self.writers = []
        self.readers = []
        self.war_base = []


class V:
    __slots__ = ("buf", "ap")

    def __init__(self, buf, ap):
        self.buf = buf
        self.ap = ap

    def __getitem__(self, k):
        return V(self.buf, self.ap[k])

    def r(self, s, **kw):
        return V(self.buf, self.ap.rearrange(s, **kw))

    def bc(self, shape):
        return V(self.buf, self.ap.to_broadcast(list(shape)))

    def bitcast(self, dt):
        return V(self.buf, self.ap.bitcast(dt))

    def alias(self, buf):
        return V(buf, self.ap)


DMA_ENGS = ("sp", "act", "pool")
COMPUTE = ("pe", "act", "dve", "pool")
NSEM_DMA = {"sp": 16, "act": 8, "pool": 8}


class Prog:
    def __init__(self, nc):
        self.nc = nc
        self.ops = []
        self.ndma = {q: 0 for q in DMA_ENGS}
        self.sb_top = 0
        self.sb_regions = []
        self.sb_max = 0
        self.uid = 0
        self.psum_n = 0
        self.arena = None

    def sb(self, name, shape, dtype, nbufs=None):
        if self.arena is None:
            self.arena_bytes = 206 * 1024
            self.arena = _ap(self.nc.alloc_sbuf_tensor("arena", [128, self.arena_bytes // 4], F32))
        per = int(np.prod(shape[1:])) * _ISZ[dtype]
        lo = (self.sb_top + 31) // 32 * 32
        hi = lo + (per + 3) // 4 * 4
        assert hi <= self.arena_bytes, f"SBUF arena overflow: {name} {hi}"
        self.sb_top = hi
        self.sb_max = max(self.sb_max, hi)
        ap = self.arena[0:shape[0], lo // 4:hi // 4]
        if dtype != F32:
            ap = ap.bitcast(dtype)
        ap = ap[:, 0:int(np.prod(shape[1:]))]
        if len(shape) == 3:
            ap = ap.rearrange("p (a b) -> p a b", a=shape[1])
        elif len(shape) == 4:
            ap = ap.rearrange("p (a b c) -> p a b c", a=shape[1], b=shape[2])
        inherit = []
        for (l2, h2, b2) in self.sb_regions:
            if l2 < hi and lo < h2:
                inherit += b2.readers + b2.writers
        if nbufs is None:
            b = Buf(name)
            b.readers = list(set(inherit))
            b.lo, b.hi = lo, hi
            self.sb_regions.append((lo, hi, b))
            return V(b, ap)
        outs = []
        n = shape[1] // nbufs
        step = per // nbufs
        for i in range(nbufs):
            b = Buf(f"{name}{i}")
            b.readers = list(set(inherit))
            b.lo, b.hi = lo + i * step, lo + (i + 1) * step
            self.sb_regions.append((b.lo, b.hi, b))
            outs.append(V(b, ap[:, i * n:(i + 1) * n]))
        return outs

    def mark(self):
        return self.sb_top

    def release(self, m):
        self.sb_top = m
        if len(self.sb_regions) > 400:
            pass

    def psum(self, name, dtype=F32, cols=512):
        h = self.nc.alloc_psum_tensor(f"{name}", [128, cols], dtype)
        b = Buf(name)
        b.psum = True
        return V(b, _ap(h))

    def dram(self, name, shape, dtype, kind="Internal"):
        h = self.nc.dram_tensor(name, list(shape), dtype, kind=kind)
        return V(Buf(name), h.ap())

    def token(self, v, name="tok"):
        return V(Buf(name), v.ap)

    def add(self, eng, fn, reads, writes, partial=False, dma=False):
        idx = len(self.ops)
        deps = set()
        rb = {v.buf for v in reads}
        wb = {v.buf for v in writes}
        for b in rb:
            for w in b.writers:
                deps.add((w, "raw"))
            if b.psum:
                for r in b.readers:
                    deps.add((r, "rar"))
        for b in wb:
            for r in b.readers:
                deps.add((r, "war"))
            for r in b.war_base:
                deps.add((r, "war"))
            if not partial:
                for w in b.writers:
                    deps.add((w, "waw"))
        fdeps = set()
        for (d, kind) in deps:
            o = self.ops[d]
            if d == idx:
                continue
            if (not dma) and (not o[3]) and o[0] == eng and kind != "raw" and (eng == "pe" or kind == "rar"):
                continue
            fdeps.add(d)
        for b in wb:
            if partial:
                if b.readers:
                    b.war_base = list(b.readers)
                    b.writers = [idx]
                    b.readers = []
                else:
                    b.writers.append(idx)
            else:
                b.writers = [idx]
                b.readers = []
                b.war_base = []
        for b in rb:
            b.readers.append(idx)
        semi = None
        if dma:
            k = self.ndma[eng]
            self.ndma[eng] += 1
            S = NSEM_DMA[eng]
            semi = (eng, k % S, 16 * (k // S + 1), k)
        self.ops.append([eng, fn, sorted(fdeps), dma, semi, False, 0, [(b.name, getattr(b, 'lo', None), getattr(b, 'hi', None)) for b in rb], [(b.name, getattr(b, 'lo', None), getattr(b, 'hi', None)) for b in wb]])
        return idx

    def _vs(self, kw):
        reads, writes = [], []
        for k, v in kw.items():
            if isinstance(v, V):
                (writes if k in ("out", "accum_out") else reads).append(v)
        return reads, writes

    def I(self, eng, meth, _partial=False, _extra_r=(), _extra_w=(), **kw):
        reads, writes = self._vs(kw)
        reads += list(_extra_r)
        writes += list(_extra_w)
        args = {k: (v.ap if isinstance(v, V) else v) for k, v in kw.items()}

        def fn(e, meth=meth, args=args):
            return getattr(e, meth)(**args)
        return self.add(eng, fn, reads, writes, partial=_partial)

    def mm(self, out, lhsT, rhs, start=True, stop=True, **kw):
        return self.I("pe", "matmul", out=out, lhsT=lhsT, rhs=rhs, start=start, stop=stop, _partial=True, **kw)

    def tr(self, out, in_, ident):
        return self.I("pe", "transpose", out=out, in_=in_, identity=ident, _partial=True)

    def dma(self, out, in_, q="sp", partial=True, **kw):
        args = dict(out=out.ap, in_=in_.ap, **kw)

        def fn(e, args=args):
            return e.dma_start(**args)
        return self.add(q, fn, [in_], [out], partial=partial, dma=True)

    def emit(self):
        nc = self.nc
        ops = self.ops
        last = {}
        for i, o in enumerate(ops):
            if o[3]:
                last[(o[4][0], o[4][1])] = i
        fin_deps = sorted(last.values())
        ops.append(["sp", None, fin_deps, False, None, False, 0, [], []])
        for o in ops:
            for d in o[2]:
                if not ops[d][3]:
                    ops[d][5] = True
        cnt = {e: 0 for e in COMPUTE + ("sp",)}
        for o in ops:
            if not o[3] and o[5]:
                cnt[o[0]] += 1
                o[6] = cnt[o[0]]
        with ExitStack() as st:
            esem = {e: st.enter_context(nc.semaphore(f"s_{e}")) for e in COMPUTE}
            dsem = {q: [st.enter_context(nc.semaphore(f"d_{q}{i}")) for i in range(NSEM_DMA[q])] for q in DMA_ENGS}
            block = st.enter_context(nc.Block())
            per_eng = {e: [] for e in ("pe", "act", "dve", "pool", "sp")}
            for i, o in enumerate(ops):
                per_eng[o[0]].append(i)

            def run(ename, e):
                waited = {}
                for i in per_eng[ename]:
                    o = ops[i]
                    need = {}
                    for d in o[2]:
                        od = ops[d]
                        if od[3]:
                            key = ("d", od[4][0], od[4][1])
                            val = od[4][2]
                        else:
                            key = ("e", od[0])
                            val = od[6]
                        need[key] = max(need.get(key, 0), val)
                    if o[3]:
                        q, si, val, k = o[4]
                        if k >= NSEM_DMA[q]:
                            key = ("d", q, si)
                            need[key] = max(need.get(key, 0), val - 16)
                    for key, val in need.items():
                        if waited.get(key, 0) >= val:
                            continue
                        waited[key] = val
                        sem = dsem[key[1]][key[2]] if key[0] == "d" else esem[key[1]]
                        e.wait_ge(sem, val)
                    if o[1] is None:
                        continue
                    ins = o[1](e)
                    if o[3]:
                        ins.then_inc(dsem[o[4][0]][o[4][1]], 16)
                    elif o[5]:
                        ins.then_inc(esem[ename], 1)

            @block.tensor
            def _(e):
                run("pe", e)

            @block.scalar
            def _(e):
                run("act", e)

            @block.vector
            def _(e):
                run("dve", e)

            @block.gpsimd
            def _(e):
                run("pool", e)

            @block.sync
            def _(e):
                run("sp", e)
        return cnt

T = 4096
D = 1024
KD = 8
DIN = 10264
ALPHA = 4 ** 0.25
OFF = dict(z=0, xs=512, bm=1024, cm=1280, dt=1536, dq=1544, dk=2056, dv=2568, dbeta=3080, da=3084, dgate=3088,
           su=3600, sv=4112, fq=4624, fk=5136, fv=5648, ff=6160, gates=6168)


class Ctx:
    pass


def build(cfg):
    nc = bass.Bass("TRN2", target_bir_lowering=False)
    P = Prog(nc)
    C = Ctx()
    C.P, C.cfg = P, cfg
    L = cfg.get("nlayers", 2)
    real = cfg.get("mixers", "abcd")
    taps = cfg.get("taps", ())
    I = {}
    shapes = dict(x=[T, D], ln_in_g=[D], ln_in_b=[D], w_in=[2, D, DIN], ssd_conv_w=[2, 4, 1024], ssd_conv_b=[2, 1024],
                  ssd_dt_bias=[2, 8], ssd_a_log=[2, 8], ssd_d=[2, 8], ssd_norm_w=[2, 512], dn_conv_w=[2, 4, 1536],
                  dn_a_log=[2, 4], dn_dt_bias=[2, 4], dn_norm_w=[2, 128], sg_ln_g=[2, 512], sg_ln_b=[2, 512],
                  sg_w=[2, 4, 128, 128], sg_b=[2, 4, 128], fox_f_bias=[2, 8], gate_b=[2, 4, 1024],
                  w_branch=[2, 4, 512, 1024], w_out=[2, D, D], ln1_g=[2, D], ln1_b=[2, D], w_up=[2, D, 4096],
                  w_down=[2, 4096, D], ln2_g=[2, D], ln2_b=[2, D])
    for k, s in shapes.items():
        I[k] = P.dram(k, s, F32, kind="ExternalInput")
    for m in "abcd":
        if m not in real:
            I["yinj_" + m] = P.dram("yinj_" + m, [2, 512, T], F32, kind="ExternalInput")
    C.I = I
    C.out = P.dram("out", [T, D], F32, kind="ExternalOutput")
    C.tap = {}
    for t in taps:
        C.tap[t] = P.dram("tap_" + t, [512 if t.startswith("y") else D, T], F32, kind="ExternalOutput")
    def tiled(v):
        return [V(Buf(v.buf.name + str(i)), v.ap) for i in range(8)]
    C.hres_d = tiled(P.dram("hres_d", [D, T], F32))
    C.hT_d = tiled(P.dram("hT_d", [D, T], BF16))
    C.G_d = [tiled(P.dram(f"G_d{i}", [D, T], BF16)) for i in range(4)]
    for t in list(C.tap):
        C.tap[t] = tiled(C.tap[t])
    C.ps = [P.psum(f"ps{i}") for i in range(8)]
    C.psi = 0
    C.identf = P.sb("identf", [128, 128], F32)
    C.identb = P.sb("identb", [128, 128], BF16)
    C.onesb = P.sb("onesb", [128, 128], BF16)
    P.I("pool", "memset", ap=C.identf, constant=1.0, _extra_w=[C.identf])
    P.I("pool", "affine_select", out=C.identf, in_=C.identf, pattern=[[-1, 128]], compare_op=ALU.is_equal, fill=0.0,
        base=0, channel_multiplier=1)
    P.I("dve", "tensor_copy", out=C.identb, in_=C.identf)
    P.I("pool", "memset", ap=C.onesb, constant=1.0, _extra_w=[C.onesb])

    stage_entry(C)
    for l in range(L):
        phase_a(C, l)
        phase_b(C, l)
        phase_c(C, l, last=(l == L - 1))
    cnt = P.emit()
    return nc, P, cnt


def dbg(C, name, v, cast=False):
    if name not in C.cfg.get("dbg", ()):
        return
    P = C.P
    shp = list(v.ap.shape)
    if cast:
        mk = P.mark()
        tmp = P.sb("dbgtmp", shp, F32)
        P.I("dve", "tensor_copy", out=tmp, in_=v)
        v = tmp
    d = P.dram("dbg_" + name, shp, F32, kind="ExternalOutput")
    P.dma(d, v, q="sp")
    if cast:
        P.release(mk)


def nextps(C):
    rr = getattr(C, "rr", None)
    if rr:
        v = C.ps[rr[C.psi % len(rr)]]
    else:
        v = C.ps[C.psi % 8]
    C.psi += 1
    return v


def colvec(C, name, src, J, q="sp"):
    P = C.P
    t = P.sb(name, [128, J], F32)
    P.dma(t, src.r("(j p) -> p j", p=128), q=q, allow_slow_non_contiguous=True)
    return t


def dsl(v, t0, n):
    if isinstance(v, list):
        v = v[t0 // 512]
    return v.r("(j p) t -> p j t", p=128)[:, :, t0:t0 + n]


def ln_gen(C, xf, g, b, N, out_f=None, out_b=None):
    P = C.P
    m = P.mark()
    xb = P.sb("ln_xb", [128, 8, N], BF16)
    sq = P.sb("ln_sq", [128, 8, N], BF16)
    st = P.sb("ln_st", [128, 3, N], F32)
    P.I("act", "activation", out=xb, in_=xf, func=AF.Copy)
    P.I("act", "activation", out=sq, in_=xf, func=AF.Square)
    yield
    if getattr(C, "ln_banks", None):
        p1, p2 = C.ps[C.ln_banks[0]], C.ps[C.ln_banks[1]]
    else:
        p1, p2 = nextps(C), nextps(C)
    for k in range(8):
        P.mm(out=p1[:, 0:N], lhsT=C.onesb, rhs=xb[:, k, :], start=(k == 0), stop=(k == 7))
    yield
    for k in range(8):
        P.mm(out=p2[:, 0:N], lhsT=C.onesb, rhs=sq[:, k, :], start=(k == 0), stop=(k == 7))
    yield
    mean, msq, rstd = st[:, 0, :], st[:, 1, :], st[:, 2, :]
    P.I("dve", "tensor_scalar", out=mean, in0=p1[:, 0:N], scalar1=1.0 / D, scalar2=0.0, op0=ALU.mult, op1=ALU.add, _partial=True)
    P.I("dve", "tensor_tensor", out=msq, in0=mean, in1=mean, op=ALU.mult, _partial=True)
    P.I("dve", "scalar_tensor_tensor", out=rstd, in0=p2[:, 0:N], scalar=1.0 / D, in1=msq, op0=ALU.mult, op1=ALU.subtract,
        _partial=True)
    P.I("act", "activation", out=rstd, in_=rstd, func=AF.Ln, bias=1e-5, scale=1.0, _partial=True)
    P.I("act", "activation", out=rstd, in_=rstd, func=AF.Exp, scale=-0.5, _partial=True)
    yield
    P.I("dve", "tensor_tensor", out=xf, in0=xf, in1=mean[:, None, :].bc([128, 8, N]), op=ALU.subtract)
    yield
    P.I("dve", "tensor_tensor", out=xf, in0=xf, in1=rstd[:, None, :].bc([128, 8, N]), op=ALU.mult)
    yield
    for k in range(8):
        if k % 2 == 0:
            yield
        if out_b is not None:
            P.I("act", "activation", out=out_b[:, k, :], in_=xf[:, k, :], func=AF.Identity, scale=g[:, k:k + 1],
                bias=b[:, k:k + 1], _partial=True)
        if out_f is not None:
            P.I("act", "activation", out=out_f[:, k, :], in_=xf[:, k, :], func=AF.Identity, scale=g[:, k:k + 1],
                bias=b[:, k:k + 1], _partial=True)
    P.release(m)


def drain(gen):
    for _ in gen:
        pass


def interleave(ga, gb, ra=1, rb=1):
    da = db = False
    while not (da and db):
        for _ in range(ra):
            if not da:
                try:
                    next(ga)
                except StopIteration:
                    da = True
        for _ in range(rb):
            if not db:
                try:
                    next(gb)
                except StopIteration:
                    db = True


def ln_fm(C, xf, g, b, N, out_f=None, out_b=None):
    drain(ln_gen(C, xf, g, b, N, out_f=out_f, out_b=out_b))


def stage_entry(C):
    P, I = C.P, C.I
    m = P.mark()
    g = colvec(C, "ln0g", I["ln_in_g"], 8)
    b = colvec(C, "ln0b", I["ln_in_b"], 8)
    N = 512
    xin = [P.sb(f"xin{i}", [128, 4, D], F32) for i in range(2)]
    xfs = [P.sb(f"xf{i}", [128, 8, N], F32) for i in range(2)]
    hfs = [P.sb(f"hf{i}", [128, 8, N], F32) for i in range(2)]
    hbs = [P.sb(f"hb{i}", [128, 8, N], BF16) for i in range(2)]
    for t in range(T // N):
        xi, xf, hf, hb = xin[t % 2], xfs[t % 2], hfs[t % 2], hbs[t % 2]
        P.dma(xi, C.I["x"][t * N:(t + 1) * N, :].r("(s p) d -> p s d", p=128))
        for k in range(8):
            pk = nextps(C)
            for s in range(4):
                P.tr(out=pk[:, s * 128:(s + 1) * 128], in_=xi[:, s, k * 128:(k + 1) * 128], ident=C.identf)
            P.I("act", "activation", out=xf[:, k, :], in_=pk, func=AF.Copy, _partial=True)
        ln_fm(C, xf, g, b, N, out_f=hf, out_b=hb)
        P.dma(dsl(C.hres_d, t * N, N), hf, q="sp")
        P.dma(dsl(C.hT_d, t * N, N), hb, q="sp")
        if "h0" in C.tap:
            P.dma(dsl(C.tap["h0"], t * N, N), hf, q="sp")
    P.release(m)


def load_w(C, dst, src, q="pool"):
    C.P.dma(dst, src, q=q)


def gate_pass(C, l, i, hT, Y):
    P, I = C.P, C.I
    m = P.mark()
    wg = P.sb("wg", [128, 8, 1024], BF16)
    wb = P.sb("wb", [128, 4, 1024], BF16)
    c0 = OFF["gates"] + i * 1024
    for k in range(8):
        load_w(C, wg[:, k, :], I["w_in"][l, k * 128:(k + 1) * 128, c0:c0 + 1024])
    for k in range(4):
        load_w(C, wb[:, k, :], I["w_branch"][l, i, k * 128:(k + 1) * 128, :])
    gb = colvec(C, "gb", I["gate_b"][l, i], 8)
    N = 512
    gts = [P.sb(f"gt{i}", [128, N], F32) for i in range(3)]
    gbuf = [P.sb(f"gbuf{i}", [128, 8, N], BF16) for i in range(4)]
    n = 0
    for t in range(T // N):
        gbf = gbuf[t % 4]
        sl = slice(t * N, (t + 1) * N)
        for j in range(8):
            p1, p2 = nextps(C), nextps(C)
            for k in range(8):
                P.mm(out=p1, lhsT=wg[:, k, j * 128:(j + 1) * 128], rhs=hT[:, k, sl], start=(k == 0), stop=(k == 7))
            gt = gts[n % 3]
            n += 1
            P.I("act", "activation", out=gt, in_=p1, func=AF.Sigmoid, bias=gb[:, j:j + 1])
            for k in range(4):
                P.mm(out=p2, lhsT=wb[:, k, j * 128:(j + 1) * 128], rhs=Y[:, k, sl], start=(k == 0), stop=(k == 3))
            P.I("dve", "tensor_tensor", out=gbf[:, j, :], in0=p2, in1=gt, op=ALU.mult, _partial=True)
        P.dma(dsl(C.G_d[i], t * N, N), gbf, q="sp")
    P.release(m)


def phase_a(C, l):
    P, I = C.P, C.I
    m = P.mark()
    hT = P.sb("hT", [128, 8, T], BF16)
    for t in range(8):
        P.dma(hT[:, :, t * 512:(t + 1) * 512], dsl(C.hT_d, t * 512, 512))
    real = C.cfg.get("mixers", "abcd")
    fns = dict(a=mixer_ssd, b=mixer_dn, c=mixer_sg, d=mixer_fox)
    for i, mx in enumerate("abcd"):
        m2 = P.mark()
        Y = P.sb("Y", [128, 4, T], BF16)
        if mx in real:
            fns[mx](C, l, hT, Y)
        else:
            for k in range(4):
                load_w(C, Y[:, k, :], I["yinj_" + mx][l, k * 128:(k + 1) * 128, :])
        tn = f"y_{mx}_{l}"
        if tn in C.tap:
            yf = P.sb("ytap", [128, 4, 1024], F32)
            for t4 in range(4):
                P.I("act", "activation", out=yf, in_=Y[:, :, t4 * 1024:(t4 + 1) * 1024], func=AF.Copy)
                for t2 in range(2):
                    P.dma(dsl(C.tap[tn], t4 * 1024 + t2 * 512, 512), yf[:, :, t2 * 512:(t2 + 1) * 512], q="sp")
        gate_pass(C, l, i, hT, Y)
        P.release(m2)
    P.release(m)


def phase_b(C, l):
    P, I = C.P, C.I
    m = P.mark()
    wo = P.sb("wo", [128, 8, 1024], BF16)
    for k in range(8):
        load_w(C, wo[:, k, :], I["w_out"][l, k * 128:(k + 1) * 128, :])
    g = colvec(C, "ln1g", I["ln1_g"][l], 8)
    b = colvec(C, "ln1b", I["ln1_b"][l], 8)
    N = 512
    gl = [P.sb(f"gl{i}", [128, 8, N], BF16) for i in range(8)]
    macc = P.sb("macc", [128, 8, N], F32)
    mb = P.sb("mb", [128, 8, N], BF16)
    hr = P.sb("hr", [128, 8, N], F32)
    x1s = [P.sb(f"x1_{i}", [128, 8, N], F32) for i in range(2)]
    hbs = [P.sb(f"hb_{i}", [128, 8, N], BF16) for i in range(2)]
    lnbuf = P.mark()
    def front(t):
        x1 = x1s[t % 2]
        P.dma(hr, dsl(C.hres_d, t * N, N), q="sp")
        gs4 = []
        for i in range(4):
            gg = gl[(4 * t + i) % 8]
            P.dma(gg, dsl(C.G_d[i], t * N, N), q="sp")
            gs4.append(gg)
        yield
        P.I("dve", "tensor_tensor", out=macc, in0=gs4[0], in1=gs4[1], op=ALU.add)
        yield
        P.I("dve", "tensor_tensor", out=macc, in0=macc, in1=gs4[2], op=ALU.add)
        yield
        P.I("dve", "tensor_tensor", out=macc, in0=macc, in1=gs4[3], op=ALU.add)
        tn = f"merged_{l}"
        if tn in C.tap:
            P.dma(dsl(C.tap[tn], t * N, N), macc, q="sp")
        yield
        P.I("act", "activation", out=mb, in_=macc, func=AF.Copy)
        yield
        for j in range(8):
            p1 = nextps(C)
            for k in range(8):
                P.mm(out=p1, lhsT=wo[:, k, j * 128:(j + 1) * 128], rhs=mb[:, k, :], start=(k == 0), stop=(k == 7))
            P.I("dve", "scalar_tensor_tensor", out=x1[:, j, :], in0=hr[:, j, :], scalar=ALPHA, in1=p1, op0=ALU.mult,
                op1=ALU.add, _partial=True)
            yield

    def tail(t):
        x1 = x1s[t % 2]
        hb = hbs[t % 2]
        yield from ln_gen(C, x1, g, b, N, out_f=x1, out_b=hb)
        P.dma(dsl(C.hres_d, t * N, N), x1, q="sp")
        P.dma(dsl(C.hT_d, t * N, N), hb, q="sp")
        tn = f"h1_{l}"
        if tn in C.tap:
            P.dma(dsl(C.tap[tn], t * N, N), x1, q="sp")

    C.rr, C.ln_banks = [0, 1, 2, 3, 4, 5], (6, 7)
    prev = None
    for t in range(T // N):
        f = front(t)
        if prev is None:
            drain(f)
        else:
            interleave(f, prev)
        prev = tail(t)
    drain(prev)
    C.rr, C.ln_banks = None, None
    P.release(m)


def phase_c(C, l, last):
    P, I = C.P, C.I
    m = P.mark()
    wu = P.sb("wu", [128, 8, 4096], BF16)
    wd = P.sb("wd", [128, 32, 1024], BF16)
    for k in range(8):
        for hh in range(2):
            load_w(C, wu[:, k, hh * 2048:(hh + 1) * 2048], I["w_up"][l, k * 128:(k + 1) * 128, hh * 2048:(hh + 1) * 2048])
    for f in range(32):
        load_w(C, wd[:, f, :], I["w_down"][l, f * 128:(f + 1) * 128, :])
    g = colvec(C, "ln2g", I["ln2_g"][l], 8)
    b = colvec(C, "ln2b", I["ln2_b"][l], 8)
    N = 256
    hbt = [P.sb(f"c_hb{i}", [128, 8, N], BF16) for i in range(2)]
    hrt = [P.sb(f"c_hr{i}", [128, 8, N], F32) for i in range(1)]
    act = P.sb("c_act", [128, 32, N], BF16)
    rl = [P.sb(f"c_rl{i}", [128, N], F32) for i in range(3)]
    x2s = [P.sb(f"c_x2_{i}", [128, 8, N], F32) for i in range(2)]
    hb = None if last else P.sb("c_hbo", [128, 8, N], BF16)
    ot = [P.sb(f"c_ot{i}", [128, D], F32) for i in range(2)] if last else None
    cnt = dict(n=0, no=0)

    def front(t):
        h1b, h1 = hbt[t % 2], hrt[0]
        x2 = x2s[t % 2]
        P.dma(h1b, dsl(C.hT_d, t * N, N), q="sp")
        P.dma(h1, dsl(C.hres_d, t * N, N), q="sp")
        yield
        for f in range(32):
            p1 = nextps(C)
            for k in range(8):
                P.mm(out=p1[:, 0:N], lhsT=wu[:, k, f * 128:(f + 1) * 128], rhs=h1b[:, k, :], start=(k == 0), stop=(k == 7))
            r = rl[cnt["n"] % 3]
            cnt["n"] += 1
            P.I("act", "activation", out=r, in_=p1[:, 0:N], func=AF.Relu)
            P.I("dve", "tensor_tensor", out=act[:, f, :], in0=r, in1=r, op=ALU.mult, _partial=True)
            if f % 2 == 1:
                yield
        for j in range(8):
            p1 = nextps(C)
            for f in range(32):
                P.mm(out=p1[:, 0:N], lhsT=wd[:, f, j * 128:(j + 1) * 128], rhs=act[:, f, :], start=(f == 0), stop=(f == 31))
            P.I("dve", "scalar_tensor_tensor", out=x2[:, j, :], in0=h1[:, j, :], scalar=ALPHA, in1=p1[:, 0:N], op0=ALU.mult,
                op1=ALU.add, _partial=True)
            yield

    def tail(t):
        x2 = x2s[t % 2]
        hf = x2
        yield from ln_gen(C, x2, g, b, N, out_f=hf, out_b=(None if last else hb))
        tn = f"h2_{l}"
        if tn in C.tap:
            P.dma(dsl(C.tap[tn], t * N, N), hf, q="sp")
        if not last:
            P.dma(dsl(C.hres_d, t * N, N), hf, q="sp")
            P.dma(dsl(C.hT_d, t * N, N), hb, q="sp")
        else:
            for s_ in range(N // 128):
                o = ot[cnt["no"] % 2]
                cnt["no"] += 1
                for half in range(2):
                    pk = nextps(C)
                    for kk in range(4):
                        k = half * 4 + kk
                        P.tr(out=pk[:, kk * 128:(kk + 1) * 128], in_=hf[:, k, s_ * 128:(s_ + 1) * 128], ident=C.identf)
                    P.I("act", "activation", out=o[:, half * 512:(half + 1) * 512], in_=pk, func=AF.Copy, _partial=True)
                    yield
                r0 = t * N + s_ * 128
                P.dma(C.out[r0:r0 + 128, :], o, q="sp")

    C.rr, C.ln_banks = [0, 1, 2, 3, 4, 5], (6, 7)
    prev = None
    for t in range(T // N):
        f_ = front(t)
        if prev is None:
            drain(f_)
        else:
            interleave(f_, prev, ra=2, rb=1)
        prev = tail(t)
    drain(prev)
    C.rr, C.ln_banks = None, None
    P.release(m)


def make_tri(C, name):
    P = C.P
    t = P.sb(name, [128, 128], F32)
    P.I("pool", "memset", ap=t, constant=1.0, _extra_w=[t])
    P.I("pool", "affine_select", out=t, in_=t, pattern=[[1, 128]], compare_op=ALU.is_ge, fill=0.0, base=0,
        channel_multiplier=-1)
    return t


def conv4(C, acc, raw, cw, nj, NT):
    P = C.P
    for j in range(nj):
        P.I("dve", "tensor_scalar", out=acc[:, j, :], in0=raw[:, j, 3:3 + NT], scalar1=cw[:, 3, j:j + 1], scalar2=0.0,
            op0=ALU.mult, op1=ALU.add, _partial=True)
        for k in range(3):
            P.I("dve", "scalar_tensor_tensor", out=acc[:, j, :], in0=raw[:, j, k:k + NT], scalar=cw[:, k, j:j + 1],
                in1=acc[:, j, :], op0=ALU.mult, op1=ALU.add, _partial=True)


def mixer_ssd(C, l, hT, Y):
    P, I = C.P, C.I
    m = P.mark()
    NT = 256
    NC = NT // 128
    wz = P.sb("sd_wz", [128, 8, 512], BF16)
    wx = P.sb("sd_wx", [128, 8, 1024], BF16)
    wdt = P.sb("sd_wdt", [128, 8, 8], BF16)
    for k in range(8):
        rows = slice(k * 128, (k + 1) * 128)
        load_w(C, wz[:, k, :], I["w_in"][l, rows, OFF["z"]:OFF["z"] + 512])
        load_w(C, wx[:, k, :], I["w_in"][l, rows, OFF["xs"]:OFF["xs"] + 1024])
        load_w(C, wdt[:, k, :], I["w_in"][l, rows, OFF["dt"]:OFF["dt"] + 8])
    cw = P.sb("sd_cw", [128, 4, 8], F32)
    for k in range(4):
        P.dma(cw[:, k, :], I["ssd_conv_w"][l, k].r("(j p) -> p j", p=128), allow_slow_non_contiguous=True)
    cb = colvec(C, "sd_cb", I["ssd_conv_b"][l], 8)
    dtb = bcast_rows(C, "sd_dtb", I["ssd_dt_bias"][l], 8)
    abc = bcast_rows(C, "sd_abc", I["ssd_a_log"][l], 8)
    P.I("act", "activation", out=abc, in_=abc, func=AF.Exp)
    P.I("dve", "tensor_scalar", out=abc, in0=abc, scalar1=-1.0, scalar2=0.0, op0=ALU.mult, op1=ALU.add)
    drep = P.sb("sd_drep", [128, 4], F32)
    for j in range(4):
        for hf in range(2):
            src = I["ssd_d"][l, 2 * j + hf:2 * j + hf + 1]
            P.dma(drep[hf * 64:(hf + 1) * 64, j:j + 1], V(src.buf, src.ap.partition_broadcast(64)))
    nws = colvec(C, "sd_nw", I["ssd_norm_w"][l], 4)
    P.I("dve", "tensor_scalar", out=nws, in0=nws, scalar1=float(512 ** 0.5), scalar2=0.0, op0=ALU.mult, op1=ALU.add)
    tri = make_tri(C, "sd_tri")
    S = P.sb("sd_S", [128, 8, 64], F32)
    Sb = P.sb("sd_Sb", [128, 512], BF16)
    P.I("pool", "memset", ap=S, constant=0.0, _extra_w=[S])
    P.I("pool", "memset", ap=Sb, constant=0.0, _extra_w=[Sb])
    raw = P.sb("sd_raw", [128, 8, 3 + NT], F32)
    P.I("pool", "memset", ap=raw[:, :, 0:3], constant=0.0, _extra_w=[raw])
    acc = P.sb("sd_acc", [128, 8, NT], F32)
    xs = P.sb("sd_xs", [128, 4, NT], F32)
    xsb = P.sb("sd_xsb", [128, 4, NT], BF16)
    bmT = P.sb("sd_bmT", [128, 2, NT], BF16)
    cmT = P.sb("sd_cmT", [128, 2, NT], BF16)
    zs = P.sb("sd_zs", [128, 4, NT], F32)
    y1 = P.sb("sd_y1", [128, 4, NT], F32)
    sq = P.sb("sd_sq", [128, 4, NT], BF16)
    rs = P.sb("sd_rs", [128, NT], F32)
    sm = P.sb("sd_sm", [128, 5, 8], F32)
    dAs = P.sb("sd_dAs", [128, 2, 8], BF16)
    dAsf = P.sb("sd_dAsf", [128, 2, 8], F32)
    dAb = P.sb("sd_dAb", [128, 2, 8, 128], BF16)
    trib = P.sb("sd_trib", [128, 128], BF16)
    P.I("dve", "tensor_copy", out=trib, in_=tri)
    E = P.sb("sd_E", [128, 8, 128], F32)
    Dt = P.sb("sd_Dt", [128, 8, 128], F32)
    MT = P.sb("sd_MT", [128, 8, 128], BF16)
    cms = P.sb("sd_cms", [128, 8, 128], BF16)
    xdt = P.sb("sd_xdt", [128, 8, 64], BF16)
    xdd = P.sb("sd_xdd", [128, 8, 64], BF16)
    bmtm = P.sb("sd_bmtm", [128, 2, 128], BF16)
    dt_, dA, acs, dte, dt2 = sm[:, 0, :], sm[:, 1, :], sm[:, 2, :], sm[:, 3, :], sm[:, 4, :]
    for t in range(T // NT):
        sl = slice(t * NT, (t + 1) * NT)
        if t > 0:
            P.I("dve", "tensor_copy", out=raw[:, :, 0:3], in_=raw[:, :, NT:NT + 3])
        for j in range(4):
            p1 = nextps(C)
            for k in range(8):
                P.mm(out=p1[:, 0:NT], lhsT=wz[:, k, j * 128:(j + 1) * 128], rhs=hT[:, k, sl], start=(k == 0), stop=(k == 7))
            P.I("act", "activation", out=zs[:, j, :], in_=p1[:, 0:NT], func=AF.Silu, _partial=True)
        for j in range(8):
            p1 = nextps(C)
            for k in range(8):
                P.mm(out=p1[:, 0:NT], lhsT=wx[:, k, j * 128:(j + 1) * 128], rhs=hT[:, k, sl], start=(k == 0), stop=(k == 7))
            P.I("act", "activation", out=raw[:, j, 3:3 + NT], in_=p1[:, 0:NT], func=AF.Copy, _partial=True)
        conv4(C, acc, raw, cw, 8, NT)
        for j in range(8):
            if j < 4:
                P.I("act", "activation", out=xs[:, j, :], in_=acc[:, j, :], func=AF.Silu, bias=cb[:, j:j + 1], _partial=True)
            else:
                dst = bmT[:, j - 4, :] if j < 6 else cmT[:, j - 6, :]
                P.I("act", "activation", out=dst, in_=acc[:, j, :], func=AF.Silu, bias=cb[:, j:j + 1], _partial=True)
        P.I("dve", "tensor_copy", out=xsb, in_=xs)
        stop = C.cfg.get("sd_stop", 99)
        for c in range(NC if stop > 1 else 0):
            c0 = t * NT + c * 128
            cl = slice(c * 128, (c + 1) * 128)
            p1 = nextps(C)
            for k in range(8):
                P.mm(out=p1[:, 0:8], lhsT=hT[:, k, c0:c0 + 128], rhs=wdt[:, k, :], start=(k == 0), stop=(k == 7))
            P.I("dve", "tensor_tensor", out=dt_, in0=p1[:, 0:8], in1=dtb, op=ALU.add, _partial=True)
            P.I("act", "activation", out=dt_, in_=dt_, func=AF.Exp, _partial=True)
            P.I("act", "activation", out=dt_, in_=dt_, func=AF.Ln, bias=1.0, scale=1.0, _partial=True)
            P.I("dve", "tensor_tensor", out=dA, in0=dt_, in1=abc, op=ALU.mult, _partial=True)
            if stop <= 1.1:
                continue
            P.I("dve", "tensor_copy", out=dAs[:, 0, :], in_=dA, _partial=True)
            P.I("dve", "tensor_tensor", out=dt2, in0=dA, in1=dAs[:, 0, :], op=ALU.subtract, _partial=True)
            P.I("dve", "tensor_copy", out=dAs[:, 1, :], in_=dt2, _partial=True)
            P.I("dve", "tensor_copy", out=dAsf, in_=dAs)
            for i2 in range(2):
                for h in range(8):
                    P.I("dve", "tensor_scalar", out=dAb[:, i2, h, :], in0=C.onesb, scalar1=dAsf[:, i2, h:h + 1], scalar2=0.0,
                        op0=ALU.mult, op1=ALU.add, _partial=True)
            if stop <= 1.2:
                continue
            p2 = nextps(C)
            P.mm(out=p2[:, 0:8], lhsT=trib, rhs=dAs[:, 0, :], start=True, stop=False)
            P.mm(out=p2[:, 0:8], lhsT=trib, rhs=dAs[:, 1, :], start=False, stop=True)
            P.I("dve", "tensor_copy", out=acs, in_=p2[:, 0:8], _partial=True)
            if stop <= 1.3:
                continue
            pB = [nextps(C), nextps(C)]
            for h in range(8):
                o = pB[h // 4][:, (h % 4) * 128:(h % 4 + 1) * 128]
                P.mm(out=o, lhsT=dAb[:, 0, h, :], rhs=trib, start=True, stop=False)
                P.mm(out=o, lhsT=dAb[:, 1, h, :], rhs=trib, start=False, stop=True)
            if stop <= 1.4:
                continue
            for hh in range(2):
                hs = slice(hh * 4, (hh + 1) * 4)
                pv = pB[hh].r("p (h l) -> p h l", h=4)
                P.I("act", "activation", out=E[:, hs, :], in_=pv, func=AF.Exp, _partial=True)
                if stop <= 1.5:
                    continue
                for h4 in range(4):
                    h = hh * 4 + h4
                    P.I("dve", "tensor_scalar", out=Dt[:, h, :], in0=pv[:, h4, :], scalar1=acs[:, h:h + 1], scalar2=0.0,
                        op0=ALU.subtract, op1=ALU.min, _partial=True)
                if stop <= 1.6:
                    continue
                P.I("dve", "tensor_tensor", out=dte[:, hs], in0=pv[:, :, 127], in1=acs[:, hs], op=ALU.subtract, _partial=True)
            if stop <= 1.7:
                continue
            P.I("act", "activation", out=Dt, in_=Dt, func=AF.Exp)
            P.I("act", "activation", out=dte, in_=dte, func=AF.Exp, _partial=True)
            if stop <= 1.8:
                continue
            P.I("dve", "tensor_tensor", out=Dt, in0=Dt, in1=tri[:, None, :].bc([128, 8, 128]), op=ALU.mult)
            if stop <= 2:
                continue
            pcb = nextps(C)
            for g in range(2):
                P.mm(out=pcb[:, g * 128:(g + 1) * 128], lhsT=bmT[:, g, cl], rhs=cmT[:, g, cl])
            for g in range(2):
                hs = slice(g * 4, (g + 1) * 4)
                P.I("dve", "tensor_tensor", out=MT[:, hs, :], in0=Dt[:, hs, :],
                    in1=pcb[:, g * 128:(g + 1) * 128][:, None, :].bc([128, 4, 128]), op=ALU.mult, _partial=True)
                P.I("dve", "tensor_tensor", out=cms[:, hs, :], in0=E[:, hs, :],
                    in1=cmT[:, g, cl][:, None, :].bc([128, 4, 128]), op=ALU.mult, _partial=True)
            if stop <= 3:
                continue
            pt = nextps(C).bitcast(BF16)
            for j in range(4):
                P.tr(out=pt[:, j * 128:(j + 1) * 128], in_=xsb[:, j, cl], ident=C.identb)
            for h in range(8):
                P.I("dve", "tensor_scalar", out=xdt[:, h, :], in0=pt[:, h * 64:(h + 1) * 64], scalar1=dt_[:, h:h + 1],
                    scalar2=0.0, op0=ALU.mult, op1=ALU.add, _partial=True)
                P.I("dve", "tensor_scalar", out=xdd[:, h, :], in0=pt[:, h * 64:(h + 1) * 64], scalar1=dt_[:, h:h + 1],
                    scalar2=dte[:, h:h + 1], op0=ALU.mult, op1=ALU.mult, _partial=True)
            pb2 = nextps(C).bitcast(BF16)
            for g in range(2):
                P.tr(out=pb2[:, g * 128:(g + 1) * 128], in_=bmT[:, g, cl], ident=C.identb)
            P.I("act", "activation", out=bmtm, in_=pb2[:, 0:256].r("p (g n) -> p g n", g=2), func=AF.Copy)
            if stop <= 4:
                continue
            py = nextps(C)
            for h in range(8):
                po = (h % 2) * 64
                o = py[po:po + 64, (h // 2) * 128:(h // 2 + 1) * 128]
                P.mm(out=o, lhsT=xdt[:, h, :], rhs=MT[:, h, :], start=True, stop=False)
                P.mm(out=o, lhsT=Sb[:, h * 64:(h + 1) * 64], rhs=cms[:, h, :], start=False, stop=True)
            pS = nextps(C)
            for g in range(2):
                P.mm(out=pS[:, g * 256:(g + 1) * 256], lhsT=bmtm[:, g, :], rhs=xdd[:, g * 4:(g + 1) * 4, :])
            for h in range(8):
                P.I("dve", "scalar_tensor_tensor", out=S[:, h, :], in0=S[:, h, :], scalar=E[:, h, 127:128],
                    in1=pS[:, h * 64:(h + 1) * 64], op0=ALU.mult, op1=ALU.add, _partial=True)
            P.I("act", "activation", out=Sb, in_=S.r("p h d -> p (h d)"), func=AF.Copy)
            for j in range(4):
                P.I("dve", "scalar_tensor_tensor", out=y1[:, j, cl], in0=xs[:, j, cl], scalar=drep[:, j:j + 1],
                    in1=py[:, j * 128:(j + 1) * 128], op0=ALU.mult, op1=ALU.add, _partial=True)
        P.I("dve", "tensor_tensor", out=y1, in0=y1, in1=zs, op=ALU.mult)
        P.I("act", "activation", out=sq, in_=y1, func=AF.Square)
        pr = nextps(C)
        for j in range(4):
            P.mm(out=pr[:, 0:NT], lhsT=C.onesb, rhs=sq[:, j, :], start=(j == 0), stop=(j == 3))
        P.I("act", "activation", out=rs, in_=pr[:, 0:NT], func=AF.Ln, bias=float(512 * 1e-6), scale=1.0)
        P.I("act", "activation", out=rs, in_=rs, func=AF.Exp, scale=-0.5)
        P.I("dve", "tensor_tensor", out=y1, in0=y1, in1=rs[:, None, :].bc([128, 4, NT]), op=ALU.mult)
        for j in range(4):
            P.I("act", "activation", out=Y[:, j, sl], in_=y1[:, j, :], func=AF.Identity, scale=nws[:, j:j + 1], _partial=True)
    P.release(m)


def mixer_dn(C, l, hT, Y):
    P, I = C.P, C.I
    m = P.mark()
    NT = 128
    wq = P.sb("dn_wq", [128, 8, 512], BF16)
    wk = P.sb("dn_wk", [128, 8, 512], BF16)
    wv = P.sb("dn_wv", [128, 8, 512], BF16)
    wg = P.sb("dn_wg", [128, 8, 512], BF16)
    wba = P.sb("dn_wba", [128, 8, 8], BF16)
    for k in range(8):
        rows = slice(k * 128, (k + 1) * 128)
        load_w(C, wq[:, k, :], I["w_in"][l, rows, OFF["dq"]:OFF["dq"] + 512])
        load_w(C, wk[:, k, :], I["w_in"][l, rows, OFF["dk"]:OFF["dk"] + 512])
        load_w(C, wv[:, k, :], I["w_in"][l, rows, OFF["dv"]:OFF["dv"] + 512])
        load_w(C, wg[:, k, :], I["w_in"][l, rows, OFF["dgate"]:OFF["dgate"] + 512])
        load_w(C, wba[:, k, :], I["w_in"][l, rows, OFF["dbeta"]:OFF["dbeta"] + 8])
    cw = P.sb("dn_cw", [128, 4, 12], F32)
    for k in range(4):
        P.dma(cw[:, k, :], I["dn_conv_w"][l, k].r("(j p) -> p j", p=128), allow_slow_non_contiguous=True)
    dtb = bcast_rows(C, "dn_dtb", I["dn_dt_bias"][l], 4)
    nal = bcast_rows(C, "dn_nal", I["dn_a_log"][l], 4)
    P.I("act", "activation", out=nal, in_=nal, func=AF.Exp)
    P.I("dve", "tensor_scalar", out=nal, in0=nal, scalar1=-1.0, scalar2=0.0, op0=ALU.mult, op1=ALU.add)
    nw = colvec(C, "dn_nw", I["dn_norm_w"][l], 1)
    def mk_mask(name, strict, lower):
        t = P.sb(name, [128, 128], F32)
        P.I("pool", "memset", ap=t, constant=1.0, _extra_w=[t])
        sgn = -1 if not lower else 1
        P.I("pool", "affine_select", out=t, in_=t, pattern=[[-sgn, 128]], compare_op=(ALU.is_gt if strict else ALU.is_ge),
            fill=0.0, base=0, channel_multiplier=sgn)
        P.I("pool", "memset", ap=t[0:64, 64:128], constant=0.0, _extra_w=[t])
        P.I("pool", "memset", ap=t[64:128, 0:64], constant=0.0, _extra_w=[t])
        return t
    Mui = mk_mask("dn_Mui", False, False)
    Mls = mk_mask("dn_Mls", True, True)
    tri2b = P.sb("dn_tri2b", [128, 128], BF16)
    P.I("dve", "tensor_copy", out=tri2b, in_=Mui)
    S = P.sb("dn_S", [128, 4, 128], F32)
    Sb = P.sb("dn_Sb", [128, 4, 128], BF16)
    P.I("pool", "memset", ap=S, constant=0.0, _extra_w=[S])
    P.I("pool", "memset", ap=Sb, constant=0.0, _extra_w=[Sb])
    raw = P.sb("dn_raw", [128, 12, 3 + NT], F32)
    P.I("pool", "memset", ap=raw[:, :, 0:3], constant=0.0, _extra_w=[raw])
    acc = P.sb("dn_acc", [128, 12, NT], F32)
    qk = P.sb("dn_qk", [128, 8, NT], F32)
    vTb = P.sb("dn_vTb", [128, 4, NT], BF16)
    sq = P.sb("dn_sq", [128, 8, NT], BF16)
    rn = [P.sb(f"dn_rn{i}", [128, NT], F32) for i in range(2)]
    qT = P.sb("dn_qT", [128, 4, NT], BF16)
    kT = P.sb("dn_kT", [128, 4, NT], BF16)
    gs = P.sb("dn_gs", [128, 4, NT], F32)
    oT = P.sb("dn_oT", [128, 4, NT], F32)
    sm = P.sb("dn_sm", [128, 8, 4], F32)
    beta, g_, gcs, egc, sc1, dk, tmp4 = (sm[:, i, :] for i in range(7))
    gsp = P.sb("dn_gsp", [128, 2, 4], BF16)
    gspf = P.sb("dn_gspf", [128, 2, 4], F32)
    gb = P.sb("dn_gb", [128, 2, 4, 128], BF16)
    Eg = P.sb("dn_Eg", [128, 4, 128], F32)
    GL = P.sb("dn_GL", [128, 4, 128], F32)
    G = P.sb("dn_G", [128, 4, 128], F32)
    A32 = P.sb("dn_A32", [128, 4, 128], F32)
    N32 = P.sb("dn_N32", [128, 4, 128], F32)
    Ab = P.sb("dn_Ab", [128, 4, 128], BF16)
    Nb = P.sb("dn_Nb", [128, 4, 128], BF16)
    R = P.sb("dn_R", [128, 4, 128], F32)
    Rb = P.sb("dn_Rb", [128, 4, 128], BF16)
    Xb = [P.sb(f"dn_Xb{i}", [128, 4, 128], BF16) for i in range(2)]
    XTb = [P.sb(f"dn_XTb{i}", [128, 4, 128], BF16) for i in range(2)]
    TTh = P.sb("dn_TTh", [128, 4, 128], BF16)
    TTl = P.sb("dn_TTl", [128, 4, 128], BF16)
    kbeg = P.sb("dn_kbeg", [128, 4, 128], BF16)
    kdec = P.sb("dn_kdec", [128, 4, 128], BF16)
    vb = P.sb("dn_vb", [128, 4, 128], BF16)
    wTb = P.sb("dn_wTb", [128, 4, 128], BF16)
    u = P.sb("dn_u", [128, 4, 128], F32)
    qgT = P.sb("dn_qgT", [128, 4, 128], BF16)
    qkTm = P.sb("dn_qkTm", [128, 4, 128], BF16)
    vnb = P.sb("dn_vnb", [128, 4, 128], BF16)
    QS = float(128 ** -0.5)
    dstop = C.cfg.get("dn_stop", 99)
    for t in range(T // NT):
        sl = slice(t * NT, (t + 1) * NT)
        if t > 0:
            P.I("dve", "tensor_copy", out=raw[:, :, 0:3], in_=raw[:, :, NT:NT + 3])
        for j in range(12):
            w_ = (wq, wk, wv)[j // 4]
            jj = j % 4
            p1 = nextps(C)
            for k in range(8):
                P.mm(out=p1[:, 0:NT], lhsT=w_[:, k, jj * 128:(jj + 1) * 128], rhs=hT[:, k, sl], start=(k == 0), stop=(k == 7))
            P.I("act", "activation", out=raw[:, j, 3:3 + NT], in_=p1[:, 0:NT], func=AF.Copy, _partial=True)
        for j in range(4):
            p1 = nextps(C)
            for k in range(8):
                P.mm(out=p1[:, 0:NT], lhsT=wg[:, k, j * 128:(j + 1) * 128], rhs=hT[:, k, sl], start=(k == 0), stop=(k == 7))
            P.I("act", "activation", out=gs[:, j, :], in_=p1[:, 0:NT], func=AF.Silu, _partial=True)
        conv4(C, acc, raw, cw, 12, NT)
        P.I("act", "activation", out=qk, in_=acc[:, 0:8, :], func=AF.Silu)
        P.I("act", "activation", out=vTb, in_=acc[:, 8:12, :], func=AF.Silu)
        P.I("act", "activation", out=sq, in_=qk, func=AF.Square)
        for i in range(8):
            p1 = nextps(C)
            P.mm(out=p1[:, 0:NT], lhsT=C.onesb, rhs=sq[:, i, :])
            r_ = rn[i % 2]
            P.I("act", "activation", out=r_, in_=p1[:, 0:NT], func=AF.Ln, bias=1e-6, scale=1.0)
            P.I("act", "activation", out=r_, in_=r_, func=AF.Exp, scale=-0.5)
            dst = qT[:, i, :] if i < 4 else kT[:, i - 4, :]
            P.I("dve", "scalar_tensor_tensor", out=dst, in0=qk[:, i, :], scalar=(QS if i < 4 else 1.0), in1=r_,
                op0=ALU.mult, op1=ALU.mult, _partial=True)
        if dstop <= 1:
            continue
        p1 = nextps(C)
        for k in range(8):
            P.mm(out=p1[:, 0:8], lhsT=hT[:, k, sl], rhs=wba[:, k, :], start=(k == 0), stop=(k == 7))
        P.I("act", "activation", out=beta, in_=p1[:, 0:4], func=AF.Sigmoid, _partial=True)
        P.I("dve", "tensor_tensor", out=g_, in0=p1[:, 4:8], in1=dtb, op=ALU.add, _partial=True)
        P.I("act", "activation", out=g_, in_=g_, func=AF.Exp, _partial=True)
        P.I("act", "activation", out=g_, in_=g_, func=AF.Ln, bias=1.0, scale=1.0, _partial=True)
        P.I("dve", "tensor_tensor", out=g_, in0=g_, in1=nal, op=ALU.mult, _partial=True)
        P.I("dve", "tensor_copy", out=gsp[:, 0, :], in_=g_, _partial=True)
        P.I("dve", "tensor_tensor", out=tmp4, in0=g_, in1=gsp[:, 0, :], op=ALU.subtract, _partial=True)
        P.I("dve", "tensor_copy", out=gsp[:, 1, :], in_=tmp4, _partial=True)
        P.I("dve", "tensor_copy", out=gspf, in_=gsp)
        for i2 in range(2):
            for h in range(4):
                P.I("dve", "tensor_scalar", out=gb[:, i2, h, :], in0=C.onesb, scalar1=gspf[:, i2, h:h + 1], scalar2=0.0,
                    op0=ALU.mult, op1=ALU.add, _partial=True)
        p2 = nextps(C)
        P.mm(out=p2[:, 0:4], lhsT=tri2b, rhs=gsp[:, 0, :], start=True, stop=False)
        P.mm(out=p2[:, 0:4], lhsT=tri2b, rhs=gsp[:, 1, :], start=False, stop=True)
        P.I("dve", "tensor_copy", out=gcs, in_=p2[:, 0:4], _partial=True)
        pBg = nextps(C)
        for h in range(4):
            o = pBg[:, h * 128:(h + 1) * 128]
            P.mm(out=o, lhsT=gb[:, 0, h, :], rhs=tri2b, start=True, stop=False)
            P.mm(out=o, lhsT=gb[:, 1, h, :], rhs=tri2b, start=False, stop=True)
        pBv = pBg.r("p (h i) -> p h i", h=4)
        P.I("act", "activation", out=Eg, in_=pBv, func=AF.Exp)
        for h in range(4):
            P.I("dve", "tensor_scalar", out=G[:, h, :], in0=pBv[:, h, :], scalar1=gcs[:, h:h + 1], scalar2=0.0,
                op0=ALU.subtract, op1=ALU.min, _partial=True)
            P.I("dve", "tensor_scalar", out=GL[:, h, :], in0=pBv[:, h, :], scalar1=gcs[:, h:h + 1], scalar2=0.0,
                op0=ALU.subtract, op1=ALU.max, _partial=True)
        P.I("dve", "tensor_tensor", out=dk[0:64, :], in0=pBv[0:64, :, 63], in1=gcs[0:64, :], op=ALU.subtract, _partial=True)
        P.I("dve", "tensor_tensor", out=dk[64:128, :], in0=pBv[64:128, :, 127], in1=gcs[64:128, :], op=ALU.subtract, _partial=True)
        P.I("act", "activation", out=G, in_=G, func=AF.Exp)
        P.I("act", "activation", out=GL, in_=GL, func=AF.Exp, scale=-1.0)
        P.I("act", "activation", out=dk, in_=dk, func=AF.Exp, _partial=True)
        P.I("act", "activation", out=egc, in_=gcs, func=AF.Exp, _partial=True)
        P.I("dve", "tensor_tensor", out=sc1, in0=egc, in1=beta, op=ALU.mult, _partial=True)
        P.I("dve", "tensor_tensor", out=G, in0=G, in1=Mui[:, None, :].bc([128, 4, 128]), op=ALU.mult)
        P.I("dve", "tensor_tensor", out=GL, in0=GL, in1=Mls[:, None, :].bc([128, 4, 128]), op=ALU.mult)
        if dstop <= 2:
            continue
        pkk, pqk = nextps(C), nextps(C)
        for h in range(4):
            P.mm(out=pkk[:, h * 128:(h + 1) * 128], lhsT=kT[:, h, :], rhs=kT[:, h, :])
        for h in range(4):
            P.mm(out=pqk[:, h * 128:(h + 1) * 128], lhsT=kT[:, h, :], rhs=qT[:, h, :])
        for h in range(4):
            P.I("dve", "scalar_tensor_tensor", out=A32[:, h, :], in0=pkk[:, h * 128:(h + 1) * 128], scalar=beta[:, h:h + 1],
                in1=GL[:, h, :], op0=ALU.mult, op1=ALU.mult, _partial=True)
        P.I("dve", "tensor_tensor", out=qkTm, in0=pqk.r("p (h i) -> p h i", h=4), in1=G, op=ALU.mult)
        P.I("dve", "tensor_tensor", out=qgT, in0=qT, in1=Eg, op=ALU.mult)
        pN = nextps(C)
        for h in range(4):
            P.tr(out=pN[:, h * 128:(h + 1) * 128], in_=A32[:, h, :], ident=C.identf)
        pNv = pN.r("p (h i) -> p h i", h=4)
        P.I("act", "activation", out=Nb, in_=pNv, func=AF.Copy)
        P.I("dve", "tensor_tensor", out=R, in0=C.identf[:, None, :].bc([128, 4, 128]), in1=pNv, op=ALU.subtract)
        P.I("act", "activation", out=Ab, in_=A32, func=AF.Copy)
        P.I("act", "activation", out=Rb, in_=R, func=AF.Copy)
        if dstop <= 3:
            continue
        Xc, XTc = Nb, Ab
        for lev in range(5):
            pX, pXT = nextps(C), nextps(C)
            last_lev = (lev == 4)
            for h in range(4):
                if not last_lev:
                    P.mm(out=pX[:, h * 128:(h + 1) * 128], lhsT=XTc[:, h, :], rhs=Xc[:, h, :])
                P.mm(out=pXT[:, h * 128:(h + 1) * 128], lhsT=Xc[:, h, :], rhs=XTc[:, h, :])
            Xn, XTn = Xb[lev % 2], XTb[lev % 2]
            if not last_lev:
                P.I("act", "activation", out=Xn, in_=pX.r("p (h i) -> p h i", h=4), func=AF.Copy)
            P.I("dve", "tensor_copy", out=XTn, in_=pXT.r("p (h i) -> p h i", h=4))
            pR = nextps(C)
            for h in range(4):
                P.mm(out=pR[:, h * 128:(h + 1) * 128], lhsT=XTn[:, h, :], rhs=Rb[:, h, :])
            P.I("dve", "tensor_tensor", out=R, in0=R, in1=pR.r("p (h i) -> p h i", h=4), op=ALU.add)
            if not last_lev:
                P.I("act", "activation", out=Rb, in_=R, func=AF.Copy)
            Xc, XTc = Xn, XTn
        P.I("act", "activation", out=TTh, in_=R, func=AF.Copy)
        P.I("dve", "tensor_tensor", out=N32, in0=R, in1=TTh, op=ALU.subtract)
        P.I("dve", "tensor_copy", out=TTl, in_=N32)
        if dstop <= 4:
            continue
        pkt = nextps(C).bitcast(BF16)
        for h in range(4):
            P.tr(out=pkt[:, h * 128:(h + 1) * 128], in_=kT[:, h, :], ident=C.identb)
        for h in range(4):
            P.I("dve", "tensor_scalar", out=kbeg[:, h, :], in0=pkt[:, h * 128:(h + 1) * 128], scalar1=sc1[:, h:h + 1], scalar2=0.0,
                op0=ALU.mult, op1=ALU.add, _partial=True)
            P.I("dve", "tensor_scalar", out=kdec[:, h, :], in0=pkt[:, h * 128:(h + 1) * 128], scalar1=dk[:, h:h + 1], scalar2=0.0,
                op0=ALU.mult, op1=ALU.add, _partial=True)
        pvt = nextps(C).bitcast(BF16)
        for h in range(4):
            P.tr(out=pvt[:, h * 128:(h + 1) * 128], in_=vTb[:, h, :], ident=C.identb)
        for h in range(4):
            P.I("dve", "tensor_scalar", out=vb[:, h, :], in0=pvt[:, h * 128:(h + 1) * 128], scalar1=beta[:, h:h + 1], scalar2=0.0,
                op0=ALU.mult, op1=ALU.add, _partial=True)
        pw, pu = nextps(C), nextps(C)
        for h in range(4):
            o = pw[:, h * 128:(h + 1) * 128]
            P.mm(out=o, lhsT=kbeg[:, h, :], rhs=TTh[:, h, :], start=True, stop=False)
            P.mm(out=o, lhsT=kbeg[:, h, :], rhs=TTl[:, h, :], start=False, stop=True)
        for h in range(4):
            o = pu[:, h * 128:(h + 1) * 128]
            P.mm(out=o, lhsT=TTh[:, h, :], rhs=vb[:, h, :], start=True, stop=False)
            P.mm(out=o, lhsT=TTl[:, h, :], rhs=vb[:, h, :], start=False, stop=True)
        P.I("act", "activation", out=wTb, in_=pw.r("p (h i) -> p h i", h=4), func=AF.Copy)
        P.I("act", "activation", out=u, in_=pu.r("p (h i) -> p h i", h=4), func=AF.Copy)
        if dstop <= 5:
            continue
        for X in range(2):
            r = slice(X * 64, (X + 1) * 64)
            lc = X * 64 + 63
            pvn = nextps(C)
            for h in range(4):
                P.mm(out=pvn[r, h * 128:(h + 1) * 128], lhsT=wTb[:, h, r], rhs=Sb[:, h, :])
            P.I("dve", "tensor_tensor", out=vnb[r, :, :], in0=u[r, :, :], in1=pvn[r, :].r("p (h e) -> p h e", h=4),
                op=ALU.subtract, _partial=True)
            po = nextps(C)
            for h in range(4):
                o = po[:, h * 64:(h + 1) * 64]
                P.mm(out=o, lhsT=Sb[:, h, :], rhs=qgT[:, h, r], start=True, stop=False)
                P.mm(out=o, lhsT=vnb[r, h, :], rhs=qkTm[r, h, r], start=False, stop=True)
            P.I("act", "activation", out=oT[:, :, r], in_=po[:, 0:256].r("p (h i) -> p h i", h=4), func=AF.Copy, _partial=True)
            pS = nextps(C)
            for h in range(4):
                P.mm(out=pS[:, h * 128:(h + 1) * 128], lhsT=kdec[r, h, :], rhs=vnb[r, h, :])
            for h in range(4):
                P.I("dve", "scalar_tensor_tensor", out=S[:, h, :], in0=S[:, h, :], scalar=Eg[:, h, lc:lc + 1],
                    in1=pS[:, h * 128:(h + 1) * 128], op0=ALU.mult, op1=ALU.add, _partial=True)
            P.I("act", "activation", out=Sb, in_=S, func=AF.Copy)
        if dstop <= 6:
            continue
        P.I("act", "activation", out=sq[:, 0:4, :], in_=oT, func=AF.Square)
        for h in range(4):
            p1 = nextps(C)
            P.mm(out=p1[:, 0:NT], lhsT=C.onesb, rhs=sq[:, h, :])
            r_ = rn[h % 2]
            P.I("act", "activation", out=r_, in_=p1[:, 0:NT], func=AF.Ln, bias=1e-6, scale=1.0 / 128)
            P.I("act", "activation", out=r_, in_=r_, func=AF.Exp, scale=-0.5)
            P.I("dve", "tensor_tensor", out=r_, in0=r_, in1=oT[:, h, :], op=ALU.mult)
            P.I("dve", "tensor_tensor", out=r_, in0=r_, in1=gs[:, h, :], op=ALU.mult)
            P.I("act", "activation", out=Y[:, h, sl], in_=r_, func=AF.Identity, scale=nw[:, 0:1], _partial=True)
    P.release(m)


def bcast_rows(C, name, src1d, n):
    P = C.P
    t = P.sb(name, [128, n], F32)
    P.dma(t, V(src1d.buf, src1d.ap.partition_broadcast(128)), q="sp")
    return t


def mixer_sg(C, l, hT, Y):
    P, I = C.P, C.I
    m = P.mark()
    wu = P.sb("sg_wu", [128, 8, 512], BF16)
    wv = P.sb("sg_wv", [128, 8, 512], BF16)
    for k in range(8):
        load_w(C, wu[:, k, :], I["w_in"][l, k * 128:(k + 1) * 128, OFF["su"]:OFF["su"] + 512])
        load_w(C, wv[:, k, :], I["w_in"][l, k * 128:(k + 1) * 128, OFF["sv"]:OFF["sv"] + 512])
    gbc = bcast_rows(C, "sg_g", I["sg_ln_g"][l], 512)
    bbc = bcast_rows(C, "sg_b", I["sg_ln_b"][l], 512)
    sbc = bcast_rows(C, "sg_sb", I["sg_b"][l].r("g t -> (g t)"), 512)
    wraw = P.sb("sg_wraw", [128, 4, 128], F32)
    wtf = P.sb("sg_wtf", [128, 4, 128], F32)
    wtb = P.sb("sg_wtb", [128, 4, 128], BF16)
    P.dma(wraw, I["sg_w"][l].r("g t s -> t g s"))
    pk = nextps(C)
    for g in range(4):
        P.tr(out=pk[:, g * 128:(g + 1) * 128], in_=wraw[:, g, :], ident=C.identf)
    P.I("act", "activation", out=wtf, in_=pk.r("p (g t) -> p g t", g=4), func=AF.Copy)
    P.I("pool", "affine_select", out=wtf, in_=wtf, pattern=[[0, 4], [1, 128]], compare_op=ALU.is_ge, fill=0.0,
        base=0, channel_multiplier=-1)
    P.I("dve", "tensor_copy", out=wtb, in_=wtf)
    N = 512
    uT = [P.sb(f"sg_uT{i}", [128, 4, N], F32) for i in range(2)]
    vg = [P.sb(f"sg_vg{i}", [128, 512], F32) for i in range(2)]
    vtm = [P.sb(f"sg_vtm{i}", [128, 512], BF16) for i in range(2)]
    stt = [P.sb(f"sg_st{i}", [128, 16], F32) for i in range(2)]
    tmp = [P.sb(f"sg_tmp{i}", [128, 4, 128], F32) for i in range(2)]
    n = 0
    for t in range(T // N):
        u = uT[t % 2]
        sl = slice(t * N, (t + 1) * N)
        for j in range(4):
            p1 = nextps(C)
            for k in range(8):
                P.mm(out=p1, lhsT=wu[:, k, j * 128:(j + 1) * 128], rhs=hT[:, k, sl], start=(k == 0), stop=(k == 7))
            P.I("act", "activation", out=u[:, j, :], in_=p1, func=AF.Gelu, _partial=True)
        for c in range(4):
            c0 = t * N + c * 128
            v, vb, st, tm = vg[n % 2], vtm[n % 2], stt[n % 2], tmp[n % 2]
            n += 1
            p1 = nextps(C)
            for k in range(8):
                P.mm(out=p1, lhsT=hT[:, k, c0:c0 + 128], rhs=wv[:, k, :], start=(k == 0), stop=(k == 7))
            P.I("act", "activation", out=v, in_=p1, func=AF.Gelu)
            P.I("dve", "bn_stats", out=st[:, 0:6], in_=v, _partial=True)
            P.I("dve", "bn_aggr", out=st[:, 8:10], in_=st[:, 0:6], _partial=True)
            P.I("act", "activation", out=st[:, 10:11], in_=st[:, 9:10], func=AF.Ln, bias=1e-5, scale=1.0, _partial=True)
            P.I("act", "activation", out=st[:, 10:11], in_=st[:, 10:11], func=AF.Exp, scale=-0.5, _partial=True)
            P.I("dve", "tensor_scalar", out=v, in0=v, scalar1=st[:, 8:9], scalar2=st[:, 10:11], op0=ALU.subtract, op1=ALU.mult)
            P.I("dve", "tensor_tensor", out=v, in0=v, in1=gbc, op=ALU.mult)
            P.I("dve", "tensor_tensor", out=vb, in0=v, in1=bbc, op=ALU.add)
            p2 = nextps(C)
            for g in range(4):
                P.mm(out=p2[:, g * 128:(g + 1) * 128], lhsT=vb[:, g * 128:(g + 1) * 128], rhs=wtb[:, g, :])
            P.I("dve", "tensor_tensor", out=tm, in0=p2.r("p (g t) -> p g t", g=4), in1=sbc.r("p (g t) -> p g t", g=4), op=ALU.add)
            P.I("dve", "tensor_tensor", out=Y[:, :, c0:c0 + 128], in0=tm, in1=u[:, :, c * 128:(c + 1) * 128], op=ALU.mult,
                _partial=True)
    P.release(m)


def mixer_fox(C, l, hT, Y):
    P, I = C.P, C.I
    m = P.mark()
    wq = P.sb("fx_wq", [128, 8, 512], BF16)
    wk = P.sb("fx_wk", [128, 8, 512], BF16)
    wv = P.sb("fx_wv", [128, 8, 512], BF16)
    wf = P.sb("fx_wf", [128, 8, 8], BF16)
    for k in range(8):
        rows = slice(k * 128, (k + 1) * 128)
        load_w(C, wq[:, k, :], I["w_in"][l, rows, OFF["fq"]:OFF["fq"] + 512])
        load_w(C, wk[:, k, :], I["w_in"][l, rows, OFF["fk"]:OFF["fk"] + 512])
        load_w(C, wv[:, k, :], I["w_in"][l, rows, OFF["fv"]:OFF["fv"] + 512])
        load_w(C, wf[:, k, :], I["w_in"][l, rows, OFF["ff"]:OFF["ff"] + 8])
    fb = P.sb("fx_fb", [8, 2], F32)
    P.dma(fb[:, 0:1], I["fox_f_bias"][l].r("(h o) -> h o", o=1))
    P.I("dve", "tensor_scalar", out=fb[:, 1:2], in0=fb[:, 0:1], scalar1=-1.0, scalar2=0.0, op0=ALU.mult, op1=ALU.add, _partial=True)
    chm = P.sb("fx_chm", [8, 2, T], BF16)
    negc = P.sb("fx_negc", [128, 32, 8], F32)
    m2 = P.mark()
    sp = P.sb("fx_sp", [8, T], F32)
    cc = P.sb("fx_c", [8, T], F32)
    r1 = P.sb("fx_r1", [8, T], F32)
    for t in range(8):
        sl = slice(t * 512, (t + 1) * 512)
        p1 = nextps(C)
        for k in range(8):
            P.mm(out=p1[0:8, :], lhsT=wf[:, k, :], rhs=hT[:, k, sl], start=(k == 0), stop=(k == 7))
        P.I("act", "activation", out=sp[:, sl], in_=p1[0:8, :], func=AF.Exp, scale=-1.0, bias=fb[:, 1:2], _partial=True)
    P.I("act", "activation", out=sp, in_=sp, func=AF.Ln, bias=1.0, scale=1.0)
    P.I("dve", "tensor_tensor_scan", out=cc, data0=sp, data1=sp, initial=0.0, op0=ALU.min, op1=ALU.subtract)
    P.I("dve", "tensor_copy", out=chm[:, 0, :], in_=cc, _partial=True)
    P.I("dve", "tensor_tensor", out=r1, in0=cc, in1=chm[:, 0, :], op=ALU.subtract)
    P.I("dve", "tensor_copy", out=chm[:, 1, :], in_=r1, _partial=True)
    pk = nextps(C)
    for blk in range(32):
        P.tr(out=pk[:, blk * 8:(blk + 1) * 8], in_=cc[:, blk * 128:(blk + 1) * 128], ident=C.identf[0:8, 0:8])
    P.I("dve", "tensor_scalar", out=negc, in0=pk[:, 0:256].r("p (b h) -> p b h", h=8), scalar1=-1.0, scalar2=0.0,
        op0=ALU.mult, op1=ALU.add)
    dbg(C, "fx_c", cc)
    dbg(C, "fx_negc", negc)
    P.release(m2)
    maskneg = P.sb("fx_mask", [128, 4, 512], BF16)
    P.I("pool", "memset", ap=maskneg, constant=0.0, _extra_w=[maskneg])
    for b in range(4):
        P.I("pool", "affine_select", out=maskneg[:, b, :], in_=maskneg[:, b, :], pattern=[[1, 512]], compare_op=ALU.is_ge,
            fill=-30000.0, base=-128 * b, channel_multiplier=-1)
    qa = P.sb("fx_qa", [128, T], BF16)
    ka = P.sb("fx_ka", [128, T], BF16)
    vaug = P.sb("fx_vaug", [128, 32, 128], BF16)
    pTs = [P.sb(f"fx_pT{i}", [128, 512], BF16) for i in range(4)]
    den = [P.sb(f"fx_den{i}", [64, 512], F32) for i in range(2)]
    P.I("pool", "memset", ap=vaug[:, :, 64:128], constant=1.0, _extra_w=[vaug])
    P.I("pool", "memset", ap=ka[64:66, :], constant=1.0, _extra_w=[ka])
    rot = 0
    npT = 0
    nq = 0
    for h in range(8):
        cs = slice(h * 64, (h + 1) * 64)
        for t in range(8):
            sl = slice(t * 512, (t + 1) * 512)
            p1, p2 = nextps(C), nextps(C)
            for k in range(8):
                P.mm(out=p1[0:64, :], lhsT=wq[:, k, cs], rhs=hT[:, k, sl], start=(k == 0), stop=(k == 7))
            P.I("act", "activation", out=qa[0:64, sl], in_=p1[0:64, :], func=AF.Identity, scale=0.125, _partial=True)
            for k in range(8):
                P.mm(out=p2[0:64, :], lhsT=wk[:, k, cs], rhs=hT[:, k, sl], start=(k == 0), stop=(k == 7))
            P.I("dve", "tensor_copy", out=ka[0:64, sl], in_=p2[0:64, :], _partial=True)
        P.dma(qa[64:65, :], chm[h:h + 1, 0, :], q="sp")
        P.dma(qa[65:66, :], chm[h:h + 1, 1, :], q="sp")
        for b4 in range(8):
            p1 = nextps(C)
            for bb in range(4):
                blk = b4 * 4 + bb
                for k in range(8):
                    P.mm(out=p1[:, bb * 64:(bb + 1) * 64], lhsT=hT[:, k, blk * 128:(blk + 1) * 128], rhs=wv[:, k, cs],
                         start=(k == 0), stop=(k == 7))
            P.I("act", "activation", out=vaug[:, b4 * 4:(b4 + 1) * 4, 0:64], in_=p1[:, 0:256].r("p (b d) -> p b d", d=64),
                func=AF.Copy, _partial=True)
        blocks = [(qt, kb) for qt in range(8) for kb in range(4 * qt + 4)]
        LA = 3
        sbank = {}
        accs = {}
        for i in range(len(blocks) + LA):
            if i < len(blocks):
                qt, kb = blocks[i]
                qs = slice(qt * 512, (qt + 1) * 512)
                sps = C.ps[2 + rot % 6]
                rot += 1
                sbank[i] = sps
                diag = kb >= 4 * qt
                P.mm(out=sps, lhsT=ka[0:66, kb * 128:(kb + 1) * 128], rhs=qa[0:66, qs], start=True, stop=(not diag))
                if diag:
                    P.mm(out=sps, lhsT=C.identb, rhs=maskneg[:, kb - 4 * qt, :], start=False, stop=True)
            j = i - LA
            if j < 0:
                continue
            qt, kb = blocks[j]
            qs = slice(qt * 512, (qt + 1) * 512)
            nkb = 4 * qt + 4
            if kb == 0:
                accs[qt] = (C.ps[nq % 2], den[nq % 2])
                nq += 1
            acc, dn = accs[qt]
            pT = pTs[npT % 4]
            npT += 1
            P.I("act", "activation", out=pT, in_=sbank.pop(j), func=AF.Exp, bias=negc[:, kb, h:h + 1], scale=1.0)
            P.mm(out=acc, lhsT=vaug[:, kb, :], rhs=pT, start=(kb == 0), stop=(kb == nkb - 1))
            if kb == nkb - 1:
                P.I("act", "activation", out=dn, in_=acc[64:128, :], func=AF.Copy)
                P.I("dve", "reciprocal", out=dn, in_=dn)
                po = (h % 2) * 64
                P.I("dve", "tensor_tensor", out=Y[po:po + 64, h // 2, qs], in0=acc[0:64, :], in1=dn, op=ALU.mult, _partial=True)
    C.psi = 0
    P.release(m)

from concourse.bass_utils import run_bass_kernel_spmd

_CACHE = {}


def kernel(**inputs):
    inputs = {k: np.ascontiguousarray(np.asarray(v, dtype=np.float32)) for k, v in inputs.items()}
    if "nc" not in _CACHE:
        _CACHE["nc"] = build(dict(nlayers=2, mixers="abcd"))[0]
    nc = _CACHE["nc"]
    x = inputs["x"]
    n = x.shape[0]
    in_maps = []
    for b in range(n):
        m = {k: v for k, v in inputs.items() if k != "x"}
        m["x"] = x[b]
        in_maps.append(m)
    res = run_bass_kernel_spmd(nc, in_maps, core_ids=list(range(n)))
    return np.stack([r["out"] for r in res.results], axis=0).astype(np.float32)
```

```python
import numpy as np
import concourse.bass as bass
import concourse.mybir as mybir
from contextlib import ExitStack

F32 = mybir.dt.float32
BF16 = mybir.dt.bfloat16
AF = mybir.ActivationFunctionType
ALU = mybir.AluOpType
AX = mybir.AxisListType
_ISZ = {F32: 4, BF16: 2}


def _ap(h):
    return h.ap() if hasattr(h, "ap") else h[:]


class Buf:
    __slots__ = ("name", "writers", "readers", "war_base", "lo", "hi", "psum")

    def __init__(self, name):
        self.name = name
        self.psum = False
        self.writers = []
        self.readers = []
        self.war_base = []


class V:
    __slots__ = ("buf", "ap")

    def __init__(self, buf, ap):
        self.buf = buf
        self.ap = ap

    def __getitem__(self, k):
        return V(self.buf, self.ap[k])

    def r(self, s, **kw):
        return V(self.buf, self.ap.rearrange(s, **kw))

    def bc(self, shape):
        return V(self.buf, self.ap.to_broadcast(list(shape)))

    def bitcast(self, dt):
        return V(self.buf, self.ap.bitcast(dt))

    def alias(self, buf):
        return V(buf, self.ap)


DMA_ENGS = ("sp", "act", "pool")
COMPUTE = ("pe", "act", "dve", "pool")
NSEM_DMA = {"sp": 16, "act": 8, "pool": 8}


class Prog:
    def __init__(self, nc):
        self.nc = nc
        self.ops = []
        self.ndma = {q: 0 for q in DMA_ENGS}
        self.sb_top = 0
        self.sb_regions = []
        self.sb_max = 0
        self.uid = 0
        self.psum_n = 0
        self.arena = None

    def sb(self, name, shape, dtype, nbufs=None):
        if self.arena is None:
            self.arena_bytes = 206 * 1024
            self.arena = _ap(self.nc.alloc_sbuf_tensor("arena", [128, self.arena_bytes // 4], F32))
        per = int(np.prod(shape[1:])) * _ISZ[dtype]
        lo = (self.sb_top + 31) // 32 * 32
        hi = lo + (per + 3) // 4 * 4
        assert hi <= self.arena_bytes, f"SBUF arena overflow: {name} {hi}"
        self.sb_top = hi
        self.sb_max = max(self.sb_max, hi)
        ap = self.arena[0:shape[0], lo // 4:hi // 4]
        if dtype != F32:
            ap = ap.bitcast(dtype)
        ap = ap[:, 0:int(np.prod(shape[1:]))]
        if len(shape) == 3:
            ap = ap.rearrange("p (a b) -> p a b", a=shape[1])
        elif len(shape) == 4:
            ap = ap.rearrange("p (a b c) -> p a b c", a=shape[1], b=shape[2])
        inherit = []
        for (l2, h2, b2) in self.sb_regions:
            if l2 < hi and lo < h2:
                inherit += b2.readers + b2.writers
        if nbufs is None:
            b = Buf(name)
            b.readers = list(set(inherit))
            b.lo, b.hi = lo, hi
            self.sb_regions.append((lo, hi, b))
            return V(b, ap)
        outs = []
        n = shape[1] // nbufs
        step = per // nbufs
        for i in range(nbufs):
            b = Buf(f"{name}{i}")
            b.readers = list(set(inherit))
            b.lo, b.hi = lo + i * step, lo + (i + 1) * step
            self.sb_regions.append((b.lo, b.hi, b))
            outs.append(V(b, ap[:, i * n:(i + 1) * n]))
        return outs

    def mark(self):
        return self.sb_top

    def release(self, m):
        self.sb_top = m
        if len(self.sb_regions) > 400:
            pass

    def psum(self, name, dtype=F32, cols=512):
        h = self.nc.alloc_psum_tensor(f"{name}", [128, cols], dtype)
        b = Buf(name)
        b.psum = True
        return V(b, _ap(h))

    def dram(self, name, shape, dtype, kind="Internal"):
        h = self.nc.dram_tensor(name, list(shape), dtype, kind=kind)
        return V(Buf(name), h.ap())

    def token(self, v, name="tok"):
        return V(Buf(name), v.ap)

    def add(self, eng, fn, reads, writes, partial=False, dma=False):
        idx = len(self.ops)
        deps = set()
        rb = {v.buf for v in reads}
        wb = {v.buf for v in writes}
        for b in rb:
            for w in b.writers:
                deps.add((w, "raw"))
            if b.psum:
                for r in b.readers:
                    deps.add((r, "rar"))
        for b in wb:
            for r in b.readers:
                deps.add((r, "war"))
            for r in b.war_base:
                deps.add((r, "war"))
            if not partial:
                for w in b.writers:
                    deps.add((w, "waw"))
        fdeps = set()
        for (d, kind) in deps:
            o = self.ops[d]
            if d == idx:
                continue
            if (not dma) and (not o[3]) and o[0] == eng and kind != "raw" and (eng == "pe" or kind == "rar"):
                continue
            fdeps.add(d)
        for b in wb:
            if partial:
                if b.readers:
                    b.war_base = list(b.readers)
                    b.writers = [idx]
                    b.readers = []
                else:
                    b.writers.append(idx)
            else:
                b.writers = [idx]
                b.readers = []
                b.war_base = []
        for b in rb:
            b.readers.append(idx)
        semi = None
        if dma:
            k = self.ndma[eng]
            self.ndma[eng] += 1
            S = NSEM_DMA[eng]
            semi = (eng, k % S, 16 * (k // S + 1), k)
        self.ops.append([eng, fn, sorted(fdeps), dma, semi, False, 0, [(b.name, getattr(b, 'lo', None), getattr(b, 'hi', None)) for b in rb], [(b.name, getattr(b, 'lo', None), getattr(b, 'hi', None)) for b in wb]])
        return idx

    def _vs(self, kw):
        reads, writes = [], []
        for k, v in kw.items():
            if isinstance(v, V):
                (writes if k in ("out", "accum_out") else reads).append(v)
        return reads, writes

    def I(self, eng, meth, _partial=False, _extra_r=(), _extra_w=(), **kw):
        reads, writes = self._vs(kw)
        reads += list(_extra_r)
        writes += list(_extra_w)
        args = {k: (v.ap if isinstance(v, V) else v) for k, v in kw.items()}

        def fn(e, meth=meth, args=args):
            return getattr(e, meth)(**args)
        return self.add(eng, fn, reads, writes, partial=_partial)

    def mm(self, out, lhsT, rhs, start=True, stop=True, **kw):
        return self.I("pe", "matmul", out=out, lhsT=lhsT, rhs=rhs, start=start, stop=stop, _partial=True, **kw)

    def tr(self, out, in_, ident):
        return self.I("pe", "transpose", out=out, in_=in_, identity=ident, _partial=True)

    def dma(self, out, in_, q="sp", partial=True, **kw):
        args = dict(out=out.ap, in_=in_.ap, **kw)

        def fn(e, args=args):
            return e.dma_start(**args)
        return self.add(q, fn, [in_], [out], partial=partial, dma=True)

    def emit(self):
        nc = self.nc
        ops = self.ops
        last = {}
        for i, o in enumerate(ops):
            if o[3]:
                last[(o[4][0], o[4][1])] = i
        fin_deps = sorted(last.values())
        ops.append(["sp", None, fin_deps, False, None, False, 0, [], []])
        for o in ops:
            for d in o[2]:
                if not ops[d][3]:
                    ops[d][5] = True
        cnt = {e: 0 for e in COMPUTE + ("sp",)}
        for o in ops:
            if not o[3] and o[5]:
                cnt[o[0]] += 1
                o[6] = cnt[o[0]]
        with ExitStack() as st:
            esem = {e: st.enter_context(nc.semaphore(f"s_{e}")) for e in COMPUTE}
            dsem = {q: [st.enter_context(nc.semaphore(f"d_{q}{i}")) for i in range(NSEM_DMA[q])] for q in DMA_ENGS}
            block = st.enter_context(nc.Block())
            per_eng = {e: [] for e in ("pe", "act", "dve", "pool", "sp")}
            for i, o in enumerate(ops):
                per_eng[o[0]].append(i)

            def run(ename, e):
                waited = {}
                for i in per_eng[ename]:
                    o = ops[i]
                    need = {}
                    for d in o[2]:
                        od = ops[d]
                        if od[3]:
                            key = ("d", od[4][0], od[4][1])
                            val = od[4][2]
                        else:
                            key = ("e", od[0])
                            val = od[6]
                        need[key] = max(need.get(key, 0), val)
                    if o[3]:
                        q, si, val, k = o[4]
                        if k >= NSEM_DMA[q]:
                            key = ("d", q, si)
                            need[key] = max(need.get(key, 0), val - 16)
                    for key, val in need.items():
                        if waited.get(key, 0) >= val:
                            continue
                        waited[key] = val
                        sem = dsem[key[1]][key[2]] if key[0] == "d" else esem[key[1]]
                        e.wait_ge(sem, val)
                    if o[1] is None:
                        continue
                    ins = o[1](e)
                    if o[3]:
                        ins.then_inc(dsem[o[4][0]][o[4][1]], 16)
                    elif o[5]:
                        ins.then_inc(esem[ename], 1)

            @block.tensor
            def _(e):
                run("pe", e)

            @block.scalar
            def _(e):
                run("act", e)

            @block.vector
            def _(e):
                run("dve", e)

            @block.gpsimd
            def _(e):
                run("pool", e)

            @block.sync
            def _(e):
                run("sp", e)
        return cnt

T = 4096
D = 1024
KD = 8
DIN = 10264
ALPHA = 4 ** 0.25
OFF = dict(z=0, xs=512, bm=1024, cm=1280, dt=1536, dq=1544, dk=2056, dv=2568, dbeta=3080, da=3084, dgate=3088,
           su=3600, sv=4112, fq=4624, fk=5136, fv=5648, ff=6160, gates=6168)


class Ctx:
    pass


def build(cfg):
    nc = bass.Bass("TRN2", target_bir_lowering=False)
    P = Prog(nc)
    C = Ctx()
    C.P, C.cfg = P, cfg
    L = cfg.get("nlayers", 2)
    real = cfg.get("mixers", "abcd")
    taps = cfg.get("taps", ())
    I = {}
    shapes = dict(x=[T, D], ln_in_g=[D], ln_in_b=[D], w_in=[2, D, DIN], ssd_conv_w=[2, 4, 1024], ssd_conv_b=[2, 1024],
                  ssd_dt_bias=[2, 8], ssd_a_log=[2, 8], ssd_d=[2, 8], ssd_norm_w=[2, 512], dn_conv_w=[2, 4, 1536],
                  dn_a_log=[2, 4], dn_dt_bias=[2, 4], dn_norm_w=[2, 128], sg_ln_g=[2, 512], sg_ln_b=[2, 512],
                  sg_w=[2, 4, 128, 128], sg_b=[2, 4, 128], fox_f_bias=[2, 8], gate_b=[2, 4, 1024],
                  w_branch=[2, 4, 512, 1024], w_out=[2, D, D], ln1_g=[2, D], ln1_b=[2, D], w_up=[2, D, 4096],
                  w_down=[2, 4096, D], ln2_g=[2, D], ln2_b=[2, D])
    for k, s in shapes.items():
        I[k] = P.dram(k, s, F32, kind="ExternalInput")
    for m in "abcd":
        if m not in real:
            I["yinj_" + m] = P.dram("yinj_" + m, [2, 512, T], F32, kind="ExternalInput")
    C.I = I
    C.out = P.dram("out", [T, D], F32, kind="ExternalOutput")
    C.tap = {}
    for t in taps:
        C.tap[t] = P.dram("tap_" + t, [512 if t.startswith("y") else D, T], F32, kind="ExternalOutput")
    def tiled(v):
        return [V(Buf(v.buf.name + str(i)), v.ap) for i in range(8)]
    C.hres_d = tiled(P.dram("hres_d", [D, T], F32))
    C.hT_d = tiled(P.dram("hT_d", [D, T], BF16))
    C.G_d = [tiled(P.dram(f"G_d{i}", [D, T], BF16)) for i in range(4)]
    for t in list(C.tap):
        C.tap[t] = tiled(C.tap[t])
    C.ps = [P.psum(f"ps{i}") for i in range(8)]
    C.psi = 0
    C.identf = P.sb("identf", [128, 128], F32)
    C.identb = P.sb("identb", [128, 128], BF16)
    C.onesb = P.sb("onesb", [128, 128], BF16)
    P.I("pool", "memset", ap=C.identf, constant=1.0, _extra_w=[C.identf])
    P.I("pool", "affine_select", out=C.identf, in_=C.identf, pattern=[[-1, 128]], compare_op=ALU.is_equal, fill=0.0,
        base=0, channel_multiplier=1)
    P.I("dve", "tensor_copy", out=C.identb, in_=C.identf)
    P.I("pool", "memset", ap=C.onesb, constant=1.0, _extra_w=[C.onesb])

    stage_entry(C)
    for l in range(L):
        phase_a(C, l)
        phase_b(C, l)
        phase_c(C, l, last=(l == L - 1))
    cnt = P.emit()
    return nc, P, cnt


def dbg(C, name, v, cast=False):
    if name not in C.cfg.get("dbg", ()):
        return
    P = C.P
    shp = list(v.ap.shape)
    if cast:
        mk = P.mark()
        tmp = P.sb("dbgtmp", shp, F32)
        P.I("dve", "tensor_copy", out=tmp, in_=v)
        v = tmp
    d = P.dram("dbg_" + name, shp, F32, kind="ExternalOutput")
    P.dma(d, v, q="sp")
    if cast:
        P.release(mk)


def nextps(C):
    rr = getattr(C, "rr", None)
    if rr:
        v = C.ps[rr[C.psi % len(rr)]]
    else:
        v = C.ps[C.psi % 8]
    C.psi += 1
    return v


def colvec(C, name, src, J, q="sp"):
    P = C.P
    t = P.sb(name, [128, J], F32)
    P.dma(t, src.r("(j p) -> p j", p=128), q=q, allow_slow_non_contiguous=True)
    return t


def dsl(v, t0, n):
    if isinstance(v, list):
        v = v[t0 // 512]
    return v.r("(j p) t -> p j t", p=128)[:, :, t0:t0 + n]


def ln_gen(C, xf, g, b, N, out_f=None, out_b=None):
    P = C.P
    m = P.mark()
    xb = P.sb("ln_xb", [128, 8, N], BF16)
    sq = P.sb("ln_sq", [128, 8, N], BF16)
    st = P.sb("ln_st", [128, 3, N], F32)
    P.I("act", "activation", out=xb, in_=xf, func=AF.Copy)
    P.I("act", "activation", out=sq, in_=xf, func=AF.Square)
    yield
    if getattr(C, "ln_banks", None):
        p1, p2 = C.ps[C.ln_banks[0]], C.ps[C.ln_banks[1]]
    else:
        p1, p2 = nextps(C), nextps(C)
    for k in range(8):
        P.mm(out=p1[:, 0:N], lhsT=C.onesb, rhs=xb[:, k, :], start=(k == 0), stop=(k == 7))
    yield
    for k in range(8):
        P.mm(out=p2[:, 0:N], lhsT=C.onesb, rhs=sq[:, k, :], start=(k == 0), stop=(k == 7))
    yield
    mean, msq, rstd = st[:, 0, :], st[:, 1, :], st[:, 2, :]
    P.I("dve", "tensor_scalar", out=mean, in0=p1[:, 0:N], scalar1=1.0 / D, scalar2=0.0, op0=ALU.mult, op1=ALU.add, _partial=True)
    P.I("dve", "tensor_tensor", out=msq, in0=mean, in1=mean, op=ALU.mult, _partial=True)
    P.I("dve", "scalar_tensor_tensor", out=rstd, in0=p2[:, 0:N], scalar=1.0 / D, in1=msq, op0=ALU.mult, op1=ALU.subtract,
        _partial=True)
    P.I("act", "activation", out=rstd, in_=rstd, func=AF.Ln, bias=1e-5, scale=1.0, _partial=True)
    P.I("act", "activation", out=rstd, in_=rstd, func=AF.Exp, scale=-0.5, _partial=True)
    yield
    P.I("dve", "tensor_tensor", out=xf, in0=xf, in1=mean[:, None, :].bc([128, 8, N]), op=ALU.subtract)
    yield
    P.I("dve", "tensor_tensor", out=xf, in0=xf, in1=rstd[:, None, :].bc([128, 8, N]), op=ALU.mult)
    yield
    for k in range(8):
        if k % 2 == 0:
            yield
        if out_b is not None:
            P.I("act", "activation", out=out_b[:, k, :], in_=xf[:, k, :], func=AF.Identity, scale=g[:, k:k + 1],
                bias=b[:, k:k + 1], _partial=True)
        if out_f is not None:
            P.I("act", "activation", out=out_f[:, k, :], in_=xf[:, k, :], func=AF.Identity, scale=g[:, k:k + 1],
                bias=b[:, k:k + 1], _partial=True)
    P.release(m)


def drain(gen):
    for _ in gen:
        pass


def interleave(ga, gb, ra=1, rb=1):
    da = db = False
    while not (da and db):
        for _ in range(ra):
            if not da:
                try:
                    next(ga)
                except StopIteration:
                    da = True
        for _ in range(rb):
            if not db:
                try:
                    next(gb)
                except StopIteration:
                    db = True


def ln_fm(C, xf, g, b, N, out_f=None, out_b=None):
    drain(ln_gen(C, xf, g, b, N, out_f=out_f, out_b=out_b))


def stage_entry(C):
    P, I = C.P, C.I
    m = P.mark()
    g = colvec(C, "ln0g", I["ln_in_g"], 8)
    b = colvec(C, "ln0b", I["ln_in_b"], 8)
    N = 512
    xin = [P.sb(f"xin{i}", [128, 4, D], F32) for i in range(2)]
    xfs = [P.sb(f"xf{i}", [128, 8, N], F32) for i in range(2)]
    hfs = [P.sb(f"hf{i}", [128, 8, N], F32) for i in range(2)]
    hbs = [P.sb(f"hb{i}", [128, 8, N], BF16) for i in range(2)]
    for t in range(T // N):
        xi, xf, hf, hb = xin[t % 2], xfs[t % 2], hfs[t % 2], hbs[t % 2]
        P.dma(xi, C.I["x"][t * N:(t + 1) * N, :].r("(s p) d -> p s d", p=128))
        for k in range(8):
            pk = nextps(C)
            for s in range(4):
                P.tr(out=pk[:, s * 128:(s + 1) * 128], in_=xi[:, s, k * 128:(k + 1) * 128], ident=C.identf)
            P.I("act", "activation", out=xf[:, k, :], in_=pk, func=AF.Copy, _partial=True)
        ln_fm(C, xf, g, b, N, out_f=hf, out_b=hb)
        P.dma(dsl(C.hres_d, t * N, N), hf, q="sp")
        P.dma(dsl(C.hT_d, t * N, N), hb, q="sp")
        if "h0" in C.tap:
            P.dma(dsl(C.tap["h0"], t * N, N), hf, q="sp")
    P.release(m)


def load_w(C, dst, src, q="pool"):
    C.P.dma(dst, src, q=q)


def gate_pass(C, l, i, hT, Y):
    P, I = C.P, C.I
    m = P.mark()
    wg = P.sb("wg", [128, 8, 1024], BF16)
    wb = P.sb("wb", [128, 4, 1024], BF16)
    c0 = OFF["gates"] + i * 1024
    for k in range(8):
        load_w(C, wg[:, k, :], I["w_in"][l, k * 128:(k + 1) * 128, c0:c0 + 1024])
    for k in range(4):
        load_w(C, wb[:, k, :], I["w_branch"][l, i, k * 128:(k + 1) * 128, :])
    gb = colvec(C, "gb", I["gate_b"][l, i], 8)
    N = 512
    gts = [P.sb(f"gt{i}", [128, N], F32) for i in range(3)]
    gbuf = [P.sb(f"gbuf{i}", [128, 8, N], BF16) for i in range(4)]
    n = 0
    for t in range(T // N):
        gbf = gbuf[t % 4]
        sl = slice(t * N, (t + 1) * N)
        for j in range(8):
            p1, p2 = nextps(C), nextps(C)
            for k in range(8):
                P.mm(out=p1, lhsT=wg[:, k, j * 128:(j + 1) * 128], rhs=hT[:, k, sl], start=(k == 0), stop=(k == 7))
            gt = gts[n % 3]
            n += 1
            P.I("act", "activation", out=gt, in_=p1, func=AF.Sigmoid, bias=gb[:, j:j + 1])
            for k in range(4):
                P.mm(out=p2, lhsT=wb[:, k, j * 128:(j + 1) * 128], rhs=Y[:, k, sl], start=(k == 0), stop=(k == 3))
            P.I("dve", "tensor_tensor", out=gbf[:, j, :], in0=p2, in1=gt, op=ALU.mult, _partial=True)
        P.dma(dsl(C.G_d[i], t * N, N), gbf, q="sp")
    P.release(m)


def phase_a(C, l):
    P, I = C.P, C.I
    m = P.mark()
    hT = P.sb("hT", [128, 8, T], BF16)
    for t in range(8):
        P.dma(hT[:, :, t * 512:(t + 1) * 512], dsl(C.hT_d, t * 512, 512))
    real = C.cfg.get("mixers", "abcd")
    fns = dict(a=mixer_ssd, b=mixer_dn, c=mixer_sg, d=mixer_fox)
    for i, mx in enumerate("abcd"):
        m2 = P.mark()
        Y = P.sb("Y", [128, 4, T], BF16)
        if mx in real:
            fns[mx](C, l, hT, Y)
        else:
            for k in range(4):
                load_w(C, Y[:, k, :], I["yinj_" + mx][l, k * 128:(k + 1) * 128, :])
        tn = f"y_{mx}_{l}"
        if tn in C.tap:
            yf = P.sb("ytap", [128, 4, 1024], F32)
            for t4 in range(4):
                P.I("act", "activation", out=yf, in_=Y[:, :, t4 * 1024:(t4 + 1) * 1024], func=AF.Copy)
                for t2 in range(2):
                    P.dma(dsl(C.tap[tn], t4 * 1024 + t2 * 512, 512), yf[:, :, t2 * 512:(t2 + 1) * 512], q="sp")
        gate_pass(C, l, i, hT, Y)
        P.release(m2)
    P.release(m)


def phase_b(C, l):
    P, I = C.P, C.I
    m = P.mark()
    wo = P.sb("wo", [128, 8, 1024], BF16)
    for k in range(8):
        load_w(C, wo[:, k, :], I["w_out"][l, k * 128:(k + 1) * 128, :])
    g = colvec(C, "ln1g", I["ln1_g"][l], 8)
    b = colvec(C, "ln1b", I["ln1_b"][l], 8)
    N = 512
    gl = [P.sb(f"gl{i}", [128, 8, N], BF16) for i in range(8)]
    macc = P.sb("macc", [128, 8, N], F32)
    mb = P.sb("mb", [128, 8, N], BF16)
    hr = P.sb("hr", [128, 8, N], F32)
    x1s = [P.sb(f"x1_{i}", [128, 8, N], F32) for i in range(2)]
    hbs = [P.sb(f"hb_{i}", [128, 8, N], BF16) for i in range(2)]
    lnbuf = P.mark()
    def front(t):
        x1 = x1s[t % 2]
        P.dma(hr, dsl(C.hres_d, t * N, N), q="sp")
        gs4 = []
        for i in range(4):
            gg = gl[(4 * t + i) % 8]
            P.dma(gg, dsl(C.G_d[i], t * N, N), q="sp")
            gs4.append(gg)
        yield
        P.I("dve", "tensor_tensor", out=macc, in0=gs4[0], in1=gs4[1], op=ALU.add)
        yield
        P.I("dve", "tensor_tensor", out=macc, in0=macc, in1=gs4[2], op=ALU.add)
        yield
        P.I("dve", "tensor_tensor", out=macc, in0=macc, in1=gs4[3], op=ALU.add)
        tn = f"merged_{l}"
        if tn in C.tap:
            P.dma(dsl(C.tap[tn], t * N, N), macc, q="sp")
        yield
        P.I("act", "activation", out=mb, in_=macc, func=AF.Copy)
        yield
        for j in range(8):
            p1 = nextps(C)
            for k in range(8):
                P.mm(out=p1, lhsT=wo[:, k, j * 128:(j + 1) * 128], rhs=mb[:, k, :], start=(k == 0), stop=(k == 7))
            P.I("dve", "scalar_tensor_tensor", out=x1[:, j, :], in0=hr[:, j, :], scalar=ALPHA, in1=p1, op0=ALU.mult,
                op1=ALU.add, _partial=True)
            yield

    def tail(t):
        x1 = x1s[t % 2]
        hb = hbs[t % 2]
        yield from ln_gen(C, x1, g, b, N, out_f=x1, out_b=hb)
        P.dma(dsl(C.hres_d, t * N, N), x1, q="sp")
        P.dma(dsl(C.hT_d, t * N, N), hb, q="sp")
        tn = f"h1_{l}"
        if tn in C.tap:
            P.dma(dsl(C.tap[tn], t * N, N), x1, q="sp")

    C.rr, C.ln_banks = [0, 1, 2, 3, 4, 5], (6, 7)
    prev = None
    for t in range(T // N):
        f = front(t)
        if prev is None:
            drain(f)
        else:
            interleave(f, prev)
        prev = tail(t)
    drain(prev)
    C.rr, C.ln_banks = None, None
    P.release(m)


def phase_c(C, l, last):
    P, I = C.P, C.I
    m = P.mark()
    wu = P.sb("wu", [128, 8, 4096], BF16)
    wd = P.sb("wd", [128, 32, 1024], BF16)
    for k in range(8):
        for hh in range(2):
            load_w(C, wu[:, k, hh * 2048:(hh + 1) * 2048], I["w_up"][l, k * 128:(k + 1) * 128, hh * 2048:(hh + 1) * 2048])
    for f in range(32):
        load_w(C, wd[:, f, :], I["w_down"][l, f * 128:(f + 1) * 128, :])
    g = colvec(C, "ln2g", I["ln2_g"][l], 8)
    b = colvec(C, "ln2b", I["ln2_b"][l], 8)
    N = 256
    hbt = [P.sb(f"c_hb{i}", [128, 8, N], BF16) for i in range(2)]
    hrt = [P.sb(f"c_hr{i}", [128, 8, N], F32) for i in range(1)]
    act = P.sb("c_act", [128, 32, N], BF16)
    rl = [P.sb(f"c_rl{i}", [128, N], F32) for i in range(3)]
    x2s = [P.sb(f"c_x2_{i}", [128, 8, N], F32) for i in range(2)]
    hb = None if last else P.sb("c_hbo", [128, 8, N], BF16)
    ot = [P.sb(f"c_ot{i}", [128, D], F32) for i in range(2)] if last else None
    cnt = dict(n=0, no=0)

    def front(t):
        h1b, h1 = hbt[t % 2], hrt[0]
        x2 = x2s[t % 2]
        P.dma(h1b, dsl(C.hT_d, t * N, N), q="sp")
        P.dma(h1, dsl(C.hres_d, t * N, N), q="sp")
        yield
        for f in range(32):
            p1 = nextps(C)
            for k in range(8):
                P.mm(out=p1[:, 0:N], lhsT=wu[:, k, f * 128:(f + 1) * 128], rhs=h1b[:, k, :], start=(k == 0), stop=(k == 7))
            r = rl[cnt["n"] % 3]
            cnt["n"] += 1
            P.I("act", "activation", out=r, in_=p1[:, 0:N], func=AF.Relu)
            P.I("dve", "tensor_tensor", out=act[:, f, :], in0=r, in1=r, op=ALU.mult, _partial=True)
            if f % 2 == 1:
                yield
        for j in range(8):
            p1 = nextps(C)
            for f in range(32):
                P.mm(out=p1[:, 0:N], lhsT=wd[:, f, j * 128:(j + 1) * 128], rhs=act[:, f, :], start=(f == 0), stop=(f == 31))
            P.I("dve", "scalar_tensor_tensor", out=x2[:, j, :], in0=h1[:, j, :], scalar=ALPHA, in1=p1[:, 0:N], op0=ALU.mult,
                op1=ALU.add, _partial=True)
            yield

    def tail(t):
        x2 = x2s[t % 2]
        hf = x2
        yield from ln_gen(C, x2, g, b, N, out_f=hf, out_b=(None if last else hb))
        tn = f"h2_{l}"
        if tn in C.tap:
            P.dma(dsl(C.tap[tn], t * N, N), hf, q="sp")
        if not last:
            P.dma(dsl(C.hres_d, t * N, N), hf, q="sp")
            P.dma(dsl(C.hT_d, t * N, N), hb, q="sp")
        else:
            for s_ in range(N // 128):
                o = ot[cnt["no"] % 2]
                cnt["no"] += 1
                for half in range(2):
                    pk = nextps(C)
                    for kk in range(4):
                        k = half * 4 + kk
                        P.tr(out=pk[:, kk * 128:(kk + 1) * 128], in_=hf[:, k, s_ * 128:(s_ + 1) * 128], ident=C.identf)
                    P.I("act", "activation", out=o[:, half * 512:(half + 1) * 512], in_=pk, func=AF.Copy, _partial=True)
                    yield
                r0 = t * N + s_ * 128
                P.dma(C.out[r0:r0 + 128, :], o, q="sp")

    C.rr, C.ln_banks = [0, 1, 2, 3, 4, 5], (6, 7)
    prev = None
    for t in range(T // N):
        f_ = front(t)
        if prev is None:
            drain(f_)
        else:
            interleave(f_, prev, ra=2, rb=1)
        prev = tail(t)
    drain(prev)
    C.rr, C.ln_banks = None, None
    P.release(m)


def make_tri(C, name):
    P = C.P
    t = P.sb(name, [128, 128], F32)
    P.I("pool", "memset", ap=t, constant=1.0, _extra_w=[t])
    P.I("pool", "affine_select", out=t, in_=t, pattern=[[1, 128]], compare_op=ALU.is_ge, fill=0.0, base=0,
        channel_multiplier=-1)
    return t


def conv4(C, acc, raw, cw, nj, NT):
    P = C.P
    for j in range(nj):
        P.I("dve", "tensor_scalar", out=acc[:, j, :], in0=raw[:, j, 3:3 + NT], scalar1=cw[:, 3, j:j + 1], scalar2=0.0,
            op0=ALU.mult, op1=ALU.add, _partial=True)
        for k in range(3):
            P.I("dve", "scalar_tensor_tensor", out=acc[:, j, :], in0=raw[:, j, k:k + NT], scalar=cw[:, k, j:j + 1],
                in1=acc[:, j, :], op0=ALU.mult, op1=ALU.add, _partial=True)


def mixer_ssd(C, l, hT, Y):
    P, I = C.P, C.I
    m = P.mark()
    NT = 256
    NC = NT // 128
    wz = P.sb("sd_wz", [128, 8, 512], BF16)
    wx = P.sb("sd_wx", [128, 8, 1024], BF16)
    wdt = P.sb("sd_wdt", [128, 8, 8], BF16)
    for k in range(8):
        rows = slice(k * 128, (k + 1) * 128)
        load_w(C, wz[:, k, :], I["w_in"][l, rows, OFF["z"]:OFF["z"] + 512])
        load_w(C, wx[:, k, :], I["w_in"][l, rows, OFF["xs"]:OFF["xs"] + 1024])
        load_w(C, wdt[:, k, :], I["w_in"][l, rows, OFF["dt"]:OFF["dt"] + 8])
    cw = P.sb("sd_cw", [128, 4, 8], F32)
    for k in range(4):
        P.dma(cw[:, k, :], I["ssd_conv_w"][l, k].r("(j p) -> p j", p=128), allow_slow_non_contiguous=True)
    cb = colvec(C, "sd_cb", I["ssd_conv_b"][l], 8)
    dtb = bcast_rows(C, "sd_dtb", I["ssd_dt_bias"][l], 8)
    abc = bcast_rows(C, "sd_abc", I["ssd_a_log"][l], 8)
    P.I("act", "activation", out=abc, in_=abc, func=AF.Exp)
    P.I("dve", "tensor_scalar", out=abc, in0=abc, scalar1=-1.0, scalar2=0.0, op0=ALU.mult, op1=ALU.add)
    drep = P.sb("sd_drep", [128, 4], F32)
    for j in range(4):
        for hf in range(2):
            src = I["ssd_d"][l, 2 * j + hf:2 * j + hf + 1]
            P.dma(drep[hf * 64:(hf + 1) * 64, j:j + 1], V(src.buf, src.ap.partition_broadcast(64)))
    nws = colvec(C, "sd_nw", I["ssd_norm_w"][l], 4)
    P.I("dve", "tensor_scalar", out=nws, in0=nws, scalar1=float(512 ** 0.5), scalar2=0.0, op0=ALU.mult, op1=ALU.add)
    tri = make_tri(C, "sd_tri")
    S = P.sb("sd_S", [128, 8, 64], F32)
    Sb = P.sb("sd_Sb", [128, 512], BF16)
    P.I("pool", "memset", ap=S, constant=0.0, _extra_w=[S])
    P.I("pool", "memset", ap=Sb, constant=0.0, _extra_w=[Sb])
    raw = P.sb("sd_raw", [128, 8, 3 + NT], F32)
    P.I("pool", "memset", ap=raw[:, :, 0:3], constant=0.0, _extra_w=[raw])
    acc = P.sb("sd_acc", [128, 8, NT], F32)
    xs = P.sb("sd_xs", [128, 4, NT], F32)
    xsb = P.sb("sd_xsb", [128, 4, NT], BF16)
    bmT = P.sb("sd_bmT", [128, 2, NT], BF16)
    cmT = P.sb("sd_cmT", [128, 2, NT], BF16)
    zs = P.sb("sd_zs", [128, 4, NT], F32)
    y1 = P.sb("sd_y1", [128, 4, NT], F32)
    sq = P.sb("sd_sq", [128, 4, NT], BF16)
    rs = P.sb("sd_rs", [128, NT], F32)
    sm = P.sb("sd_sm", [128, 5, 8], F32)
    dAs = P.sb("sd_dAs", [128, 2, 8], BF16)
    dAsf = P.sb("sd_dAsf", [128, 2, 8], F32)
    dAb = P.sb("sd_dAb", [128, 2, 8, 128], BF16)
    trib = P.sb("sd_trib", [128, 128], BF16)
    P.I("dve", "tensor_copy", out=trib, in_=tri)
    E = P.sb("sd_E", [128, 8, 128], F32)
    Dt = P.sb("sd_Dt", [128, 8, 128], F32)
    MT = P.sb("sd_MT", [128, 8, 128], BF16)
    cms = P.sb("sd_cms", [128, 8, 128], BF16)
    xdt = P.sb("sd_xdt", [128, 8, 64], BF16)
    xdd = P.sb("sd_xdd", [128, 8, 64], BF16)
    bmtm = P.sb("sd_bmtm", [128, 2, 128], BF16)
    dt_, dA, acs, dte, dt2 = sm[:, 0, :], sm[:, 1, :], sm[:, 2, :], sm[:, 3, :], sm[:, 4, :]
    for t in range(T // NT):
        sl = slice(t * NT, (t + 1) * NT)
        if t > 0:
            P.I("dve", "tensor_copy", out=raw[:, :, 0:3], in_=raw[:, :, NT:NT + 3])
        for j in range(4):
            p1 = nextps(C)
            for k in range(8):
                P.mm(out=p1[:, 0:NT], lhsT=wz[:, k, j * 128:(j + 1) * 128], rhs=hT[:, k, sl], start=(k == 0), stop=(k == 7))
            P.I("act", "activation", out=zs[:, j, :], in_=p1[:, 0:NT], func=AF.Silu, _partial=True)
        for j in range(8):
            p1 = nextps(C)
            for k in range(8):
                P.mm(out=p1[:, 0:NT], lhsT=wx[:, k, j * 128:(j + 1) * 128], rhs=hT[:, k, sl], start=(k == 0), stop=(k == 7))
            P.I("act", "activation", out=raw[:, j, 3:3 + NT], in_=p1[:, 0:NT], func=AF.Copy, _partial=True)
        conv4(C, acc, raw, cw, 8, NT)
        for j in range(8):
            if j < 4:
                P.I("act", "activation", out=xs[:, j, :], in_=acc[:, j, :], func=AF.Silu, bias=cb[:, j:j + 1], _partial=True)
            else:
                dst = bmT[:, j - 4, :] if j < 6 else cmT[:, j - 6, :]
                P.I("act", "activation", out=dst, in_=acc[:, j, :], func=AF.Silu, bias=cb[:, j:j + 1], _partial=True)
        P.I("dve", "tensor_copy", out=xsb, in_=xs)
        stop = C.cfg.get("sd_stop", 99)
        for c in range(NC if stop > 1 else 0):
            c0 = t * NT + c * 128
            cl = slice(c * 128, (c + 1) * 128)
            p1 = nextps(C)
            for k in range(8):
                P.mm(out=p1[:, 0:8], lhsT=hT[:, k, c0:c0 + 128], rhs=wdt[:, k, :], start=(k == 0), stop=(k == 7))
            P.I("dve", "tensor_tensor", out=dt_, in0=p1[:, 0:8], in1=dtb, op=ALU.add, _partial=True)
            P.I("act", "activation", out=dt_, in_=dt_, func=AF.Exp, _partial=True)
            P.I("act", "activation", out=dt_, in_=dt_, func=AF.Ln, bias=1.0, scale=1.0, _partial=True)
            P.I("dve", "tensor_tensor", out=dA, in0=dt_, in1=abc, op=ALU.mult, _partial=True)
            if stop <= 1.1:
                continue
            P.I("dve", "tensor_copy", out=dAs[:, 0, :], in_=dA, _partial=True)
            P.I("dve", "tensor_tensor", out=dt2, in0=dA, in1=dAs[:, 0, :], op=ALU.subtract, _partial=True)
            P.I("dve", "tensor_copy", out=dAs[:, 1, :], in_=dt2, _partial=True)
            P.I("dve", "tensor_copy", out=dAsf, in_=dAs)
            for i2 in range(2):
                for h in range(8):
                    P.I("dve", "tensor_scalar", out=dAb[:, i2, h, :], in0=C.onesb, scalar1=dAsf[:, i2, h:h + 1], scalar2=0.0,
                        op0=ALU.mult, op1=ALU.add, _partial=True)
            if stop <= 1.2:
                continue
            p2 = nextps(C)
            P.mm(out=p2[:, 0:8], lhsT=trib, rhs=dAs[:, 0, :], start=True, stop=False)
            P.mm(out=p2[:, 0:8], lhsT=trib, rhs=dAs[:, 1, :], start=False, stop=True)
            P.I("dve", "tensor_copy", out=acs, in_=p2[:, 0:8], _partial=True)
            if stop <= 1.3:
                continue
            pB = [nextps(C), nextps(C)]
            for h in range(8):
                o = pB[h // 4][:, (h % 4) * 128:(h % 4 + 1) * 128]
                P.mm(out=o, lhsT=dAb[:, 0, h, :], rhs=trib, start=True, stop=False)
                P.mm(out=o, lhsT=dAb[:, 1, h, :], rhs=trib, start=False, stop=True)
            if stop <= 1.4:
                continue
            for hh in range(2):
                hs = slice(hh * 4, (hh + 1) * 4)
                pv = pB[hh].r("p (h l) -> p h l", h=4)
                P.I("act", "activation", out=E[:, hs, :], in_=pv, func=AF.Exp, _partial=True)
                if stop <= 1.5:
                    continue
                for h4 in range(4):
                    h = hh * 4 + h4
                    P.I("dve", "tensor_scalar", out=Dt[:, h, :], in0=pv[:, h4, :], scalar1=acs[:, h:h + 1], scalar2=0.0,
                        op0=ALU.subtract, op1=ALU.min, _partial=True)
                if stop <= 1.6:
                    continue
                P.I("dve", "tensor_tensor", out=dte[:, hs], in0=pv[:, :, 127], in1=acs[:, hs], op=ALU.subtract, _partial=True)
            if stop <= 1.7:
                continue
            P.I("act", "activation", out=Dt, in_=Dt, func=AF.Exp)
            P.I("act", "activation", out=dte, in_=dte, func=AF.Exp, _partial=True)
            if stop <= 1.8:
                continue
            P.I("dve", "tensor_tensor", out=Dt, in0=Dt, in1=tri[:, None, :].bc([128, 8, 128]), op=ALU.mult)
            if stop <= 2:
                continue
            pcb = nextps(C)
            for g in range(2):
                P.mm(out=pcb[:, g * 128:(g + 1) * 128], lhsT=bmT[:, g, cl], rhs=cmT[:, g, cl])
            for g in range(2):
                hs = slice(g * 4, (g + 1) * 4)
                P.I("dve", "tensor_tensor", out=MT[:, hs, :], in0=Dt[:, hs, :],
                    in1=pcb[:, g * 128:(g + 1) * 128][:, None, :].bc([128, 4, 128]), op=ALU.mult, _partial=True)
                P.I("dve", "tensor_tensor", out=cms[:, hs, :], in0=E[:, hs, :],
                    in1=cmT[:, g, cl][:, None, :].bc([128, 4, 128]), op=ALU.mult, _partial=True)
            if stop <= 3:
                continue
            pt = nextps(C).bitcast(BF16)
            for j in range(4):
                P.tr(out=pt[:, j * 128:(j + 1) * 128], in_=xsb[:, j, cl], ident=C.identb)
            for h in range(8):
                P.I("dve", "tensor_scalar", out=xdt[:, h, :], in0=pt[:, h * 64:(h + 1) * 64], scalar1=dt_[:, h:h + 1],
                    scalar2=0.0, op0=ALU.mult, op1=ALU.add, _partial=True)
                P.I("dve", "tensor_scalar", out=xdd[:, h, :], in0=pt[:, h * 64:(h + 1) * 64], scalar1=dt_[:, h:h + 1],
                    scalar2=dte[:, h:h + 1], op0=ALU.mult, op1=ALU.mult, _partial=True)
            pb2 = nextps(C).bitcast(BF16)
            for g in range(2):
                P.tr(out=pb2[:, g * 128:(g + 1) * 128], in_=bmT[:, g, cl], ident=C.identb)
            P.I("act", "activation", out=bmtm, in_=pb2[:, 0:256].r("p (g n) -> p g n", g=2), func=AF.Copy)
            if stop <= 4:
                continue
            py = nextps(C)
            for h in range(8):
                po = (h % 2) * 64
                o = py[po:po + 64, (h // 2) * 128:(h // 2 + 1) * 128]
                P.mm(out=o, lhsT=xdt[:, h, :], rhs=MT[:, h, :], start=True, stop=False)
                P.mm(out=o, lhsT=Sb[:, h * 64:(h + 1) * 64], rhs=cms[:, h, :], start=False, stop=True)
            pS = nextps(C)
            for g in range(2):
                P.mm(out=pS[:, g * 256:(g + 1) * 256], lhsT=bmtm[:, g, :], rhs=xdd[:, g * 4:(g + 1) * 4, :])
            for h in range(8):
                P.I("dve", "scalar_tensor_tensor", out=S[:, h, :], in0=S[:, h, :], scalar=E[:, h, 127:128],
                    in1=pS[:, h * 64:(h + 1) * 64], op0=ALU.mult, op1=ALU.add, _partial=True)
            P.I("act", "activation", out=Sb, in_=S.r("p h d -> p (h d)"), func=AF.Copy)
            for j in range(4):
                P.I("dve", "scalar_tensor_tensor", out=y1[:, j, cl], in0=xs[:, j, cl], scalar=drep[:, j:j + 1],
                    in1=py[:, j * 128:(j + 1) * 128], op0=ALU.mult, op1=ALU.add, _partial=True)
        P.I("dve", "tensor_tensor", out=y1, in0=y1, in1=zs, op=ALU.mult)
        P.I("act", "activation", out=sq, in_=y1, func=AF.Square)
        pr = nextps(C)
        for j in range(4):
            P.mm(out=pr[:, 0:NT], lhsT=C.onesb, rhs=sq[:, j, :], start=(j == 0), stop=(j == 3))
        P.I("act", "activation", out=rs, in_=pr[:, 0:NT], func=AF.Ln, bias=float(512 * 1e-6), scale=1.0)
        P.I("act", "activation", out=rs, in_=rs, func=AF.Exp, scale=-0.5)
        P.I("dve", "tensor_tensor", out=y1, in0=y1, in1=rs[:, None, :].bc([128, 4, NT]), op=ALU.mult)
        for j in range(4):
            P.I("act", "activation", out=Y[:, j, sl], in_=y1[:, j, :], func=AF.Identity, scale=nws[:, j:j + 1], _partial=True)
    P.release(m)


def mixer_dn(C, l, hT, Y):
    P, I = C.P, C.I
    m = P.mark()
    NT = 128
    wq = P.sb("dn_wq", [128, 8, 512], BF16)
    wk = P.sb("dn_wk", [128, 8, 512], BF16)
    wv = P.sb("dn_wv", [128, 8, 512], BF16)
    wg = P.sb("dn_wg", [128, 8, 512], BF16)
    wba = P.sb("dn_wba", [128, 8, 8], BF16)
    for k in range(8):
        rows = slice(k * 128, (k + 1) * 128)
        load_w(C, wq[:, k, :], I["w_in"][l, rows, OFF["dq"]:OFF["dq"] + 512])
        load_w(C, wk[:, k, :], I["w_in"][l, rows, OFF["dk"]:OFF["dk"] + 512])
        load_w(C, wv[:, k, :], I["w_in"][l, rows, OFF["dv"]:OFF["dv"] + 512])
        load_w(C, wg[:, k, :], I["w_in"][l, rows, OFF["dgate"]:OFF["dgate"] + 512])
        load_w(C, wba[:, k, :], I["w_in"][l, rows, OFF["dbeta"]:OFF["dbeta"] + 8])
    cw = P.sb("dn_cw", [128, 4, 12], F32)
    for k in range(4):
        P.dma(cw[:, k, :], I["dn_conv_w"][l, k].r("(j p) -> p j", p=128), allow_slow_non_contiguous=True)
    dtb = bcast_rows(C, "dn_dtb", I["dn_dt_bias"][l], 4)
    nal = bcast_rows(C, "dn_nal", I["dn_a_log"][l], 4)
    P.I("act", "activation", out=nal, in_=nal, func=AF.Exp)
    P.I("dve", "tensor_scalar", out=nal, in0=nal, scalar1=-1.0, scalar2=0.0, op0=ALU.mult, op1=ALU.add)
    nw = colvec(C, "dn_nw", I["dn_norm_w"][l], 1)
    def mk_mask(name, strict, lower):
        t = P.sb(name, [128, 128], F32)
        P.I("pool", "memset", ap=t, constant=1.0, _extra_w=[t])
        sgn = -1 if not lower else 1
        P.I("pool", "affine_select", out=t, in_=t, pattern=[[-sgn, 128]], compare_op=(ALU.is_gt if strict else ALU.is_ge),
            fill=0.0, base=0, channel_multiplier=sgn)
        P.I("pool", "memset", ap=t[0:64, 64:128], constant=0.0, _extra_w=[t])
        P.I("pool", "memset", ap=t[64:128, 0:64], constant=0.0, _extra_w=[t])
        return t
    Mui = mk_mask("dn_Mui", False, False)
    Mls = mk_mask("dn_Mls", True, True)
    tri2b = P.sb("dn_tri2b", [128, 128], BF16)
    P.I("dve", "tensor_copy", out=tri2b, in_=Mui)
    S = P.sb("dn_S", [128, 4, 128], F32)
    Sb = P.sb("dn_Sb", [128, 4, 128], BF16)
    P.I("pool", "memset", ap=S, constant=0.0, _extra_w=[S])
    P.I("pool", "memset", ap=Sb, constant=0.0, _extra_w=[Sb])
    raw = P.sb("dn_raw", [128, 12, 3 + NT], F32)
    P.I("pool", "memset", ap=raw[:, :, 0:3], constant=0.0, _extra_w=[raw])
    acc = P.sb("dn_acc", [128, 12, NT], F32)
    qk = P.sb("dn_qk", [128, 8, NT], F32)
    vTb = P.sb("dn_vTb", [128, 4, NT], BF16)
    sq = P.sb("dn_sq", [128, 8, NT], BF16)
    rn = [P.sb(f"dn_rn{i}", [128, NT], F32) for i in range(2)]
    qT = P.sb("dn_qT", [128, 4, NT], BF16)
    kT = P.sb("dn_kT", [128, 4, NT], BF16)
    gs_2 = [P.sb(f"dn_gs{i}", [128, 4, NT], F32) for i in range(2)]
    sq2 = P.sb("dn_sq2", [128, 4, NT], BF16)
    rn2 = [P.sb(f"dn_rn2{i}", [128, NT], F32) for i in range(2)]
    oT = P.sb("dn_oT", [128, 4, NT], F32)
    sm = P.sb("dn_sm", [128, 8, 4], F32)
    beta, g_, gcs, egc, sc1, dk, tmp4 = (sm[:, i, :] for i in range(7))
    gsp = P.sb("dn_gsp", [128, 2, 4], BF16)
    gspf = P.sb("dn_gspf", [128, 2, 4], F32)
    gb = P.sb("dn_gb", [128, 2, 4, 128], BF16)
    Eg_2 = [P.sb(f"dn_Eg{i}", [128, 4, 128], F32) for i in range(2)]
    GL = P.sb("dn_GL", [128, 4, 128], F32)
    G = P.sb("dn_G", [128, 4, 128], F32)
    A32 = P.sb("dn_A32", [128, 4, 128], F32)
    N32 = P.sb("dn_N32", [128, 4, 128], F32)
    Ab = P.sb("dn_Ab", [128, 4, 128], BF16)
    Nb = P.sb("dn_Nb", [128, 4, 128], BF16)
    R = P.sb("dn_R", [128, 4, 128], F32)
    Rb = P.sb("dn_Rb", [128, 4, 128], BF16)
    Xb = [P.sb(f"dn_Xb{i}", [128, 4, 128], BF16) for i in range(2)]
    XTb = [P.sb(f"dn_XTb{i}", [128, 4, 128], BF16) for i in range(2)]
    TTh = P.sb("dn_TTh", [128, 4, 128], BF16)
    TTl = P.sb("dn_TTl", [128, 4, 128], BF16)
    kbeg = P.sb("dn_kbeg", [128, 4, 128], BF16)
    kdec_2 = [P.sb(f"dn_kdec{i}", [128, 4, 128], BF16) for i in range(2)]
    vb = P.sb("dn_vb", [128, 4, 128], BF16)
    wTb_2 = [P.sb(f"dn_wTb{i}", [128, 4, 128], BF16) for i in range(2)]
    u_2 = [P.sb(f"dn_u{i}", [128, 4, 128], F32) for i in range(2)]
    qgT_2 = [P.sb(f"dn_qgT{i}", [128, 4, 128], BF16) for i in range(2)]
    qkTm_2 = [P.sb(f"dn_qkTm{i}", [128, 4, 128], BF16) for i in range(2)]
    vnb = P.sb("dn_vnb", [128, 4, 128], BF16)
    QS = float(128 ** -0.5)
    def prep(t):
            wTb, u, qgT, qkTm, kdec, Eg, gs = (x[t % 2] for x in (wTb_2, u_2, qgT_2, qkTm_2, kdec_2, Eg_2, gs_2))
            sl = slice(t * NT, (t + 1) * NT)
            if t > 0:
                P.I("dve", "tensor_copy", out=raw[:, :, 0:3], in_=raw[:, :, NT:NT + 3])
            for j in range(12):
                w_ = (wq, wk, wv)[j // 4]
                jj = j % 4
                p1 = nextps(C)
                for k in range(8):
                    P.mm(out=p1[:, 0:NT], lhsT=w_[:, k, jj * 128:(jj + 1) * 128], rhs=hT[:, k, sl], start=(k == 0), stop=(k == 7))
                P.I("act", "activation", out=raw[:, j, 3:3 + NT], in_=p1[:, 0:NT], func=AF.Copy, _partial=True)
                if j % 3 == 2:
                    yield
            for j in range(4):
                p1 = nextps(C)
                for k in range(8):
                    P.mm(out=p1[:, 0:NT], lhsT=wg[:, k, j * 128:(j + 1) * 128], rhs=hT[:, k, sl], start=(k == 0), stop=(k == 7))
                P.I("act", "activation", out=gs[:, j, :], in_=p1[:, 0:NT], func=AF.Silu, _partial=True)
                yield
            conv4(C, acc, raw, cw, 12, NT)
            yield
            P.I("act", "activation", out=qk, in_=acc[:, 0:8, :], func=AF.Silu)
            P.I("act", "activation", out=vTb, in_=acc[:, 8:12, :], func=AF.Silu)
            P.I("act", "activation", out=sq, in_=qk, func=AF.Square)
            yield
            for i in range(8):
                p1 = nextps(C)
                P.mm(out=p1[:, 0:NT], lhsT=C.onesb, rhs=sq[:, i, :])
                r_ = rn[i % 2]
                P.I("act", "activation", out=r_, in_=p1[:, 0:NT], func=AF.Ln, bias=1e-6, scale=1.0)
                P.I("act", "activation", out=r_, in_=r_, func=AF.Exp, scale=-0.5)
                dst = qT[:, i, :] if i < 4 else kT[:, i - 4, :]
                P.I("dve", "scalar_tensor_tensor", out=dst, in0=qk[:, i, :], scalar=(QS if i < 4 else 1.0), in1=r_,
                    op0=ALU.mult, op1=ALU.mult, _partial=True)
                if i % 2 == 1:
                    yield
            p1 = nextps(C)
            for k in range(8):
                P.mm(out=p1[:, 0:8], lhsT=hT[:, k, sl], rhs=wba[:, k, :], start=(k == 0), stop=(k == 7))
            P.I("act", "activation", out=beta, in_=p1[:, 0:4], func=AF.Sigmoid, _partial=True)
            P.I("dve", "tensor_tensor", out=g_, in0=p1[:, 4:8], in1=dtb, op=ALU.add, _partial=True)
            P.I("act", "activation", out=g_, in_=g_, func=AF.Exp, _partial=True)
            P.I("act", "activation", out=g_, in_=g_, func=AF.Ln, bias=1.0, scale=1.0, _partial=True)
            P.I("dve", "tensor_tensor", out=g_, in0=g_, in1=nal, op=ALU.mult, _partial=True)
            yield
            P.I("dve", "tensor_copy", out=gsp[:, 0, :], in_=g_, _partial=True)
            P.I("dve", "tensor_tensor", out=tmp4, in0=g_, in1=gsp[:, 0, :], op=ALU.subtract, _partial=True)
            P.I("dve", "tensor_copy", out=gsp[:, 1, :], in_=tmp4, _partial=True)
            P.I("dve", "tensor_copy", out=gspf, in_=gsp)
            for i2 in range(2):
                for h in range(4):
                    P.I("dve", "tensor_scalar", out=gb[:, i2, h, :], in0=C.onesb, scalar1=gspf[:, i2, h:h + 1], scalar2=0.0,
                        op0=ALU.mult, op1=ALU.add, _partial=True)
            p2 = nextps(C)
            P.mm(out=p2[:, 0:4], lhsT=tri2b, rhs=gsp[:, 0, :], start=True, stop=False)
            P.mm(out=p2[:, 0:4], lhsT=tri2b, rhs=gsp[:, 1, :], start=False, stop=True)
            P.I("dve", "tensor_copy", out=gcs, in_=p2[:, 0:4], _partial=True)
            yield
            pBg = nextps(C)
            for h in range(4):
                o = pBg[:, h * 128:(h + 1) * 128]
                P.mm(out=o, lhsT=gb[:, 0, h, :], rhs=tri2b, start=True, stop=False)
                P.mm(out=o, lhsT=gb[:, 1, h, :], rhs=tri2b, start=False, stop=True)
            pBv = pBg.r("p (h i) -> p h i", h=4)
            P.I("act", "activation", out=Eg, in_=pBv, func=AF.Exp)
            for h in range(4):
                P.I("dve", "tensor_scalar", out=G[:, h, :], in0=pBv[:, h, :], scalar1=gcs[:, h:h + 1], scalar2=0.0,
                    op0=ALU.subtract, op1=ALU.min, _partial=True)
                P.I("dve", "tensor_scalar", out=GL[:, h, :], in0=pBv[:, h, :], scalar1=gcs[:, h:h + 1], scalar2=0.0,
                    op0=ALU.subtract, op1=ALU.max, _partial=True)
            P.I("dve", "tensor_tensor", out=dk[0:64, :], in0=pBv[0:64, :, 63], in1=gcs[0:64, :], op=ALU.subtract, _partial=True)
            P.I("dve", "tensor_tensor", out=dk[64:128, :], in0=pBv[64:128, :, 127], in1=gcs[64:128, :], op=ALU.subtract, _partial=True)
            yield
            P.I("act", "activation", out=G, in_=G, func=AF.Exp)
            P.I("act", "activation", out=GL, in_=GL, func=AF.Exp, scale=-1.0)
            P.I("act", "activation", out=dk, in_=dk, func=AF.Exp, _partial=True)
            P.I("act", "activation", out=egc, in_=gcs, func=AF.Exp, _partial=True)
            P.I("dve", "tensor_tensor", out=sc1, in0=egc, in1=beta, op=ALU.mult, _partial=True)
            P.I("dve", "tensor_tensor", out=G, in0=G, in1=Mui[:, None, :].bc([128, 4, 128]), op=ALU.mult)
            P.I("dve", "tensor_tensor", out=GL, in0=GL, in1=Mls[:, None, :].bc([128, 4, 128]), op=ALU.mult)
            yield
            pkk, pqk = nextps(C), nextps(C)
            for h in range(4):
                P.mm(out=pkk[:, h * 128:(h + 1) * 128], lhsT=kT[:, h, :], rhs=kT[:, h, :])
            for h in range(4):
                P.mm(out=pqk[:, h * 128:(h + 1) * 128], lhsT=kT[:, h, :], rhs=qT[:, h, :])
            for h in range(4):
                P.I("dve", "scalar_tensor_tensor", out=A32[:, h, :], in0=pkk[:, h * 128:(h + 1) * 128], scalar=beta[:, h:h + 1],
                    in1=GL[:, h, :], op0=ALU.mult, op1=ALU.mult, _partial=True)
            P.I("dve", "tensor_tensor", out=qkTm, in0=pqk.r("p (h i) -> p h i", h=4), in1=G, op=ALU.mult)
            P.I("dve", "tensor_tensor", out=qgT, in0=qT, in1=Eg, op=ALU.mult)
            yield
            pN = nextps(C)
            for h in range(4):
                P.tr(out=pN[:, h * 128:(h + 1) * 128], in_=A32[:, h, :], ident=C.identf)
            pNv = pN.r("p (h i) -> p h i", h=4)
            P.I("act", "activation", out=Nb, in_=pNv, func=AF.Copy)
            P.I("dve", "tensor_tensor", out=R, in0=C.identf[:, None, :].bc([128, 4, 128]), in1=pNv, op=ALU.subtract)
            P.I("act", "activation", out=Ab, in_=A32, func=AF.Copy)
            P.I("act", "activation", out=Rb, in_=R, func=AF.Copy)
            yield
            Xc, XTc = Nb, Ab
            for lev in range(5):
                pX, pXT = nextps(C), nextps(C)
                last_lev = (lev == 4)
                for h in range(4):
                    if not last_lev:
                        P.mm(out=pX[:, h * 128:(h + 1) * 128], lhsT=XTc[:, h, :], rhs=Xc[:, h, :])
                    P.mm(out=pXT[:, h * 128:(h + 1) * 128], lhsT=Xc[:, h, :], rhs=XTc[:, h, :])
                Xn, XTn = Xb[lev % 2], XTb[lev % 2]
                if not last_lev:
                    P.I("act", "activation", out=Xn, in_=pX.r("p (h i) -> p h i", h=4), func=AF.Copy)
                P.I("dve", "tensor_copy", out=XTn, in_=pXT.r("p (h i) -> p h i", h=4))
                pR = nextps(C)
                for h in range(4):
                    P.mm(out=pR[:, h * 128:(h + 1) * 128], lhsT=XTn[:, h, :], rhs=Rb[:, h, :])
                P.I("dve", "tensor_tensor", out=R, in0=R, in1=pR.r("p (h i) -> p h i", h=4), op=ALU.add)
                if not last_lev:
                    P.I("act", "activation", out=Rb, in_=R, func=AF.Copy)
                Xc, XTc = Xn, XTn
                yield
            P.I("act", "activation", out=TTh, in_=R, func=AF.Copy)
            P.I("dve", "tensor_tensor", out=N32, in0=R, in1=TTh, op=ALU.subtract)
            P.I("dve", "tensor_copy", out=TTl, in_=N32)
            yield
            pkt = nextps(C).bitcast(BF16)
            for h in range(4):
                P.tr(out=pkt[:, h * 128:(h + 1) * 128], in_=kT[:, h, :], ident=C.identb)
            for h in range(4):
                P.I("dve", "tensor_scalar", out=kbeg[:, h, :], in0=pkt[:, h * 128:(h + 1) * 128], scalar1=sc1[:, h:h + 1], scalar2=0.0,
                    op0=ALU.mult, op1=ALU.add, _partial=True)
                P.I("dve", "tensor_scalar", out=kdec[:, h, :], in0=pkt[:, h * 128:(h + 1) * 128], scalar1=dk[:, h:h + 1], scalar2=0.0,
                    op0=ALU.mult, op1=ALU.add, _partial=True)
            yield
            pvt = nextps(C).bitcast(BF16)
            for h in range(4):
                P.tr(out=pvt[:, h * 128:(h + 1) * 128], in_=vTb[:, h, :], ident=C.identb)
            for h in range(4):
                P.I("dve", "tensor_scalar", out=vb[:, h, :], in0=pvt[:, h * 128:(h + 1) * 128], scalar1=beta[:, h:h + 1], scalar2=0.0,
                    op0=ALU.mult, op1=ALU.add, _partial=True)
            yield
            pw, pu = nextps(C), nextps(C)
            for h in range(4):
                o = pw[:, h * 128:(h + 1) * 128]
                P.mm(out=o, lhsT=kbeg[:, h, :], rhs=TTh[:, h, :], start=True, stop=False)
                P.mm(out=o, lhsT=kbeg[:, h, :], rhs=TTl[:, h, :], start=False, stop=True)
            for h in range(4):
                o = pu[:, h * 128:(h + 1) * 128]
                P.mm(out=o, lhsT=TTh[:, h, :], rhs=vb[:, h, :], start=True, stop=False)
                P.mm(out=o, lhsT=TTl[:, h, :], rhs=vb[:, h, :], start=False, stop=True)
            P.I("act", "activation", out=wTb, in_=pw.r("p (h i) -> p h i", h=4), func=AF.Copy)
            P.I("act", "activation", out=u, in_=pu.r("p (h i) -> p h i", h=4), func=AF.Copy)

    def scan(t):
            wTb, u, qgT, qkTm, kdec, Eg, gs = (x[t % 2] for x in (wTb_2, u_2, qgT_2, qkTm_2, kdec_2, Eg_2, gs_2))
            sl = slice(t * NT, (t + 1) * NT)
            for X in range(2):
                r = slice(X * 64, (X + 1) * 64)
                lc = X * 64 + 63
                pvn = nextps(C)
                for h in range(4):
                    P.mm(out=pvn[r, h * 128:(h + 1) * 128], lhsT=wTb[:, h, r], rhs=Sb[:, h, :])
                P.I("dve", "tensor_tensor", out=vnb[r, :, :], in0=u[r, :, :], in1=pvn[r, :].r("p (h e) -> p h e", h=4),
                    op=ALU.subtract, _partial=True)
                yield
                po = nextps(C)
                for h in range(4):
                    o = po[:, h * 64:(h + 1) * 64]
                    P.mm(out=o, lhsT=Sb[:, h, :], rhs=qgT[:, h, r], start=True, stop=False)
                    P.mm(out=o, lhsT=vnb[r, h, :], rhs=qkTm[r, h, r], start=False, stop=True)
                P.I("act", "activation", out=oT[:, :, r], in_=po[:, 0:256].r("p (h i) -> p h i", h=4), func=AF.Copy, _partial=True)
                yield
                pS = nextps(C)
                for h in range(4):
                    P.mm(out=pS[:, h * 128:(h + 1) * 128], lhsT=kdec[r, h, :], rhs=vnb[r, h, :])
                for h in range(4):
                    P.I("dve", "scalar_tensor_tensor", out=S[:, h, :], in0=S[:, h, :], scalar=Eg[:, h, lc:lc + 1],
                        in1=pS[:, h * 128:(h + 1) * 128], op0=ALU.mult, op1=ALU.add, _partial=True)
                P.I("act", "activation", out=Sb, in_=S, func=AF.Copy)
                yield
            P.I("act", "activation", out=sq2, in_=oT, func=AF.Square)
            for h in range(4):
                p1 = nextps(C)
                P.mm(out=p1[:, 0:NT], lhsT=C.onesb, rhs=sq2[:, h, :])
                r_ = rn2[h % 2]
                P.I("act", "activation", out=r_, in_=p1[:, 0:NT], func=AF.Ln, bias=1e-6, scale=1.0 / 128)
                P.I("act", "activation", out=r_, in_=r_, func=AF.Exp, scale=-0.5)
                P.I("dve", "tensor_tensor", out=r_, in0=r_, in1=oT[:, h, :], op=ALU.mult)
                P.I("dve", "tensor_tensor", out=r_, in0=r_, in1=gs[:, h, :], op=ALU.mult)
                P.I("act", "activation", out=Y[:, h, sl], in_=r_, func=AF.Identity, scale=nw[:, 0:1], _partial=True)
                yield

    NB = T // NT
    drain(prep(0))
    for t in range(NB):
        if t + 1 < NB:
            interleave(prep(t + 1), scan(t), ra=3, rb=1)
        else:
            drain(scan(t))
    P.release(m)


def bcast_rows(C, name, src1d, n):
    P = C.P
    t = P.sb(name, [128, n], F32)
    P.dma(t, V(src1d.buf, src1d.ap.partition_broadcast(128)), q="sp")
    return t


def mixer_sg(C, l, hT, Y):
    P, I = C.P, C.I
    m = P.mark()
    wu = P.sb("sg_wu", [128, 8, 512], BF16)
    wv = P.sb("sg_wv", [128, 8, 512], BF16)
    for k in range(8):
        load_w(C, wu[:, k, :], I["w_in"][l, k * 128:(k + 1) * 128, OFF["su"]:OFF["su"] + 512])
        load_w(C, wv[:, k, :], I["w_in"][l, k * 128:(k + 1) * 128, OFF["sv"]:OFF["sv"] + 512])
    gbc = bcast_rows(C, "sg_g", I["sg_ln_g"][l], 512)
    bbc = bcast_rows(C, "sg_b", I["sg_ln_b"][l], 512)
    sbc = bcast_rows(C, "sg_sb", I["sg_b"][l].r("g t -> (g t)"), 512)
    wraw = P.sb("sg_wraw", [128, 4, 128], F32)
    wtf = P.sb("sg_wtf", [128, 4, 128], F32)
    wtb = P.sb("sg_wtb", [128, 4, 128], BF16)
    P.dma(wraw, I["sg_w"][l].r("g t s -> t g s"))
    pk = nextps(C)
    for g in range(4):
        P.tr(out=pk[:, g * 128:(g + 1) * 128], in_=wraw[:, g, :], ident=C.identf)
    P.I("act", "activation", out=wtf, in_=pk.r("p (g t) -> p g t", g=4), func=AF.Copy)
    P.I("pool", "affine_select", out=wtf, in_=wtf, pattern=[[0, 4], [1, 128]], compare_op=ALU.is_ge, fill=0.0,
        base=0, channel_multiplier=-1)
    P.I("dve", "tensor_copy", out=wtb, in_=wtf)
    N = 512
    uT = [P.sb(f"sg_uT{i}", [128, 4, N], F32) for i in range(2)]
    vg = [P.sb(f"sg_vg{i}", [128, 512], F32) for i in range(2)]
    vtm = [P.sb(f"sg_vtm{i}", [128, 512], BF16) for i in range(2)]
    stt = [P.sb(f"sg_st{i}", [128, 16], F32) for i in range(2)]
    tmp = [P.sb(f"sg_tmp{i}", [128, 4, 128], F32) for i in range(2)]
    n = 0
    for t in range(T // N):
        u = uT[t % 2]
        sl = slice(t * N, (t + 1) * N)
        for j in range(4):
            p1 = nextps(C)
            for k in range(8):
                P.mm(out=p1, lhsT=wu[:, k, j * 128:(j + 1) * 128], rhs=hT[:, k, sl], start=(k == 0), stop=(k == 7))
            P.I("act", "activation", out=u[:, j, :], in_=p1, func=AF.Gelu, _partial=True)
        for c in range(4):
            c0 = t * N + c * 128
            v, vb, st, tm = vg[n % 2], vtm[n % 2], stt[n % 2], tmp[n % 2]
            n += 1
            p1 = nextps(C)
            for k in range(8):
                P.mm(out=p1, lhsT=hT[:, k, c0:c0 + 128], rhs=wv[:, k, :], start=(k == 0), stop=(k == 7))
            P.I("act", "activation", out=v, in_=p1, func=AF.Gelu)
            P.I("dve", "bn_stats", out=st[:, 0:6], in_=v, _partial=True)
            P.I("dve", "bn_aggr", out=st[:, 8:10], in_=st[:, 0:6], _partial=True)
            P.I("act", "activation", out=st[:, 10:11], in_=st[:, 9:10], func=AF.Ln, bias=1e-5, scale=1.0, _partial=True)
            P.I("act", "activation", out=st[:, 10:11], in_=st[:, 10:11], func=AF.Exp, scale=-0.5, _partial=True)
            P.I("dve", "tensor_scalar", out=v, in0=v, scalar1=st[:, 8:9], scalar2=st[:, 10:11], op0=ALU.subtract, op1=ALU.mult)
            P.I("dve", "tensor_tensor", out=v, in0=v, in1=gbc, op=ALU.mult)
            P.I("dve", "tensor_tensor", out=vb, in0=v, in1=bbc, op=ALU.add)
            p2 = nextps(C)
            for g in range(4):
                P.mm(out=p2[:, g * 128:(g + 1) * 128], lhsT=vb[:, g * 128:(g + 1) * 128], rhs=wtb[:, g, :])
            P.I("dve", "tensor_tensor", out=tm, in0=p2.r("p (g t) -> p g t", g=4), in1=sbc.r("p (g t) -> p g t", g=4), op=ALU.add)
            P.I("dve", "tensor_tensor", out=Y[:, :, c0:c0 + 128], in0=tm, in1=u[:, :, c * 128:(c + 1) * 128], op=ALU.mult,
                _partial=True)
    P.release(m)


def mixer_fox(C, l, hT, Y):
    P, I = C.P, C.I
    m = P.mark()
    wq = P.sb("fx_wq", [128, 8, 512], BF16)
    wk = P.sb("fx_wk", [128, 8, 512], BF16)
    wv = P.sb("fx_wv", [128, 8, 512], BF16)
    wf = P.sb("fx_wf", [128, 8, 8], BF16)
    for k in range(8):
        rows = slice(k * 128, (k + 1) * 128)
        load_w(C, wq[:, k, :], I["w_in"][l, rows, OFF["fq"]:OFF["fq"] + 512])
        load_w(C, wk[:, k, :], I["w_in"][l, rows, OFF["fk"]:OFF["fk"] + 512])
        load_w(C, wv[:, k, :], I["w_in"][l, rows, OFF["fv"]:OFF["fv"] + 512])
        load_w(C, wf[:, k, :], I["w_in"][l, rows, OFF["ff"]:OFF["ff"] + 8])
    fb = P.sb("fx_fb", [8, 2], F32)
    P.dma(fb[:, 0:1], I["fox_f_bias"][l].r("(h o) -> h o", o=1))
    P.I("dve", "tensor_scalar", out=fb[:, 1:2], in0=fb[:, 0:1], scalar1=-1.0, scalar2=0.0, op0=ALU.mult, op1=ALU.add, _partial=True)
    chm = P.sb("fx_chm", [8, 2, T], BF16)
    negc = P.sb("fx_negc", [128, 32, 8], F32)
    m2 = P.mark()
    sp = P.sb("fx_sp", [8, T], F32)
    cc = P.sb("fx_c", [8, T], F32)
    r1 = P.sb("fx_r1", [8, T], F32)
    for t in range(8):
        sl = slice(t * 512, (t + 1) * 512)
        p1 = nextps(C)
        for k in range(8):
            P.mm(out=p1[0:8, :], lhsT=wf[:, k, :], rhs=hT[:, k, sl], start=(k == 0), stop=(k == 7))
        P.I("act", "activation", out=sp[:, sl], in_=p1[0:8, :], func=AF.Exp, scale=-1.0, bias=fb[:, 1:2], _partial=True)
    P.I("act", "activation", out=sp, in_=sp, func=AF.Ln, bias=1.0, scale=1.0)
    P.I("dve", "tensor_tensor_scan", out=cc, data0=sp, data1=sp, initial=0.0, op0=ALU.min, op1=ALU.subtract)
    P.I("dve", "tensor_copy", out=chm[:, 0, :], in_=cc, _partial=True)
    P.I("dve", "tensor_tensor", out=r1, in0=cc, in1=chm[:, 0, :], op=ALU.subtract)
    P.I("dve", "tensor_copy", out=chm[:, 1, :], in_=r1, _partial=True)
    pk = nextps(C)
    for blk in range(32):
        P.tr(out=pk[:, blk * 8:(blk + 1) * 8], in_=cc[:, blk * 128:(blk + 1) * 128], ident=C.identf[0:8, 0:8])
    P.I("dve", "tensor_scalar", out=negc, in0=pk[:, 0:256].r("p (b h) -> p b h", h=8), scalar1=-1.0, scalar2=0.0,
        op0=ALU.mult, op1=ALU.add)
    dbg(C, "fx_c", cc)
    dbg(C, "fx_negc", negc)
    P.release(m2)
    maskneg = P.sb("fx_mask", [128, 4, 512], BF16)
    P.I("pool", "memset", ap=maskneg, constant=0.0, _extra_w=[maskneg])
    for b in range(4):
        P.I("pool", "affine_select", out=maskneg[:, b, :], in_=maskneg[:, b, :], pattern=[[1, 512]], compare_op=ALU.is_ge,
            fill=-30000.0, base=-128 * b, channel_multiplier=-1)
    qa = P.sb("fx_qa", [128, T], BF16)
    ka = P.sb("fx_ka", [128, T], BF16)
    vaug = P.sb("fx_vaug", [128, 32, 128], BF16)
    pTs = [P.sb(f"fx_pT{i}", [128, 512], BF16) for i in range(4)]
    den = [P.sb(f"fx_den{i}", [64, 512], F32) for i in range(2)]
    P.I("pool", "memset", ap=vaug[:, :, 64:128], constant=1.0, _extra_w=[vaug])
    P.I("pool", "memset", ap=ka[64:66, :], constant=1.0, _extra_w=[ka])
    rot = 0
    npT = 0
    nq = 0
    for h in range(8):
        cs = slice(h * 64, (h + 1) * 64)
        for t in range(8):
            sl = slice(t * 512, (t + 1) * 512)
            p1, p2 = nextps(C), nextps(C)
            for k in range(8):
                P.mm(out=p1[0:64, :], lhsT=wq[:, k, cs], rhs=hT[:, k, sl], start=(k == 0), stop=(k == 7))
            P.I("act", "activation", out=qa[0:64, sl], in_=p1[0:64, :], func=AF.Identity, scale=0.125, _partial=True)
            for k in range(8):
                P.mm(out=p2[0:64, :], lhsT=wk[:, k, cs], rhs=hT[:, k, sl], start=(k == 0), stop=(k == 7))
            P.I("dve", "tensor_copy", out=ka[0:64, sl], in_=p2[0:64, :], _partial=True)
        P.dma(qa[64:65, :], chm[h:h + 1, 0, :], q="sp")
        P.dma(qa[65:66, :], chm[h:h + 1, 1, :], q="sp")
        for b4 in range(8):
            p1 = nextps(C)
            for bb in range(4):
                blk = b4 * 4 + bb
                for k in range(8):
                    P.mm(out=p1[:, bb * 64:(bb + 1) * 64], lhsT=hT[:, k, blk * 128:(blk + 1) * 128], rhs=wv[:, k, cs],
                         start=(k == 0), stop=(k == 7))
            P.I("act", "activation", out=vaug[:, b4 * 4:(b4 + 1) * 4, 0:64], in_=p1[:, 0:256].r("p (b d) -> p b d", d=64),
                func=AF.Copy, _partial=True)
        blocks = [(qt, kb) for qt in range(8) for kb in range(4 * qt + 4)]
        LA = 3
        sbank = {}
        accs = {}
        for i in range(len(blocks) + LA):
            if i < len(blocks):
                qt, kb = blocks[i]
                qs = slice(qt * 512, (qt + 1) * 512)
                sps = C.ps[2 + rot % 6]
                rot += 1
                sbank[i] = sps
                diag = kb >= 4 * qt
                P.mm(out=sps, lhsT=ka[0:66, kb * 128:(kb + 1) * 128], rhs=qa[0:66, qs], start=True, stop=(not diag))
                if diag:
                    P.mm(out=sps, lhsT=C.identb, rhs=maskneg[:, kb - 4 * qt, :], start=False, stop=True)
            j = i - LA
            if j < 0:
                continue
            qt, kb = blocks[j]
            qs = slice(qt * 512, (qt + 1) * 512)
            nkb = 4 * qt + 4
            if kb == 0:
                accs[qt] = (C.ps[nq % 2], den[nq % 2])
                nq += 1
            acc, dn = accs[qt]
            pT = pTs[npT % 4]
            npT += 1
            P.I("act", "activation", out=pT, in_=sbank.pop(j), func=AF.Exp, bias=negc[:, kb, h:h + 1], scale=1.0)
            P.mm(out=acc, lhsT=vaug[:, kb, :], rhs=pT, start=(kb == 0), stop=(kb == nkb - 1))
            if kb == nkb - 1:
                P.I("act", "activation", out=dn, in_=acc[64:128, :], func=AF.Copy)
                P.I("dve", "reciprocal", out=dn, in_=dn)
                po = (h % 2) * 64
                P.I("dve", "tensor_tensor", out=Y[po:po + 64, h // 2, qs], in0=acc[0:64, :], in1=dn, op=ALU.mult, _partial=True)
    C.psi = 0
    P.release(m)

from concourse.bass_utils import run_bass_kernel_spmd

_CACHE = {}


def kernel(**inputs):
    inputs = {k: np.ascontiguousarray(np.asarray(v, dtype=np.float32)) for k, v in inputs.items()}
    if "nc" not in _CACHE:
        _CACHE["nc"] = build(dict(nlayers=2, mixers="abcd"))[0]
    nc = _CACHE["nc"]
    x = inputs["x"]
    n = x.shape[0]
    in_maps = []
    for b in range(n):
        m = {k: v for k, v in inputs.items() if k != "x"}
        m["x"] = x[b]
        in_maps.append(m)
    res = run_bass_kernel_spmd(nc, in_maps, core_ids=list(range(n)))
    return np.stack([r["out"] for r in res.results], axis=0).astype(np.float32)
```
